# Optimizing a Trainium2 kernel written in Bass

```python
import jax, jax.numpy as jnp
from jax import lax
import numpy as np

D_MODEL = 1024
BATCH = 8
SEQ = 4096
DEPTH = 4

N_MIXERS = 3
SB_HEADS = 16
SB_HEAD_DIM = D_MODEL // SB_HEADS
SB_Q_BLOCK = 128
CONV_WIDTH = 3
GLA_HEADS = 4
GLA_DK = D_MODEL // 2
GLA_DV = D_MODEL
GLA_DK_HEAD = GLA_DK // GLA_HEADS
GLA_DV_HEAD = GLA_DV // GLA_HEADS
GLA_GATE_RANK = 16
GLA_GATE_NORMALIZER = 16.0
GLA_CHUNK = 64
D_FF = 4 * D_MODEL
RMS_EPS = 1e-6

kernel_name = "hybrid_sb_conv_gla_sqrelu_sandwich"


def rms_norm(x, gain):
    xf = x.astype(jnp.float32)
    y = xf * lax.rsqrt(jnp.mean(xf * xf, axis=-1, keepdims=True) + RMS_EPS) * gain.astype(jnp.float32)
    return y.astype(x.dtype)


def stick_breaking_mixer(xn, w_qkv, w_o):
    b, s, _ = xn.shape
    qkv = (xn @ w_qkv).reshape(b, s, 3, SB_HEADS, SB_HEAD_DIM)
    q, k, v = [qkv[:, :, i].transpose(0, 2, 1, 3).astype(jnp.float32) for i in range(3)]
    scale = SB_HEAD_DIM ** -0.5
    n_blocks = s // SB_Q_BLOCK
    q_blocks = q.reshape(b, SB_HEADS, n_blocks, SB_Q_BLOCK, SB_HEAD_DIM).transpose(2, 0, 1, 3, 4)
    key_pos = jnp.arange(s)

    def one_block(args):
        qb, start = args
        z = jnp.einsum('bhqd,bhkd->bhqk', qb, k) * scale
        q_pos = start + jnp.arange(SB_Q_BLOCK)
        mask = key_pos[None, :] < q_pos[:, None]
        log_beta = jax.nn.log_sigmoid(z)
        log_1m_beta = jnp.where(mask, jax.nn.log_sigmoid(-z), 0.0)
        suffix = lax.cumsum(log_1m_beta, axis=3, reverse=True) - log_1m_beta
        weights = jnp.where(mask, jnp.exp(log_beta + suffix), 0.0)
        return jnp.einsum('bhqk,bhkd->bhqd', weights, v)

    o = lax.map(one_block, (q_blocks, jnp.arange(n_blocks) * SB_Q_BLOCK))
    o = o.transpose(1, 0, 3, 2, 4).reshape(b, s, D_MODEL).astype(xn.dtype)
    return o @ w_o


def short_conv_mixer(xn, w_in, conv_w, w_out):
    s = xn.shape[1]
    bcu = xn @ w_in
    b_gate, c_gate, u = jnp.split(bcu, 3, axis=-1)
    h = c_gate * u
    hp = jnp.pad(h, ((0, 0), (CONV_WIDTH - 1, 0), (0, 0)))
    conv = sum(conv_w[i] * hp[:, i:i + s] for i in range(CONV_WIDTH))
    return (b_gate * conv) @ w_out


def gla_mixer(xn, w_in, w_gate_up, b_gate, head_norm, w_o):
    b, s, _ = xn.shape
    n_chunks = s // GLA_CHUNK
    proj = xn @ w_in
    q, k, v, g, a_low = jnp.split(
        proj, np.cumsum([GLA_DK, GLA_DK, GLA_DV, GLA_DV]).tolist(), axis=-1)
    log_gate = jax.nn.log_sigmoid(
        (a_low @ w_gate_up + b_gate).astype(jnp.float32)) / GLA_GATE_NORMALIZER

    def to_chunks(t, dh):
        t = t.astype(jnp.float32).reshape(b, n_chunks, GLA_CHUNK, GLA_HEADS, dh)
        return t.transpose(1, 0, 3, 2, 4)

    qc = to_chunks(q, GLA_DK_HEAD) * (GLA_DK_HEAD ** -0.5)
    kc = to_chunks(k, GLA_DK_HEAD)
    vc = to_chunks(v, GLA_DV_HEAD)
    gc = to_chunks(log_gate, GLA_DK_HEAD)
    causal = jnp.tril(jnp.ones((GLA_CHUNK, GLA_CHUNK), dtype=bool))

    def step(state, inp):
        qi, ki, vi, gi = inp
        cum = jnp.cumsum(gi, axis=2)
        inter = jnp.einsum('bhcd,bhde->bhce', qi * jnp.exp(cum), state)
        diff = cum[:, :, :, None, :] - cum[:, :, None, :, :]
        decay = jnp.exp(jnp.where(causal[None, None, :, :, None], diff, -jnp.inf))
        scores = jnp.einsum('bhid,bhjd,bhijd->bhij', qi, ki, decay)
        out = inter + jnp.einsum('bhij,bhje->bhie', scores, vi)
        last = cum[:, :, -1:, :]
        new_state = jnp.exp(last)[:, :, 0, :, None] * state + jnp.einsum(
            'bhcd,bhce->bhde', ki * jnp.exp(last - cum), vi)
        return new_state, out

    state0 = jnp.zeros((b, GLA_HEADS, GLA_DK_HEAD, GLA_DV_HEAD), jnp.float32)
    _, o = lax.scan(step, state0, (qc, kc, vc, gc))
    o = o.transpose(1, 0, 3, 2, 4).reshape(b, s, GLA_HEADS, GLA_DV_HEAD)
    o = rms_norm(o, head_norm).reshape(b, s, GLA_DV).astype(xn.dtype)
    return (o * jax.nn.silu(g)) @ w_o


def squared_relu_mlp(xn, w_up, w_down):
    return jnp.square(jax.nn.relu(xn @ w_up)) @ w_down


def setup_inputs(seed: int = 0) -> dict:
    key = jax.random.key(seed)
    ks = jax.random.split(key, 16)
    n_sb = (DEPTH + 2) // 3
    n_conv = (DEPTH + 1) // 3
    n_gla = DEPTH // 3
    f32 = jnp.float32

    def nrm(k, shape, scale):
        return jax.random.normal(k, shape, f32) * scale

    d_in_gla = 2 * GLA_DK + 2 * GLA_DV + GLA_GATE_RANK
    return {
        "x": nrm(ks[0], (BATCH, SEQ, D_MODEL), 1.0),
        "norm_gains": 1.0 + nrm(ks[1], (DEPTH, 4, D_MODEL), 0.02),
        "sb_w_qkv": nrm(ks[2], (n_sb, D_MODEL, 3 * D_MODEL), D_MODEL ** -0.5),
        "sb_w_o": nrm(ks[3], (n_sb, D_MODEL, D_MODEL), D_MODEL ** -0.5),
        "conv_w_in": nrm(ks[4], (n_conv, D_MODEL, 3 * D_MODEL), D_MODEL ** -0.5),
        "conv_w": nrm(ks[5], (n_conv, CONV_WIDTH, D_MODEL), CONV_WIDTH ** -0.5),
        "conv_w_out": nrm(ks[6], (n_conv, D_MODEL, D_MODEL), D_MODEL ** -0.5),
        "gla_w_in": nrm(ks[7], (n_gla, D_MODEL, d_in_gla), D_MODEL ** -0.5),
        "gla_w_gate_up": nrm(ks[8], (n_gla, GLA_GATE_RANK, GLA_DK), GLA_GATE_RANK ** -0.5),
        "gla_b_gate": nrm(ks[9], (n_gla, GLA_DK), 0.1),
        "gla_head_norm": 1.0 + nrm(ks[10], (n_gla, GLA_HEADS, GLA_DV_HEAD), 0.02),
        "gla_w_o": nrm(ks[11], (n_gla, GLA_DV, D_MODEL), GLA_DV ** -0.5),
        "ffn_w_up": nrm(ks[12], (DEPTH, D_MODEL, D_FF), D_MODEL ** -0.5),
        "ffn_w_down": nrm(ks[13], (DEPTH, D_FF, D_MODEL), D_FF ** -0.5),
    }


def reference(x, norm_gains, sb_w_qkv, sb_w_o, conv_w_in, conv_w, conv_w_out,
              gla_w_in, gla_w_gate_up, gla_b_gate, gla_head_norm, gla_w_o,
              ffn_w_up, ffn_w_down):
    h = x
    for i in range(DEPTH):
        kind, j = i % N_MIXERS, i // N_MIXERS
        xn = rms_norm(h, norm_gains[i, 0])
        if kind == 0:
            m = stick_breaking_mixer(xn, sb_w_qkv[j], sb_w_o[j])
        elif kind == 1:
            m = short_conv_mixer(xn, conv_w_in[j], conv_w[j], conv_w_out[j])
        else:
            m = gla_mixer(xn, gla_w_in[j], gla_w_gate_up[j], gla_b_gate[j],
                          gla_head_norm[j], gla_w_o[j])
        h = h + rms_norm(m, norm_gains[i, 1])
        f = squared_relu_mlp(rms_norm(h, norm_gains[i, 2]), ffn_w_up[i], ffn_w_down[i])
        h = h + rms_norm(f, norm_gains[i, 3])
    return h
```

```python
import numpy as np
from contextlib import ExitStack
import concourse.bass as bass
import concourse.mybir as mybir
from concourse.bass_utils import run_bass_kernel_spmd

F32 = mybir.dt.float32
BF16 = mybir.dt.bfloat16
AF = mybir.ActivationFunctionType
ALU = mybir.AluOpType


class Buf:
    __slots__ = ("name", "t", "w", "r", "sem", "tot")

    def __init__(self, name, t=None):
        self.name = name
        self.t = t
        self.w = None
        self.r = {}
        self.sem = None
        self.tot = 0

    def __getitem__(self, idx):
        return self.t[idx]


class Eng:
    def __init__(self, name, h, sem):
        self.name = name
        self.h = h
        self.sem = sem
        self.count = 0
        self.waited = {}
        self.nwaits = 0
        self.nins = 0


class FW:
    def __init__(self, nc, es):
        self.nc = nc
        self.es = es
        self.E = {}
        for name, h in (("pe", nc.tensor), ("act", nc.scalar), ("dve", nc.vector),
                        ("pool", nc.gpsimd), ("sp", nc.sync)):
            sem = es.enter_context(nc.semaphore("sem_" + name))
            self.E[name] = Eng(name, h, sem)
        self.nsem = 5
        self.uid = 0
        self.dma_slots = []
        self.free_sems = []
        self.phase_bufs = []

    def sb(self, name, shape, dt, stack=None):
        self.uid += 1
        t = (stack or self.es).enter_context(self.nc.sbuf_tensor(f"{name}_{self.uid}", list(shape), dt))
        b = Buf(name, t)
        if stack is not None:
            self.phase_bufs.append(b)
        return b

    def ps(self, name, shape=(128, 512), dt=F32, stack=None):
        self.uid += 1
        t = (stack or self.es).enter_context(self.nc.psum_tensor(f"{name}_{self.uid}", list(shape), dt))
        return Buf(name, t)

    def dsem(self, b):
        if b.sem is None:
            if b.t is not None and self.free_sems:
                b.sem, b.tot = self.free_sems.pop()
            else:
                b.sem = self.es.enter_context(self.nc.semaphore(f"dq_{b.name}_{self.nsem}"))
                self.nsem += 1
            if b.t is not None:
                self.dma_slots.append(b)
        return b.sem

    def end_phase(self):
        self.barrier()
        for b in self.phase_bufs:
            if b.sem is not None:
                self.free_sems.append((b.sem, b.tot))
                self.dma_slots.remove(b)
                b.sem = None
        self.phase_bufs = []

    def _waits(self, E, reads, writes):
        need = {}

        def add(ev, same_ok):
            if ev is None:
                return
            sem, val = ev
            if sem is E.sem and not same_ok:
                return
            k = id(sem)
            if k not in need or need[k][1] < val:
                need[k] = (sem, val)

        for b in reads:
            add(b.w, True)
        for b in writes:
            add(b.w, False)
            for sem_id, (sem, val) in b.r.items():
                add((sem, val), False)
        for k, (sem, val) in need.items():
            if E.waited.get(k, 0) >= val:
                continue
            E.h.wait_ge(sem, val)
            E.waited[k] = val
            E.nwaits += 1

    def _commit(self, ev, reads, writes):
        sem, val = ev
        k = id(sem)
        for b in reads:
            if k not in b.r or b.r[k][1] < val:
                b.r[k] = (sem, val)
        for b in writes:
            b.w = ev
            b.r = {}

    def op(self, eng, fn, reads=(), writes=()):
        E = self.E[eng]
        self._waits(E, reads, writes)
        ins = fn(E.h)
        E.count += 1
        E.nins += 1
        ins.then_inc(E.sem, 1)
        self._commit((E.sem, E.count), reads, writes)

    def ops(self, eng, fns, reads=(), writes=()):
        E = self.E[eng]
        self._waits(E, reads, writes)
        ins = None
        for fn in fns:
            ins = fn(E.h)
            E.nins += 1
        E.count += 1
        ins.then_inc(E.sem, 1)
        self._commit((E.sem, E.count), reads, writes)

    def dma(self, q, pairs, slot, reads=(), writes=()):
        E = self.E[q]
        sem = self.dsem(slot)
        self._waits(E, reads, writes)
        k = id(sem)
        if slot.tot > 0 and E.waited.get(k, 0) < slot.tot:
            E.h.wait_ge(sem, slot.tot)
            E.waited[k] = slot.tot
            E.nwaits += 1
        for (o, i) in pairs:
            E.h.dma_start(out=o, in_=i).then_inc(sem, 16)
            slot.tot += 16
            E.nins += 1
        self._commit((sem, slot.tot), reads, writes)

    def finish(self, bufs):
        E = self.E["sp"]
        self._waits(E, bufs, ())

    def barrier(self):
        SP = self.E["sp"]
        for b in self.dma_slots:
            k = id(b.sem)
            if b.tot > 0 and SP.waited.get(k, 0) < b.tot:
                SP.h.wait_ge(b.sem, b.tot)
                SP.waited[k] = b.tot
        SP.count += 1
        SP.h.nop().then_inc(SP.sem, 1)
        evs = []
        for E in self.E.values():
            if E.count > 0:
                evs.append((E.sem, E.count))
        for E in self.E.values():
            for sem, val in evs:
                if sem is E.sem:
                    continue
                k = id(sem)
                if E.waited.get(k, 0) < val:
                    E.h.wait_ge(sem, val)
                    E.waited[k] = val
                    E.nwaits += 1

    def stats(self):
        return {n: (e.nins, e.nwaits) for n, e in self.E.items()}


D = 1024
KC = 8
TT = 512
EPS = 1e-6


class Ctx:
    pass


def setup_common(fw, cx, gains_dram, ngains):
    nc = fw.nc
    cx.onesD = fw.sb("onesD", [128, 128], BF16)
    cx.ones256 = fw.sb("ones256", [128, 128], BF16)
    cx.gains = fw.sb("gains", [128, ngains], F32)
    fw.op("pool", lambda e: e.memset(cx.onesD[:, :], 1.0 / 1024), writes=[cx.onesD])
    fw.op("pool", lambda e: e.memset(cx.ones256[:, :], 1.0 / 256), writes=[cx.ones256])
    fw.dma("sp", [(cx.gains[:, :], gains_dram)], cx.gains, writes=[cx.gains])
    cx.dram = {}


def dbuf(cx, key):
    if key not in cx.dram:
        cx.dram[key] = Buf("dram_%s" % (key,))
    return cx.dram[key]


def cast_weight(fw, cx, key, dst_ap, src_ap, nsplit=1):
    b = dbuf(cx, key)
    if nsplit == 1:
        pairs = [(dst_ap, src_ap)]
    else:
        pairs = [(dst_ap[i], src_ap[i]) for i in range(nsplit)]
    fw.dma("pool", pairs, b, writes=[b])
    return b


def rms_rstd(fw, cx, src, nch, T, ones, sq, ps, rstd):
    fw.op("act", lambda e: e.activation(out=sq[:, 0:nch, 0:T], in_=src[:, 0:nch, 0:T], func=AF.Square),
          reads=[src], writes=[sq])
    fw.ops("pe", [
        (lambda e, c=c: e.matmul(ps[:, 0:T], ones[:, :], sq[:, c, 0:T], start=(c == 0), stop=(c == nch - 1)))
        for c in range(nch)], reads=[sq, ones], writes=[ps])
    fw.op("act", lambda e: e.activation(out=rstd[:, 0:T], in_=ps[:, 0:T], func=AF.Ln, bias=EPS, scale=1.0),
          reads=[ps], writes=[rstd])
    fw.op("act", lambda e: e.activation(out=rstd[:, 0:T], in_=rstd[:, 0:T], func=AF.Exp, scale=-0.5),
          reads=[rstd], writes=[rstd])


def ffn_phase(fw, cx, S, h_in, h_out, hkey_in, hkey_out, g_pre, g_post, wup, wdn, wup_b, wdn_b):
    nt = S // TT
    with ExitStack() as st:
        hts = [fw.sb("hT%d" % i, [128, KC, TT], F32, st) for i in range(2)]
        sq = fw.sb("sq", [128, KC, TT], BF16, st)
        xn = fw.sb("xn", [128, KC, TT], BF16, st)
        act = fw.sb("act", [128, 32, TT], BF16, st)
        mT = fw.sb("mT", [128, KC, TT], F32, st)
        tmp = fw.sb("tmp", [128, KC, TT], F32, st)
        rstd = fw.sb("rstd", [128, TT], F32, st)
        rstd2 = fw.sb("rstd2", [128, TT], F32, st)
        relus = [fw.sb("relu%d" % i, [128, TT], F32, st) for i in range(2)]
        wus = [fw.sb("wu%d" % i, [128, KC, 512], BF16, st) for i in range(2)]
        wds = [fw.sb("wd%d" % i, [128, 32, 128], BF16, st) for i in range(2)]
        ps_stat = fw.ps("ps_stat", stack=st)
        ps_up = [fw.ps("ps_up%d" % i, stack=st) for i in range(4)]
        ps_dn = [fw.ps("ps_dn%d" % i, stack=st) for i in range(2)]
        nu = 0
        nd = 0
        for t in range(nt):
            ht = hts[t % 2]
            tsl = slice(t * TT, (t + 1) * TT)
            hb_in = dbuf(cx, (hkey_in, t))
            hb_out = dbuf(cx, (hkey_out, t))
            fw.dma("sp", [(ht[:, :, :], h_in[:, :, tsl])], ht, reads=[hb_in], writes=[ht])
            rms_rstd(fw, cx, ht, KC, TT, cx.onesD, sq, ps_stat, rstd)
            for c in range(KC):
                fw.op("dve", lambda e, c=c: e.scalar_tensor_tensor(
                    out=xn[:, c, :], in0=ht[:, c, :], scalar=cx.gains[:, g_pre * KC + c:g_pre * KC + c + 1],
                    in1=rstd[:, :], op0=ALU.mult, op1=ALU.mult),
                    reads=[ht, rstd, cx.gains], writes=[xn])
            for blk in range(8):
                wu = wus[nu % 2]
                nu += 1
                fw.dma("sp", [(wu[:, :, :], wup[blk])], wu, reads=[wup_b], writes=[wu])
                for j in range(4):
                    f = blk * 4 + j
                    pu = ps_up[f % 4]
                    fw.ops("pe", [
                        (lambda e, c=c, j=j, pu=pu, wu=wu: e.matmul(pu[:, :], wu[:, c, j * 128:(j + 1) * 128], xn[:, c, :],
                                                                   start=(c == 0), stop=(c == KC - 1)))
                        for c in range(KC)], reads=[wu, xn], writes=[pu])
                    rl = relus[f % 2]
                    fw.op("act", lambda e, pu=pu, rl=rl: e.activation(out=rl[:, :], in_=pu[:, :], func=AF.Relu),
                          reads=[pu], writes=[rl])
                    fw.op("dve", lambda e, f=f, rl=rl: e.tensor_tensor(out=act[:, f, :], in0=rl[:, :], in1=rl[:, :], op=ALU.mult),
                          reads=[rl], writes=[act])
            for d in range(KC):
                wd = wds[nd % 2]
                nd += 1
                fw.dma("sp", [(wd[:, :, :], wdn[d])], wd, reads=[wdn_b], writes=[wd])
                pd = ps_dn[d % 2]
                fw.ops("pe", [
                    (lambda e, f=f, pd=pd, wd=wd: e.matmul(pd[:, :], wd[:, f, :], act[:, f, :],
                                                          start=(f == 0), stop=(f == 31)))
                    for f in range(32)], reads=[wd, act], writes=[pd])
                fw.op("act", lambda e, d=d, pd=pd: e.copy(out=mT[:, d, :], in_=pd[:, :]), reads=[pd], writes=[mT])
            rms_rstd(fw, cx, mT, KC, TT, cx.onesD, sq, ps_stat, rstd2)
            for c in range(KC):
                fw.op("dve", lambda e, c=c: e.scalar_tensor_tensor(
                    out=tmp[:, c, :], in0=mT[:, c, :], scalar=cx.gains[:, g_post * KC + c:g_post * KC + c + 1],
                    in1=rstd2[:, :], op0=ALU.mult, op1=ALU.mult),
                    reads=[mT, rstd2, cx.gains], writes=[tmp])
            fw.op("dve", lambda e: e.tensor_tensor(out=tmp[:, :, :], in0=tmp[:, :, :], in1=ht[:, :, :], op=ALU.add),
                  reads=[tmp, ht], writes=[tmp])
            fw.dma("pool", [(h_out[:, :, tsl], tmp[:, :, :])], tmp, reads=[tmp], writes=[hb_out])
        fw.end_phase()


class TileBufs:
    pass


def alloc_tile_bufs(fw, st, need_m=True, TT=TT, nh=2):
    tb = TileBufs()
    tb.TT = TT
    tb.hts = [fw.sb("hT%d" % i, [128, KC, TT], F32, st) for i in range(nh)]
    tb.sq = fw.sb("sq", [128, KC, TT], BF16, st)
    tb.xn = fw.sb("xn", [128, KC, TT], BF16, st)
    tb.rstd = fw.sb("rstd", [128, TT], F32, st)
    tb.ps_stat = fw.ps("ps_stat", stack=st)
    if need_m:
        tb.mT = fw.sb("mT", [128, KC, TT], F32, st)
        tb.tmp = fw.sb("tmp", [128, KC, TT], F32, st)
        tb.rstd2 = fw.sb("rstd2", [128, TT], F32, st)
    return tb


def prologue(fw, cx, tb, t, h_in, hkey_in, g_pre):
    TT = tb.TT
    ht = tb.hts[t % len(tb.hts)]
    tsl = slice(t * TT, (t + 1) * TT)
    fw.dma("sp", [(ht[:, :, :], h_in[:, :, tsl])], ht, reads=[dbuf(cx, (hkey_in, t))], writes=[ht])
    rms_rstd(fw, cx, ht, KC, TT, cx.onesD, tb.sq, tb.ps_stat, tb.rstd)
    for c in range(KC):
        fw.op("dve", lambda e, c=c: e.scalar_tensor_tensor(
            out=tb.xn[:, c, :], in0=ht[:, c, :], scalar=cx.gains[:, g_pre * KC + c:g_pre * KC + c + 1],
            in1=tb.rstd[:, :], op0=ALU.mult, op1=ALU.mult),
            reads=[ht, tb.rstd, cx.gains], writes=[tb.xn])
    return ht


def load_h(fw, cx, tb, t, h_in, hkey_in):
    TT = tb.TT
    ht = tb.hts[t % len(tb.hts)]
    tsl = slice(t * TT, (t + 1) * TT)
    fw.dma("sp", [(ht[:, :, :], h_in[:, :, tsl])], ht, reads=[dbuf(cx, (hkey_in, t))], writes=[ht])
    return ht


def epilogue(fw, cx, tb, t, ht, h_out, hkey_out, g_post):
    TT = tb.TT
    tsl = slice(t * TT, (t + 1) * TT)
    rms_rstd(fw, cx, tb.mT, KC, TT, cx.onesD, tb.sq, tb.ps_stat, tb.rstd2)
    for c in range(KC):
        fw.op("dve", lambda e, c=c: e.scalar_tensor_tensor(
            out=tb.tmp[:, c, :], in0=tb.mT[:, c, :], scalar=cx.gains[:, g_post * KC + c:g_post * KC + c + 1],
            in1=tb.rstd2[:, :], op0=ALU.mult, op1=ALU.mult),
            reads=[tb.mT, tb.rstd2, cx.gains], writes=[tb.tmp])
    fw.op("dve", lambda e: e.tensor_tensor(out=tb.tmp[:, :, :], in0=tb.tmp[:, :, :], in1=ht[:, :, :], op=ALU.add),
          reads=[tb.tmp, ht], writes=[tb.tmp])
    fw.dma("pool", [(h_out[:, :, tsl], tb.tmp[:, :, :])], tb.tmp, reads=[tb.tmp], writes=[dbuf(cx, (hkey_out, t))])


def out_proj(fw, cx, tb, wo, src, ps_list):
    for d in range(KC):
        pd = ps_list[d % len(ps_list)]
        fw.ops("pe", [
            (lambda e, c=c, d=d, pd=pd: e.matmul(pd[:, 0:tb.TT], wo[:, c, d * 128:(d + 1) * 128], src[:, c, :],
                                                 start=(c == 0), stop=(c == KC - 1)))
            for c in range(KC)], reads=[wo, src], writes=[pd])
        fw.op("act", lambda e, d=d, pd=pd: e.copy(out=tb.mT[:, d, :], in_=pd[:, 0:tb.TT]), reads=[pd], writes=[tb.mT])


def sb_proj_phase(fw, cx, S, h_in, hkey_in, g_pre, wqkv, wqkv_b, qT, kT, v, key):
    nt = S // TT
    with ExitStack() as st:
        tb = alloc_tile_bufs(fw, st, need_m=False)
        ws = [fw.sb("wqkv%d" % i, [128, KC, 512], BF16, st) for i in range(2)]
        qk_sb = [fw.sb("qk_sb%d" % i, [128, 4, TT], BF16, st) for i in range(2)]
        v_sb = fw.sb("v_sb", [128, 4, 1024], BF16, st)
        pss = [fw.ps("ps_p%d" % i, stack=st) for i in range(4)]
        nw = 0
        npz = 0
        for t in range(nt):
            tsl = slice(t * TT, (t + 1) * TT)
            prologue(fw, cx, tb, t, h_in, hkey_in, g_pre)
            for blk in range(6):
                w = ws[nw % 2]
                nw += 1
                fw.dma("sp", [(w[:, :, :], wqkv[blk])], w, reads=[wqkv_b], writes=[w])
                if blk < 4:
                    dst = qT if blk < 2 else kT
                    stage = qk_sb[blk % 2]
                    for j in range(4):
                        p = pss[npz % 4]
                        npz += 1
                        fw.ops("pe", [
                            (lambda e, c=c, j=j, p=p, w=w: e.matmul(p[:, :], w[:, c, j * 128:(j + 1) * 128], tb.xn[:, c, :],
                                                                   start=(c == 0), stop=(c == KC - 1)))
                            for c in range(KC)], reads=[w, tb.xn], writes=[p])
                        fw.op("act", lambda e, j=j, p=p, stage=stage: e.copy(out=stage[:, j, :], in_=p[:, :]),
                              reads=[p], writes=[stage])
                    r0 = (blk % 2) * 512
                    fw.dma("pool", [(dst[r0:r0 + 512, tsl].rearrange("(j p) t -> p j t", p=128), stage[:, :, :])],
                           stage, reads=[stage], writes=[dbuf(cx, (key + ("q" if blk < 2 else "k"), t, blk % 2))])
                else:
                    vb = blk - 4
                    for tk in range(4):
                        p = pss[npz % 4]
                        npz += 1
                        fw.ops("pe", [
                            (lambda e, c=c, tk=tk, p=p, w=w: e.matmul(p[:, :], tb.xn[:, c, tk * 128:(tk + 1) * 128], w[:, c, :],
                                                                     start=(c == 0), stop=(c == KC - 1)))
                            for c in range(KC)], reads=[w, tb.xn], writes=[p])
                        fw.op("act", lambda e, tk=tk, p=p, vb=vb: e.copy(out=v_sb[:, tk, vb * 512:(vb + 1) * 512], in_=p[:, :]),
                              reads=[p], writes=[v_sb])
            fw.dma("pool", [(v[tsl, :].rearrange("(b p) f -> p b f", p=128), v_sb[:, :, :])], v_sb,
                   reads=[v_sb], writes=[dbuf(cx, (key + "v", t))])
        fw.end_phase()


def sb_attn_phase(fw, cx, S, qT, kT, v, oT, key, consts):
    nT = S // 512
    nB = S // 128
    H = 16
    with ExitStack() as st:
        kts = [fw.sb("kt%d" % i, [64, S], BF16, st) for i in range(2)]
        qts = [fw.sb("qt%d" % i, [64, S], BF16, st) for i in range(2)]
        vhs = [fw.sb("vh%d" % i, [128, nB, 64], BF16, st) for i in range(2)]
        es_ = [fw.sb("e%d" % i, [128, 512], F32, st) for i in range(3)]
        sps = [fw.sb("sp%d" % i, [128, 512], BF16, st) for i in range(2)]
        lsums = [fw.sb("lsum%d" % i, [128, 512], BF16, st) for i in range(2)]
        e2s = [fw.sb("e2%d" % i, [128, 512], F32, st) for i in range(2)]
        ws_ = [fw.sb("w%d" % i, [128, 512], BF16, st) for i in range(3)]
        osb = [fw.sb("osb%d" % i, [64, 512], BF16, st) for i in range(2)]
        pz = [fw.ps("pz%d" % i, stack=st) for i in range(2)]
        psum_s = [fw.ps("pss%d" % i, stack=st) for i in range(2)]
        po = [fw.ps("po%d" % i, stack=st) for i in range(2)]
        tri, negones, masks = consts.tri, consts.negones, consts.masks
        kdeps = [dbuf(cx, (key + "k", t, b)) for t in range(S // TT) for b in range(2)]
        qdeps = [dbuf(cx, (key + "q", t, b)) for t in range(S // TT) for b in range(2)]
        vdeps = [dbuf(cx, (key + "v", t)) for t in range(S // TT)]
        tiles = []
        g = 0
        for h in range(H):
            for T in range(nT):
                Js = list(range(4 * T + 3, -1, -1))
                for idx, J in enumerate(Js):
                    tiles.append((h, T, J, idx, len(Js), g))
                g += 1
        N = len(tiles)
        loaded = set()

        def load_head(h):
            if h in loaded or h >= H:
                return
            loaded.add(h)
            kt, qt, vh = kts[h % 2], qts[h % 2], vhs[h % 2]
            fw.dma("sp", [(kt[:, :], kT[h * 64:(h + 1) * 64, :])], kt, reads=kdeps, writes=[kt])
            fw.dma("sp", [(qt[:, :], qT[h * 64:(h + 1) * 64, :])], qt, reads=qdeps, writes=[qt])
            fw.dma("sp", [(vh[:, :, :], v[:, h * 64:(h + 1) * 64].rearrange("(b p) d -> p b d", p=128))], vh,
                   reads=vdeps, writes=[vh])

        def A(i):
            h, T, J, idx, n, g = tiles[i]
            load_head(h)
            kt, qt = kts[h % 2], qts[h % 2]
            z_b, e_b = pz[i % 2], es_[i % 3]
            fw.op("pe", lambda e: e.matmul(z_b[:, :], kt[:, J * 128:(J + 1) * 128], qt[:, T * 512:(T + 1) * 512],
                                           start=True, stop=True), reads=[kt, qt], writes=[z_b])
            fw.op("act", lambda e: e.activation(out=e_b[:, :], in_=z_b[:, :], func=AF.Exp, scale=0.125),
                  reads=[z_b], writes=[e_b])
            r = J - 4 * T
            if r >= 0:
                fw.op("dve", lambda e: e.tensor_tensor(out=e_b[:, :], in0=e_b[:, :], in1=masks[:, r, :], op=ALU.mult),
                      reads=[e_b, masks], writes=[e_b])

        def Bln(i):
            e_b, sp_b = es_[i % 3], sps[i % 2]
            fw.op("act", lambda e: e.activation(out=sp_b[:, :], in_=e_b[:, :], func=AF.Ln, bias=1.0, scale=1.0),
                  reads=[e_b], writes=[sp_b])

        def C(i):
            h, T, J, idx, n, g = tiles[i]
            sp_b, s_b = sps[i % 2], psum_s[i % 2]
            l_old, l_new = lsums[i % 2], lsums[(i + 1) % 2]
            if idx == 0:
                fw.op("pe", lambda e: e.matmul(s_b[:, :], tri[:, :], sp_b[:, :], start=True, stop=True),
                      reads=[tri, sp_b], writes=[s_b])
                fw.op("dve", lambda e: e.tensor_copy(out=l_new[:, :], in_=sp_b[:, :]), reads=[sp_b], writes=[l_new])
            else:
                fw.ops("pe", [
                    lambda e: e.matmul(s_b[:, :], tri[:, :], sp_b[:, :], start=True, stop=False),
                    lambda e: e.matmul(s_b[:, :], negones[:, :], l_old[:, :], start=False, stop=True)],
                    reads=[tri, negones, sp_b, l_old], writes=[s_b])
                if idx != n - 1:
                    fw.op("dve", lambda e: e.tensor_tensor(out=l_new[:, :], in0=l_old[:, :], in1=sp_b[:, :], op=ALU.add),
                          reads=[sp_b, l_old], writes=[l_new])

        def DE(i):
            s_b, e2_b, e_b, w_b = psum_s[i % 2], e2s[i % 2], es_[i % 3], ws_[i % 3]
            fw.op("act", lambda e: e.activation(out=e2_b[:, :], in_=s_b[:, :], func=AF.Exp), reads=[s_b], writes=[e2_b])
            fw.op("dve", lambda e: e.tensor_tensor(out=w_b[:, :], in0=e_b[:, :], in1=e2_b[:, :], op=ALU.mult),
                  reads=[e_b, e2_b], writes=[w_b])

        def F(i):
            h, T, J, idx, n, g = tiles[i]
            vh, w_b, pout, ob = vhs[h % 2], ws_[i % 3], po[g % 2], osb[g % 2]
            fw.op("pe", lambda e: e.matmul(pout[0:64, :], vh[:, J, :], w_b[:, :], start=(idx == 0), stop=(idx == n - 1)),
                  reads=[vh, w_b], writes=[pout])
            if idx == n - 1:
                fw.op("act", lambda e: e.copy(out=ob[:, :], in_=pout[0:64, :]), reads=[pout], writes=[ob])
                fw.dma("pool", [(oT[h * 64:(h + 1) * 64, T * 512:(T + 1) * 512], ob[:, :])], ob, reads=[ob],
                       writes=[dbuf(cx, (key + "o", h, T))])
                if T == nT - 1:
                    load_head(h + 2)

        load_head(0)
        load_head(1)
        for s_ in range(N + 2):
            if s_ < N:
                A(s_)
            if 0 <= s_ - 1 < N:
                C(s_ - 1)
            if 0 <= s_ - 2 < N:
                F(s_ - 2)
            if 0 <= s_ - 1 < N:
                DE(s_ - 1)
            if s_ < N:
                Bln(s_)
        fw.end_phase()


def sb_out_phase(fw, cx, S, h_in, h_out, hkey_in, hkey_out, g_post, wo_dram, wo_b, oT, key):
    nt = S // TT
    with ExitStack() as st:
        tb = alloc_tile_bufs(fw, st, need_m=True)
        wo = fw.sb("wo", [128, KC, 1024], BF16, st)
        ots = [fw.sb("ot%d" % i, [128, KC, TT], BF16, st) for i in range(2)]
        pss = [fw.ps("ps_o%d" % i, stack=st) for i in range(2)]
        fw.dma("sp", [(wo[:, :, :], wo_dram)], wo, reads=[wo_b], writes=[wo])
        for t in range(nt):
            tsl = slice(t * TT, (t + 1) * TT)
            ht = load_h(fw, cx, tb, t, h_in, hkey_in)
            ot = ots[t % 2]
            odeps = [dbuf(cx, (key + "o", h, t)) for h in range(16)]
            fw.dma("sp", [(ot[:, :, :], oT[:, tsl].rearrange("(c p) t -> p c t", p=128))], ot, reads=odeps, writes=[ot])
            out_proj(fw, cx, tb, wo, ot, pss)
            epilogue(fw, cx, tb, t, ht, h_out, hkey_out, g_post)
        fw.end_phase()


def conv_phase(fw, cx, S, h_in, h_out, hkey_in, hkey_out, g_pre, g_post, wcin, wcin_b, wo_dram, wo_b, cw_off):
    nt = S // TT
    with ExitStack() as st:
        tb = alloc_tile_bufs(fw, st, need_m=True)
        wo = fw.sb("wo", [128, KC, 1024], BF16, st)
        wis = [fw.sb("wci%d" % i, [128, KC, 384], BF16, st) for i in range(2)]
        hcs = [fw.sb("hc%d" % i, [128, TT + 2], F32, st) for i in range(8)]
        cs = [fw.sb("cs%d" % i, [128, TT], F32, st) for i in range(2)]
        t0s = [fw.sb("t0%d" % i, [128, TT], F32, st) for i in range(2)]
        gated = fw.sb("gated", [128, KC, TT], BF16, st)
        pss = [fw.ps("ps_c%d" % i, stack=st) for i in range(6)]
        pso = [fw.ps("ps_o%d" % i, stack=st) for i in range(1)]
        fw.dma("sp", [(wo[:, :, :], wo_dram)], wo, reads=[wo_b], writes=[wo])
        nw = 0
        for t in range(nt):
            ht = prologue(fw, cx, tb, t, h_in, hkey_in, g_pre)
            for i in range(8):
                w = wis[nw % 2]
                pb, pc, pu = pss[(nw % 2) * 3 + 0], pss[(nw % 2) * 3 + 1], pss[(nw % 2) * 3 + 2]
                c_sb, t0 = cs[nw % 2], t0s[nw % 2]
                nw += 1
                hc = hcs[i]
                fw.dma("sp", [(w[:, :, :], wcin[i])], w, reads=[wcin_b], writes=[w])
                for gi, p in ((1, pc), (2, pu), (0, pb)):
                    fw.ops("pe", [
                        (lambda e, c=c, gi=gi, p=p, w=w: e.matmul(p[:, :], w[:, c, gi * 128:(gi + 1) * 128], tb.xn[:, c, :],
                                                                 start=(c == 0), stop=(c == KC - 1)))
                        for c in range(KC)], reads=[w, tb.xn], writes=[p])
                if t == 0:
                    fw.op("dve", lambda e, hc=hc: e.memset(hc[:, 0:2], 0.0), writes=[hc])
                else:
                    fw.op("dve", lambda e, hc=hc: e.tensor_copy(out=hc[:, 0:2], in_=hc[:, TT:TT + 2]), reads=[hc], writes=[hc])
                fw.op("act", lambda e, c_sb=c_sb, pc=pc: e.copy(out=c_sb[:, :], in_=pc[:, :]), reads=[pc], writes=[c_sb])
                fw.op("dve", lambda e, hc=hc, c_sb=c_sb, pu=pu: e.tensor_tensor(out=hc[:, 2:TT + 2], in0=c_sb[:, :], in1=pu[:, :], op=ALU.mult),
                      reads=[c_sb, pu, hc], writes=[hc])
                k0 = cw_off + 0 * 8 + i
                k1 = cw_off + 1 * 8 + i
                k2 = cw_off + 2 * 8 + i
                fw.op("dve", lambda e, hc=hc, t0=t0, k0=k0: e.tensor_scalar(out=t0[:, :], in0=hc[:, 0:TT], scalar1=cx.gains[:, k0:k0 + 1],
                                                                          scalar2=None, op0=ALU.mult),
                      reads=[hc, cx.gains], writes=[t0])
                fw.op("dve", lambda e, hc=hc, t0=t0, k1=k1: e.scalar_tensor_tensor(out=t0[:, :], in0=hc[:, 1:TT + 1], scalar=cx.gains[:, k1:k1 + 1],
                                                                                  in1=t0[:, :], op0=ALU.mult, op1=ALU.add),
                      reads=[hc, t0, cx.gains], writes=[t0])
                fw.op("dve", lambda e, hc=hc, t0=t0, k2=k2: e.scalar_tensor_tensor(out=t0[:, :], in0=hc[:, 2:TT + 2], scalar=cx.gains[:, k2:k2 + 1],
                                                                                  in1=t0[:, :], op0=ALU.mult, op1=ALU.add),
                      reads=[hc, t0, cx.gains], writes=[t0])
                fw.op("dve", lambda e, i=i, t0=t0, pb=pb: e.tensor_tensor(out=gated[:, i, :], in0=t0[:, :], in1=pb[:, :], op=ALU.mult),
                      reads=[t0, pb], writes=[gated])
            out_proj(fw, cx, tb, wo, gated, pso)
            epilogue(fw, cx, tb, t, ht, h_out, hkey_out, g_post)
        fw.end_phase()


TG = 256
CH = 64


def gla_phase(fw, cx, S, h_in, h_out, hkey_in, hkey_out, g_pre, g_post, win_d, win_b, wa_d, wa_b, wg_d, wg_b,
              wo_dram, wo_b, hn_off, gc):
    nt = S // TG
    NCH = TG // CH
    with ExitStack() as st:
        tb = alloc_tile_bufs(fw, st, need_m=True, TT=TG, nh=1)
        wo = fw.sb("wo", [128, KC, 1024], BF16, st)
        win = [fw.sb("win%d" % i, [128, KC, 512], BF16, st) for i in range(6)]
        wa = fw.sb("wa", [128, KC, 16], BF16, st)
        wg = fw.sb("wg", [32, 512], BF16, st)
        a_ext = fw.sb("a_ext", [32, TG], BF16, st)
        qT_sb = fw.sb("qT_sb", [128, 4, TG], F32, st)
        kT_sb = fw.sb("kT_sb", [128, 4, TG], F32, st)
        sg = fw.sb("sg", [128, KC, TG], BF16, st)
        o_n = fw.sb("o_n", [128, KC, TG], BF16, st)
        egs = [fw.sb("eg%d" % i, [128, TG], F32, st) for i in range(2)]
        ex = fw.sb("ex", [64, 512], F32, st)
        sp = fw.sb("sp", [64, 512], BF16, st)
        ek = fw.sb("ek", [64, 512], F32, st)
        kt_tok = fw.sb("kt_tok", [64, 512], BF16, st)
        v_tok = fw.sb("v_tok", [64, 1024], BF16, st)
        eq = fw.sb("eq", [128, 4, CH], F32, st)
        ekT = fw.sb("ekT", [128, 4, CH], F32, st)
        qtl = fw.sb("qtl", [128, 4, CH], BF16, st)
        ktl = fw.sb("ktl", [128, 4, CH], BF16, st)
        sT = fw.sb("sT", [64, 4, CH], BF16, st)
        state = fw.sb("state", [128, 4, 256], F32, st)
        state_bf = fw.sb("state_bf", [128, 4, 256], BF16, st)
        rstdh = fw.sb("rstdh", [128, 4, TG], F32, st)
        pA = [fw.ps("pA%d" % i, stack=st) for i in range(2)]
        pct = fw.ps("pct", [128, 256], stack=st)
        pst = fw.ps("pst", [128, 256], stack=st)
        po = fw.ps("po", stack=st)
        pkv = [fw.ps("pkv%d" % i, stack=st) for i in range(2)]
        o_tile = tb.tmp

        fw.dma("sp", [(wo[:, :, :], wo_dram)], wo, reads=[wo_b], writes=[wo])
        for i in range(6):
            fw.dma("sp", [(win[i][:, :, :], win_d[i])], win[i], reads=[win_b], writes=[win[i]])
        fw.dma("sp", [(wa[:, :, :], wa_d)], wa, reads=[wa_b], writes=[wa])
        fw.dma("sp", [(wg[0:17, :], wg_d)], wg, reads=[wg_b], writes=[wg])
        fw.op("dve", lambda e: e.memset(a_ext[:, :], 1.0), writes=[a_ext])
        fw.op("dve", lambda e: e.memset(state[:, :, :], 0.0), writes=[state])
        fw.op("dve", lambda e: e.memset(state_bf[:, :, :], 0.0), writes=[state_bf])
        npa = [0]

        def nextp():
            p = pA[npa[0] % 2]
            npa[0] += 1
            return p

        def proj_fm(w, j0, width, p, N=TG):
            fw.ops("pe", [
                (lambda e, c=c: e.matmul(p[0:width, 0:N], w[:, c, j0:j0 + width], tb.xn[:, c, 0:N],
                                         start=(c == 0), stop=(c == KC - 1)))
                for c in range(KC)], reads=[w, tb.xn], writes=[p])

        for t in range(nt):
            ht = prologue(fw, cx, tb, t, h_in, hkey_in, g_pre)
            for hh in range(4):
                p = nextp()
                proj_fm(win[0], hh * 128, 128, p)
                fw.op("act", lambda e, hh=hh, p=p: e.copy(out=qT_sb[:, hh, :], in_=p[:, 0:TG]), reads=[p], writes=[qT_sb])
            for hh in range(4):
                p = nextp()
                proj_fm(win[1], hh * 128, 128, p)
                fw.op("act", lambda e, hh=hh, p=p: e.copy(out=kT_sb[:, hh, :], in_=p[:, 0:TG]), reads=[p], writes=[kT_sb])
            for c8 in range(8):
                p = nextp()
                eg = egs[c8 % 2]
                proj_fm(win[4 + c8 // 4], (c8 % 4) * 128, 128, p)
                fw.op("act", lambda e, p=p, eg=eg: e.activation(out=eg[:, :], in_=p[:, 0:TG], func=AF.Exp, scale=-1.0),
                      reads=[p], writes=[eg])
                fw.op("dve", lambda e, eg=eg: e.tensor_scalar(out=eg[:, :], in0=eg[:, :], scalar1=1.0, scalar2=None, op0=ALU.add),
                      reads=[eg], writes=[eg])
                fw.op("dve", lambda e, eg=eg: e.reciprocal(out=eg[:, :], in_=eg[:, :]), reads=[eg], writes=[eg])
                fw.op("dve", lambda e, c8=c8, p=p, eg=eg: e.tensor_tensor(out=sg[:, c8, :], in0=eg[:, :], in1=p[:, 0:TG], op=ALU.mult),
                      reads=[eg, p], writes=[sg])
            p = nextp()
            proj_fm(wa, 0, 16, p)
            fw.op("act", lambda e, p=p: e.copy(out=a_ext[0:16, :], in_=p[0:16, 0:TG]), reads=[p], writes=[a_ext])
            for n in range(NCH):
                csl = slice(n * CH, (n + 1) * CH)
                px = nextp()
                fw.op("pe", lambda e, px=px: e.matmul(px[0:64, :], a_ext[0:17, csl], wg[0:17, :], start=True, stop=True),
                      reads=[a_ext, wg], writes=[px])
                fw.op("act", lambda e, px=px: e.activation(out=ex[:, :], in_=px[0:64, :], func=AF.Exp, scale=-1.0),
                      reads=[px], writes=[ex])
                fw.op("act", lambda e: e.activation(out=sp[:, :], in_=ex[:, :], func=AF.Ln, bias=1.0, scale=1.0),
                      reads=[ex], writes=[sp])
                pc = nextp()
                fw.op("pe", lambda e, pc=pc: e.matmul(pc[0:64, :], gc.btn[:, :], sp[:, :], start=True, stop=True),
                      reads=[gc.btn, sp], writes=[pc])
                fw.op("act", lambda e, pc=pc: e.activation(out=ek[:, :], in_=pc[0:64, :], func=AF.Exp, scale=-1.0),
                      reads=[pc], writes=[ek])
                pk = nextp()
                fw.ops("pe", [
                    (lambda e, c=c, pk=pk: e.matmul(pk[0:64, :], tb.xn[:, c, csl], win[1][:, c, :], start=(c == 0), stop=(c == KC - 1)))
                    for c in range(KC)], reads=[win[1], tb.xn], writes=[pk])
                fw.op("dve", lambda e, pk=pk: e.tensor_tensor(out=kt_tok[:, :], in0=ek[:, :], in1=pk[0:64, :], op=ALU.mult),
                      reads=[ek, pk], writes=[kt_tok])
                for vb in range(2):
                    pv = nextp()
                    fw.ops("pe", [
                        (lambda e, c=c, pv=pv, vb=vb: e.matmul(pv[0:64, :], tb.xn[:, c, csl], win[2 + vb][:, c, :],
                                                             start=(c == 0), stop=(c == KC - 1)))
                        for c in range(KC)], reads=[win[2 + vb], tb.xn], writes=[pv])
                    fw.op("act", lambda e, pv=pv, vb=vb: e.copy(out=v_tok[:, vb * 512:(vb + 1) * 512], in_=pv[0:64, :]),
                          reads=[pv], writes=[v_tok])
                fw.ops("pe", [
                    (lambda e, hh=hh: e.matmul(pct[:, hh * CH:(hh + 1) * CH], sp[:, hh * 128:(hh + 1) * 128], gc.btn[:, :],
                                               start=True, stop=True))
                    for hh in range(4)], reads=[sp, gc.btn], writes=[pct])
                fw.op("act", lambda e: e.activation(out=eq[:, :, :], in_=pct[:, :].rearrange("p (h c) -> p h c", h=4), func=AF.Exp),
                      reads=[pct], writes=[eq])
                fw.op("act", lambda e: e.activation(out=ekT[:, :, :], in_=pct[:, :].rearrange("p (h c) -> p h c", h=4), func=AF.Exp, scale=-1.0),
                      reads=[pct], writes=[ekT])
                fw.op("dve", lambda e: e.scalar_tensor_tensor(out=qtl[:, :, :], in0=qT_sb[:, :, csl], scalar=128 ** -0.5, in1=eq[:, :, :],
                                                             op0=ALU.mult, op1=ALU.mult),
                      reads=[qT_sb, eq], writes=[qtl])
                fw.op("dve", lambda e: e.tensor_tensor(out=ktl[:, :, :], in0=kT_sb[:, :, csl], in1=ekT[:, :, :], op=ALU.mult),
                      reads=[kT_sb, ekT], writes=[ktl])
                fw.ops("pe", [
                    (lambda e, hh=hh: e.matmul(pst[0:64, hh * CH:(hh + 1) * CH], ktl[:, hh, :], qtl[:, hh, :], start=True, stop=True))
                    for hh in range(4)], reads=[ktl, qtl], writes=[pst])
                fw.op("dve", lambda e: e.tensor_tensor(out=sT[:, :, :], in0=pst[0:64, :].rearrange("p (h c) -> p h c", h=4),
                                                       in1=gc.mask4[:, :, :], op=ALU.mult),
                      reads=[pst, gc.mask4], writes=[sT])
                mm = []
                for hh in range(4):
                    for ec in range(2):
                        col = (hh * 2 + ec) * CH
                        mm.append(lambda e, hh=hh, ec=ec, col=col: e.matmul(
                            po[:, col:col + CH], state_bf[:, hh, ec * 128:(ec + 1) * 128], qtl[:, hh, :], start=True, stop=False))
                        mm.append(lambda e, hh=hh, ec=ec, col=col: e.matmul(
                            po[:, col:col + CH], v_tok[:, hh * 256 + ec * 128:hh * 256 + (ec + 1) * 128], sT[:, hh, :],
                            start=False, stop=True))
                fw.ops("pe", mm, reads=[state_bf, qtl, v_tok, sT], writes=[po])
                fw.op("act", lambda e: e.copy(out=o_tile[:, :, csl], in_=po[:, :].rearrange("p (k c) -> p k c", k=8)),
                      reads=[po], writes=[o_tile])
                for hh in range(4):
                    pkvb = pkv[hh // 2]
                    fw.op("pe", lambda e, hh=hh, pkvb=pkvb: e.matmul(
                        pkvb[:, (hh % 2) * 256:(hh % 2 + 1) * 256], kt_tok[:, hh * 128:(hh + 1) * 128], v_tok[:, hh * 256:(hh + 1) * 256],
                        start=True, stop=True), reads=[kt_tok, v_tok], writes=[pkvb])
                for hh in range(4):
                    pkvb = pkv[hh // 2]
                    fw.op("dve", lambda e, hh=hh: e.tensor_scalar(out=state[:, hh, :], in0=state[:, hh, :], scalar1=eq[:, hh, CH - 1:CH],
                                                                 scalar2=None, op0=ALU.mult),
                          reads=[state, eq], writes=[state])
                    fw.op("dve", lambda e, hh=hh, pkvb=pkvb: e.scalar_tensor_tensor(
                        out=state[:, hh, :], in0=pkvb[:, (hh % 2) * 256:(hh % 2 + 1) * 256], scalar=eq[:, hh, CH - 1:CH],
                        in1=state[:, hh, :], op0=ALU.mult, op1=ALU.add),
                        reads=[pkvb, eq, state], writes=[state])
                fw.op("act", lambda e: e.copy(out=state_bf[:, :, :], in_=state[:, :, :]), reads=[state], writes=[state_bf])
            fw.op("act", lambda e: e.activation(out=tb.sq[:, :, :], in_=o_tile[:, :, :], func=AF.Square), reads=[o_tile], writes=[tb.sq])
            for hh in range(4):
                pkvb = pkv[hh // 2]
                fw.ops("pe", [
                    (lambda e, hh=hh, ec=ec, pkvb=pkvb: e.matmul(pkvb[:, (hh % 2) * 256:(hh % 2 + 1) * 256], cx.ones256[:, :],
                                                                 tb.sq[:, hh * 2 + ec, :], start=(ec == 0), stop=(ec == 1)))
                    for ec in range(2)], reads=[cx.ones256, tb.sq], writes=[pkvb])
            for half in range(2):
                fw.op("act", lambda e, half=half: e.activation(
                    out=rstdh[:, 2 * half:2 * half + 2, :], in_=pkv[half][:, :].rearrange("p (h c) -> p h c", h=2),
                    func=AF.Ln, bias=EPS, scale=1.0), reads=[pkv[half]], writes=[rstdh])
            fw.op("act", lambda e: e.activation(out=rstdh[:, :, :], in_=rstdh[:, :, :], func=AF.Exp, scale=-0.5),
                  reads=[rstdh], writes=[rstdh])
            for k8 in range(8):
                fw.op("dve", lambda e, k8=k8: e.scalar_tensor_tensor(
                    out=o_tile[:, k8, :], in0=o_tile[:, k8, :], scalar=cx.gains[:, hn_off + k8:hn_off + k8 + 1],
                    in1=rstdh[:, k8 // 2, :], op0=ALU.mult, op1=ALU.mult),
                    reads=[o_tile, rstdh, cx.gains], writes=[o_tile])
            fw.op("dve", lambda e: e.tensor_tensor(out=o_n[:, :, :], in0=o_tile[:, :, :], in1=sg[:, :, :], op=ALU.mult),
                  reads=[o_tile, sg], writes=[o_n])
            out_proj(fw, cx, tb, wo, o_n, pA)
            epilogue(fw, cx, tb, t, ht, h_out, hkey_out, g_post)
        fw.end_phase()


S_FULL = 4096
NSMALL = 160
G_OFF, CW_OFF, HN_OFF = 0, 128, 152


def build_program(S=S_FULL):
    nc = bass.Bass("TRN2", target_bir_lowering=False)

    def din(name, shape, dt=F32):
        return nc.dram_tensor(name, list(shape), dt, kind="ExternalInput").ap()

    def dscr(name, shape, dt=BF16):
        return nc.dram_tensor(name, list(shape), dt).ap()

    x = din("x", [128, 8, S])
    y = nc.dram_tensor("y", [128, 8, S], F32, kind="ExternalOutput").ap()
    small = din("small", [128, NSMALL])
    tri_d = din("tri", [128, 128])
    neg_d = din("negones", [128, 128])
    masks_d = din("masks", [128, 4, 512])
    btn_d = din("btn", [64, 64])
    mask4_d = din("mask4", [64, 4, 64])
    w32 = {}
    wbf = {}

    def wpair(name, shape):
        w32[name] = din(name + "_f32", shape)
        wbf[name] = dscr(name + "_bf", shape)

    for j in range(2):
        wpair("sbqkv%d" % j, [6, 128, 8, 512])
        wpair("sbo%d" % j, [128, 8, 1024])
    wpair("cin", [8, 128, 8, 384])
    wpair("cout", [128, 8, 1024])
    wpair("gin", [6, 128, 8, 512])
    wpair("ga", [128, 8, 16])
    wpair("gg", [17, 512])
    wpair("go", [128, 8, 1024])
    for l in range(4):
        wpair("up%d" % l, [8, 128, 8, 512])
        wpair("dn%d" % l, [8, 128, 32, 128])
    qT = dscr("qT", [1024, S])
    kT = dscr("kT", [1024, S])
    v = dscr("v", [S, 1024])
    oT = dscr("oT", [1024, S])

    with ExitStack() as es:
        fw = FW(nc, es)
        cx = Ctx()
        setup_common(fw, cx, small, NSMALL)
        sbc = Ctx()
        sbc.tri = fw.sb("tri", [128, 128], BF16)
        sbc.negones = fw.sb("negones", [128, 128], BF16)
        sbc.masks = fw.sb("masks", [128, 4, 512], F32)
        gc = Ctx()
        gc.btn = fw.sb("btn", [64, 64], BF16)
        gc.mask4 = fw.sb("mask4", [64, 4, 64], F32)
        fw.dma("pool", [(sbc.tri[:, :], tri_d)], sbc.tri, writes=[sbc.tri])
        fw.dma("pool", [(sbc.negones[:, :], neg_d)], sbc.negones, writes=[sbc.negones])
        fw.dma("pool", [(gc.btn[:, :], btn_d)], gc.btn, writes=[gc.btn])
        fw.dma("sp", [(sbc.masks[:, :, :], masks_d)], sbc.masks, writes=[sbc.masks])
        fw.dma("sp", [(gc.mask4[:, :, :], mask4_d)], gc.mask4, writes=[gc.mask4])

        wb = {}

        def cast(name):
            shape = wbf[name].shape
            n = shape[0] if len(shape) == 4 else 1
            wb[name] = cast_weight(fw, cx, name, wbf[name], w32[name], n)

        order = ["sbqkv0", "sbo0", "up0", "dn0", "cin", "cout", "up1", "dn1",
                 "gin", "ga", "gg", "go", "up2", "dn2", "sbqkv1", "sbo1", "up3", "dn3"]
        for name in order:
            cast(name)

        def gi(l, k):
            return l * 4 + k

        def ffn(l, hin, hkin):
            ffn_phase(fw, cx, S, hin, y, hkin, "y", gi(l, 2), gi(l, 3), wbf["up%d" % l], wbf["dn%d" % l],
                      wb["up%d" % l], wb["dn%d" % l])

        def sb(l, j, hin, hkin):
            key = "sb%d" % j
            sb_proj_phase(fw, cx, S, hin, hkin, gi(l, 0), wbf["sbqkv%d" % j], wb["sbqkv%d" % j], qT, kT, v, key)
            sb_attn_phase(fw, cx, S, qT, kT, v, oT, key, sbc)
            sb_out_phase(fw, cx, S, hin, y, hkin, "y", gi(l, 1), wbf["sbo%d" % j], wb["sbo%d" % j], oT, key)

        sb(0, 0, x, "x")
        ffn(0, y, "y")
        conv_phase(fw, cx, S, y, y, "y", "y", gi(1, 0), gi(1, 1), wbf["cin"], wb["cin"], wbf["cout"], wb["cout"], CW_OFF)
        ffn(1, y, "y")
        gla_phase(fw, cx, S, y, y, "yg", "yg", gi(2, 0), gi(2, 1), wbf["gin"], wb["gin"], wbf["ga"], wb["ga"],
                  wbf["gg"], wb["gg"], wbf["go"], wb["go"], HN_OFF, gc)
        ffn(2, y, "y")
        sb(3, 1, y, "y")
        ffn(3, y, "y")
        fw.finish([cx.dram[("y", t)] for t in range(S // TT)])
        build_program.stats = fw.stats()
    return nc


def _fm(a):
    return np.ascontiguousarray(a.T.reshape(8, 128, -1).transpose(1, 0, 2))


def _blk512(W, nb):
    return np.ascontiguousarray(W.reshape(8, 128, nb, 512).transpose(2, 1, 0, 3))


def _rows(W):
    return np.ascontiguousarray(W.reshape(8, 128, W.shape[1]).transpose(1, 0, 2))


def prepare_inputs(x, norm_gains, sb_w_qkv, sb_w_o, conv_w_in, conv_w, conv_w_out,
                   gla_w_in, gla_w_gate_up, gla_b_gate, gla_head_norm, gla_w_o, ffn_w_up, ffn_w_down):
    f = np.float32
    shared = {}
    small = np.zeros((128, NSMALL), f)
    ng = np.asarray(norm_gains, f)
    small[:, 0:128] = ng.reshape(16, 8, 128).transpose(2, 0, 1).reshape(128, 128)
    small[:, 128:152] = np.asarray(conv_w, f)[0].reshape(3, 8, 128).transpose(2, 0, 1).reshape(128, 24)
    small[:, 152:160] = np.asarray(gla_head_norm, f)[0].reshape(8, 128).T
    shared["small"] = small
    k = np.arange(128)
    shared["tri"] = -(k[:, None] >= k[None, :]).astype(f)
    shared["negones"] = -np.ones((128, 128), f)
    col = np.arange(512)
    shared["masks"] = np.ascontiguousarray(
        np.stack([(col[None, :] > (r * 128 + k[:, None])).astype(f) for r in range(4)], 1))
    s = np.arange(64)
    shared["btn"] = np.ascontiguousarray(-(s[:, None] <= s[None, :]).astype(f) / 16.0)
    shared["mask4"] = np.ascontiguousarray(np.repeat((s[:, None] <= s[None, :]).astype(f)[:, None, :], 4, 1))
    for j in range(2):
        shared["sbqkv%d_f32" % j] = _blk512(np.asarray(sb_w_qkv[j], f), 6)
        shared["sbo%d_f32" % j] = _rows(np.asarray(sb_w_o[j], f))
    Win = np.asarray(conv_w_in[0], f)
    shared["cin_f32"] = np.ascontiguousarray(
        Win.reshape(8, 128, 3, 8, 128).transpose(3, 1, 0, 2, 4).reshape(8, 128, 8, 384))
    shared["cout_f32"] = _rows(np.asarray(conv_w_out[0], f))
    Wg = np.asarray(gla_w_in[0], f)
    shared["gin_f32"] = _blk512(np.ascontiguousarray(Wg[:, :3072]), 6)
    shared["ga_f32"] = _rows(np.ascontiguousarray(Wg[:, 3072:]))
    shared["gg_f32"] = np.ascontiguousarray(
        np.concatenate([np.asarray(gla_w_gate_up[0], f), np.asarray(gla_b_gate[0], f)[None]], 0))
    shared["go_f32"] = _rows(np.asarray(gla_w_o[0], f))
    for l in range(4):
        shared["up%d_f32" % l] = _blk512(np.asarray(ffn_w_up[l], f), 8)
        Wd = np.asarray(ffn_w_down[l], f)
        shared["dn%d_f32" % l] = np.ascontiguousarray(Wd.reshape(32, 128, 8, 128).transpose(2, 1, 0, 3))
    xs = np.asarray(x, f)
    in_maps = []
    for b in range(8):
        m = dict(shared)
        m["x"] = _fm(xs[b])
        in_maps.append(m)
    return in_maps


_NC_CACHE = {}


def kernel(**inputs):
    in_maps = prepare_inputs(**inputs)
    if "nc" not in _NC_CACHE:
        _NC_CACHE["nc"] = build_program(S_FULL)
    nc = _NC_CACHE["nc"]
    res = run_bass_kernel_spmd(nc, in_maps, core_ids=list(range(8)))
    out = np.empty((8, S_FULL, 1024), np.float32)
    for b in range(8):
        yb = np.asarray(res.results[b]["y"])
        out[b] = yb.transpose(1, 0, 2).reshape(1024, S_FULL).T
    return out
```

```python
import numpy as np
from contextlib import ExitStack
import concourse.bass as bass
import concourse.mybir as mybir
from concourse.bass_utils import run_bass_kernel_spmd

F32 = mybir.dt.float32
BF16 = mybir.dt.bfloat16
AF = mybir.ActivationFunctionType
ALU = mybir.AluOpType


class Buf:
    __slots__ = ("name", "t", "w", "r", "sem", "tot")

    def __init__(self, name, t=None):
        self.name = name
        self.t = t
        self.w = None
        self.r = {}
        self.sem = None
        self.tot = 0

    def __getitem__(self, idx):
        return self.t[idx]


class Eng:
    def __init__(self, name, h, sem):
        self.name = name
        self.h = h
        self.sem = sem
        self.count = 0
        self.waited = {}
        self.nwaits = 0
        self.nins = 0


class FW:
    def __init__(self, nc, es):
        self.nc = nc
        self.es = es
        self.E = {}
        for name, h in (("pe", nc.tensor), ("act", nc.scalar), ("dve", nc.vector),
                        ("pool", nc.gpsimd), ("sp", nc.sync)):
            sem = es.enter_context(nc.semaphore("sem_" + name))
            self.E[name] = Eng(name, h, sem)
        self.nsem = 5
        self.uid = 0
        self.dma_slots = []
        self.free_sems = []
        self.phase_bufs = []

    def sb(self, name, shape, dt, stack=None):
        self.uid += 1
        t = (stack or self.es).enter_context(self.nc.sbuf_tensor(f"{name}_{self.uid}", list(shape), dt))
        b = Buf(name, t)
        if stack is not None:
            self.phase_bufs.append(b)
        return b

    def ps(self, name, shape=(128, 512), dt=F32, stack=None):
        self.uid += 1
        t = (stack or self.es).enter_context(self.nc.psum_tensor(f"{name}_{self.uid}", list(shape), dt))
        return Buf(name, t)

    def dsem(self, b):
        if b.sem is None:
            if b.t is not None and self.free_sems:
                b.sem, b.tot = self.free_sems.pop()
            else:
                b.sem = self.es.enter_context(self.nc.semaphore(f"dq_{b.name}_{self.nsem}"))
                self.nsem += 1
            if b.t is not None:
                self.dma_slots.append(b)
        return b.sem

    def end_phase(self):
        self.barrier()
        for b in self.phase_bufs:
            if b.sem is not None:
                self.free_sems.append((b.sem, b.tot))
                self.dma_slots.remove(b)
                b.sem = None
        self.phase_bufs = []

    def _waits(self, E, reads, writes):
        need = {}

        def add(ev, same_ok):
            if ev is None:
                return
            sem, val = ev
            if sem is E.sem and not same_ok:
                return
            k = id(sem)
            if k not in need or need[k][1] < val:
                need[k] = (sem, val)

        for b in reads:
            add(b.w, True)
        for b in writes:
            add(b.w, False)
            for sem_id, (sem, val) in b.r.items():
                add((sem, val), False)
        for k, (sem, val) in need.items():
            if E.waited.get(k, 0) >= val:
                continue
            E.h.wait_ge(sem, val)
            E.waited[k] = val
            E.nwaits += 1

    def _commit(self, ev, reads, writes):
        sem, val = ev
        k = id(sem)
        for b in reads:
            if k not in b.r or b.r[k][1] < val:
                b.r[k] = (sem, val)
        for b in writes:
            b.w = ev
            b.r = {}

    def op(self, eng, fn, reads=(), writes=()):
        E = self.E[eng]
        self._waits(E, reads, writes)
        ins = fn(E.h)
        E.count += 1
        E.nins += 1
        ins.then_inc(E.sem, 1)
        self._commit((E.sem, E.count), reads, writes)

    def ops(self, eng, fns, reads=(), writes=()):
        E = self.E[eng]
        self._waits(E, reads, writes)
        ins = None
        for fn in fns:
            ins = fn(E.h)
            E.nins += 1
        E.count += 1
        ins.then_inc(E.sem, 1)
        self._commit((E.sem, E.count), reads, writes)

    def dma(self, q, pairs, slot, reads=(), writes=()):
        E = self.E[q]
        sem = self.dsem(slot)
        self._waits(E, reads, writes)
        k = id(sem)
        if slot.tot > 0 and E.waited.get(k, 0) < slot.tot:
            E.h.wait_ge(sem, slot.tot)
            E.waited[k] = slot.tot
            E.nwaits += 1
        for (o, i) in pairs:
            E.h.dma_start(out=o, in_=i).then_inc(sem, 16)
            slot.tot += 16
            E.nins += 1
        self._commit((sem, slot.tot), reads, writes)

    def finish(self, bufs):
        E = self.E["sp"]
        self._waits(E, bufs, ())

    def barrier(self):
        SP = self.E["sp"]
        for b in self.dma_slots:
            k = id(b.sem)
            if b.tot > 0 and SP.waited.get(k, 0) < b.tot:
                SP.h.wait_ge(b.sem, b.tot)
                SP.waited[k] = b.tot
        SP.count += 1
        SP.h.nop().then_inc(SP.sem, 1)
        evs = []
        for E in self.E.values():
            if E.count > 0:
                evs.append((E.sem, E.count))
        for E in self.E.values():
            for sem, val in evs:
                if sem is E.sem:
                    continue
                k = id(sem)
                if E.waited.get(k, 0) < val:
                    E.h.wait_ge(sem, val)
                    E.waited[k] = val
                    E.nwaits += 1

    def stats(self):
        return {n: (e.nins, e.nwaits) for n, e in self.E.items()}


D = 1024
KC = 8
TT = 512
EPS = 1e-6


class Ctx:
    pass


def setup_common(fw, cx, gains_dram, ngains):
    nc = fw.nc
    cx.onesD = fw.sb("onesD", [128, 128], BF16)
    cx.ones256 = fw.sb("ones256", [128, 128], BF16)
    cx.gains = fw.sb("gains", [128, ngains], F32)
    fw.op("pool", lambda e: e.memset(cx.onesD[:, :], 1.0 / 1024), writes=[cx.onesD])
    fw.op("pool", lambda e: e.memset(cx.ones256[:, :], 1.0 / 256), writes=[cx.ones256])
    fw.dma("sp", [(cx.gains[:, :], gains_dram)], cx.gains, writes=[cx.gains])
    cx.dram = {}


def dbuf(cx, key):
    if key not in cx.dram:
        cx.dram[key] = Buf("dram_%s" % (key,))
    return cx.dram[key]


def cast_weight(fw, cx, key, dst_ap, src_ap, nsplit=1):
    b = dbuf(cx, key)
    if nsplit == 1:
        pairs = [(dst_ap, src_ap)]
    else:
        pairs = [(dst_ap[i], src_ap[i]) for i in range(nsplit)]
    fw.dma("pool", pairs, b, writes=[b])
    return b


def rms_rstd(fw, cx, src, nch, T, ones, sq, ps, rstd):
    fw.op("act", lambda e: e.activation(out=sq[:, 0:nch, 0:T], in_=src[:, 0:nch, 0:T], func=AF.Square),
          reads=[src], writes=[sq])
    fw.ops("pe", [
        (lambda e, c=c: e.matmul(ps[:, 0:T], ones[:, :], sq[:, c, 0:T], start=(c == 0), stop=(c == nch - 1)))
        for c in range(nch)], reads=[sq, ones], writes=[ps])
    fw.op("act", lambda e: e.activation(out=rstd[:, 0:T], in_=ps[:, 0:T], func=AF.Ln, bias=EPS, scale=1.0),
          reads=[ps], writes=[rstd])
    fw.op("act", lambda e: e.activation(out=rstd[:, 0:T], in_=rstd[:, 0:T], func=AF.Exp, scale=-0.5),
          reads=[rstd], writes=[rstd])


def ffn_phase(fw, cx, S, h_in, h_out, hkey_in, hkey_out, g_pre, g_post, wup, wdn, wup_b, wdn_b):
    nt = S // TT
    with ExitStack() as st:
        hts = [fw.sb("hT%d" % i, [128, KC, TT], F32, st) for i in range(2)]
        sq = fw.sb("sq", [128, KC, TT], BF16, st)
        xn = fw.sb("xn", [128, KC, TT], BF16, st)
        act = fw.sb("act", [128, 32, TT], BF16, st)
        mT = fw.sb("mT", [128, KC, TT], F32, st)
        tmp = fw.sb("tmp", [128, KC, TT], F32, st)
        rstd = fw.sb("rstd", [128, TT], F32, st)
        rstd2 = fw.sb("rstd2", [128, TT], F32, st)
        relus = [fw.sb("relu%d" % i, [128, TT], F32, st) for i in range(2)]
        wus = [fw.sb("wu%d" % i, [128, KC, 512], BF16, st) for i in range(2)]
        wds = [fw.sb("wd%d" % i, [128, 32, 128], BF16, st) for i in range(2)]
        ps_stat = fw.ps("ps_stat", stack=st)
        ps_up = [fw.ps("ps_up%d" % i, stack=st) for i in range(4)]
        ps_dn = [fw.ps("ps_dn%d" % i, stack=st) for i in range(2)]
        nu = 0
        nd = 0
        for t in range(nt):
            ht = hts[t % 2]
            tsl = slice(t * TT, (t + 1) * TT)
            hb_in = dbuf(cx, (hkey_in, t))
            hb_out = dbuf(cx, (hkey_out, t))
            fw.dma("sp", [(ht[:, :, :], h_in[:, :, tsl])], ht, reads=[hb_in], writes=[ht])
            rms_rstd(fw, cx, ht, KC, TT, cx.onesD, sq, ps_stat, rstd)
            for c in range(KC):
                fw.op("dve", lambda e, c=c: e.scalar_tensor_tensor(
                    out=xn[:, c, :], in0=ht[:, c, :], scalar=cx.gains[:, g_pre * KC + c:g_pre * KC + c + 1],
                    in1=rstd[:, :], op0=ALU.mult, op1=ALU.mult),
                    reads=[ht, rstd, cx.gains], writes=[xn])
            for blk in range(8):
                wu = wus[nu % 2]
                nu += 1
                fw.dma("sp", [(wu[:, :, :], wup[blk])], wu, reads=[wup_b], writes=[wu])
                for j in range(4):
                    f = blk * 4 + j
                    pu = ps_up[f % 4]
                    fw.ops("pe", [
                        (lambda e, c=c, j=j, pu=pu, wu=wu: e.matmul(pu[:, :], wu[:, c, j * 128:(j + 1) * 128], xn[:, c, :],
                                                                   start=(c == 0), stop=(c == KC - 1)))
                        for c in range(KC)], reads=[wu, xn], writes=[pu])
                    rl = relus[f % 2]
                    fw.op("act", lambda e, pu=pu, rl=rl: e.activation(out=rl[:, :], in_=pu[:, :], func=AF.Relu),
                          reads=[pu], writes=[rl])
                    fw.op("dve", lambda e, f=f, rl=rl: e.tensor_tensor(out=act[:, f, :], in0=rl[:, :], in1=rl[:, :], op=ALU.mult),
                          reads=[rl], writes=[act])
            for d in range(KC):
                wd = wds[nd % 2]
                nd += 1
                fw.dma("sp", [(wd[:, :, :], wdn[d])], wd, reads=[wdn_b], writes=[wd])
                pd = ps_dn[d % 2]
                fw.ops("pe", [
                    (lambda e, f=f, pd=pd, wd=wd: e.matmul(pd[:, :], wd[:, f, :], act[:, f, :],
                                                          start=(f == 0), stop=(f == 31)))
                    for f in range(32)], reads=[wd, act], writes=[pd])
                fw.op("act", lambda e, d=d, pd=pd: e.copy(out=mT[:, d, :], in_=pd[:, :]), reads=[pd], writes=[mT])
            rms_rstd(fw, cx, mT, KC, TT, cx.onesD, sq, ps_stat, rstd2)
            for c in range(KC):
                fw.op("dve", lambda e, c=c: e.scalar_tensor_tensor(
                    out=tmp[:, c, :], in0=mT[:, c, :], scalar=cx.gains[:, g_post * KC + c:g_post * KC + c + 1],
                    in1=rstd2[:, :], op0=ALU.mult, op1=ALU.mult),
                    reads=[mT, rstd2, cx.gains], writes=[tmp])
            fw.op("dve", lambda e: e.tensor_tensor(out=tmp[:, :, :], in0=tmp[:, :, :], in1=ht[:, :, :], op=ALU.add),
                  reads=[tmp, ht], writes=[tmp])
            fw.dma("pool", [(h_out[:, :, tsl], tmp[:, :, :])], tmp, reads=[tmp], writes=[hb_out])
        fw.end_phase()


class TileBufs:
    pass


def alloc_tile_bufs(fw, st, need_m=True, TT=TT, nh=2):
    tb = TileBufs()
    tb.TT = TT
    tb.hts = [fw.sb("hT%d" % i, [128, KC, TT], F32, st) for i in range(nh)]
    tb.sq = fw.sb("sq", [128, KC, TT], BF16, st)
    tb.xn = fw.sb("xn", [128, KC, TT], BF16, st)
    tb.rstd = fw.sb("rstd", [128, TT], F32, st)
    tb.ps_stat = fw.ps("ps_stat", stack=st)
    if need_m:
        tb.mT = fw.sb("mT", [128, KC, TT], F32, st)
        tb.tmp = fw.sb("tmp", [128, KC, TT], F32, st)
        tb.rstd2 = fw.sb("rstd2", [128, TT], F32, st)
    return tb


def prologue(fw, cx, tb, t, h_in, hkey_in, g_pre):
    TT = tb.TT
    ht = tb.hts[t % len(tb.hts)]
    tsl = slice(t * TT, (t + 1) * TT)
    fw.dma("sp", [(ht[:, :, :], h_in[:, :, tsl])], ht, reads=[dbuf(cx, (hkey_in, t))], writes=[ht])
    rms_rstd(fw, cx, ht, KC, TT, cx.onesD, tb.sq, tb.ps_stat, tb.rstd)
    for c in range(KC):
        fw.op("dve", lambda e, c=c: e.scalar_tensor_tensor(
            out=tb.xn[:, c, :], in0=ht[:, c, :], scalar=cx.gains[:, g_pre * KC + c:g_pre * KC + c + 1],
            in1=tb.rstd[:, :], op0=ALU.mult, op1=ALU.mult),
            reads=[ht, tb.rstd, cx.gains], writes=[tb.xn])
    return ht


def load_h(fw, cx, tb, t, h_in, hkey_in):
    TT = tb.TT
    ht = tb.hts[t % len(tb.hts)]
    tsl = slice(t * TT, (t + 1) * TT)
    fw.dma("sp", [(ht[:, :, :], h_in[:, :, tsl])], ht, reads=[dbuf(cx, (hkey_in, t))], writes=[ht])
    return ht


def epilogue(fw, cx, tb, t, ht, h_out, hkey_out, g_post):
    TT = tb.TT
    tsl = slice(t * TT, (t + 1) * TT)
    rms_rstd(fw, cx, tb.mT, KC, TT, cx.onesD, tb.sq, tb.ps_stat, tb.rstd2)
    for c in range(KC):
        fw.op("dve", lambda e, c=c: e.scalar_tensor_tensor(
            out=tb.tmp[:, c, :], in0=tb.mT[:, c, :], scalar=cx.gains[:, g_post * KC + c:g_post * KC + c + 1],
            in1=tb.rstd2[:, :], op0=ALU.mult, op1=ALU.mult),
            reads=[tb.mT, tb.rstd2, cx.gains], writes=[tb.tmp])
    fw.op("dve", lambda e: e.tensor_tensor(out=tb.tmp[:, :, :], in0=tb.tmp[:, :, :], in1=ht[:, :, :], op=ALU.add),
          reads=[tb.tmp, ht], writes=[tb.tmp])
    fw.dma("pool", [(h_out[:, :, tsl], tb.tmp[:, :, :])], tb.tmp, reads=[tb.tmp], writes=[dbuf(cx, (hkey_out, t))])


def out_proj(fw, cx, tb, wo, src, ps_list):
    for d in range(KC):
        pd = ps_list[d % len(ps_list)]
        fw.ops("pe", [
            (lambda e, c=c, d=d, pd=pd: e.matmul(pd[:, 0:tb.TT], wo[:, c, d * 128:(d + 1) * 128], src[:, c, :],
                                                 start=(c == 0), stop=(c == KC - 1)))
            for c in range(KC)], reads=[wo, src], writes=[pd])
        fw.op("act", lambda e, d=d, pd=pd: e.copy(out=tb.mT[:, d, :], in_=pd[:, 0:tb.TT]), reads=[pd], writes=[tb.mT])


def sb_proj_phase(fw, cx, S, h_in, hkey_in, g_pre, wqkv, wqkv_b, qT, kT, v, key):
    nt = S // TT
    with ExitStack() as st:
        tb = alloc_tile_bufs(fw, st, need_m=False)
        ws = [fw.sb("wqkv%d" % i, [128, KC, 512], BF16, st) for i in range(2)]
        qk_sb = [fw.sb("qk_sb%d" % i, [128, 4, TT], BF16, st) for i in range(2)]
        v_sb = fw.sb("v_sb", [128, 4, 1024], BF16, st)
        pss = [fw.ps("ps_p%d" % i, stack=st) for i in range(4)]
        nw = 0
        npz = 0
        for t in range(nt):
            tsl = slice(t * TT, (t + 1) * TT)
            prologue(fw, cx, tb, t, h_in, hkey_in, g_pre)
            for blk in range(6):
                w = ws[nw % 2]
                nw += 1
                fw.dma("sp", [(w[:, :, :], wqkv[blk])], w, reads=[wqkv_b], writes=[w])
                if blk < 4:
                    dst = qT if blk < 2 else kT
                    stage = qk_sb[blk % 2]
                    for j in range(4):
                        p = pss[npz % 4]
                        npz += 1
                        fw.ops("pe", [
                            (lambda e, c=c, j=j, p=p, w=w: e.matmul(p[:, :], w[:, c, j * 128:(j + 1) * 128], tb.xn[:, c, :],
                                                                   start=(c == 0), stop=(c == KC - 1)))
                            for c in range(KC)], reads=[w, tb.xn], writes=[p])
                        fw.op("act", lambda e, j=j, p=p, stage=stage: e.copy(out=stage[:, j, :], in_=p[:, :]),
                              reads=[p], writes=[stage])
                    r0 = (blk % 2) * 512
                    fw.dma("pool", [(dst[r0:r0 + 512, tsl].rearrange("(j p) t -> p j t", p=128), stage[:, :, :])],
                           stage, reads=[stage], writes=[dbuf(cx, (key + ("q" if blk < 2 else "k"), t, blk % 2))])
                else:
                    vb = blk - 4
                    for tk in range(4):
                        p = pss[npz % 4]
                        npz += 1
                        fw.ops("pe", [
                            (lambda e, c=c, tk=tk, p=p, w=w: e.matmul(p[:, :], tb.xn[:, c, tk * 128:(tk + 1) * 128], w[:, c, :],
                                                                     start=(c == 0), stop=(c == KC - 1)))
                            for c in range(KC)], reads=[w, tb.xn], writes=[p])
                        fw.op("act", lambda e, tk=tk, p=p, vb=vb: e.copy(out=v_sb[:, tk, vb * 512:(vb + 1) * 512], in_=p[:, :]),
                              reads=[p], writes=[v_sb])
            fw.dma("pool", [(v[tsl, :].rearrange("(b p) f -> p b f", p=128), v_sb[:, :, :])], v_sb,
                   reads=[v_sb], writes=[dbuf(cx, (key + "v", t))])
        fw.end_phase()


def sb_attn_phase(fw, cx, S, qT, kT, v, oT, key, consts):
    nT = S // 512
    nB = S // 128
    H = 16
    with ExitStack() as st:
        kts = [fw.sb("kt%d" % i, [64, S], BF16, st) for i in range(2)]
        qts = [fw.sb("qt%d" % i, [64, S], BF16, st) for i in range(2)]
        vhs = [fw.sb("vh%d" % i, [128, nB, 64], BF16, st) for i in range(2)]
        es_ = [fw.sb("e%d" % i, [128, 512], F32, st) for i in range(4)]
        sps = [fw.sb("sp%d" % i, [128, 512], BF16, st) for i in range(2)]
        lsums = [fw.sb("lsum%d" % i, [128, 512], BF16, st) for i in range(2)]
        e2s = [fw.sb("e2%d" % i, [128, 512], F32, st) for i in range(2)]
        ws_ = [fw.sb("w%d" % i, [128, 512], BF16, st) for i in range(3)]
        osb = [fw.sb("osb%d" % i, [64, 512], BF16, st) for i in range(2)]
        pz = [fw.ps("pz%d" % i, stack=st) for i in range(2)]
        psum_s = [fw.ps("pss%d" % i, stack=st) for i in range(2)]
        po = [fw.ps("po%d" % i, stack=st) for i in range(2)]
        pj = fw.ps("pj", stack=st)
        junk = fw.sb("junk", [128, 256], BF16, st)
        tri, negones, masks = consts.tri, consts.negones, consts.masks
        fw.op("dve", lambda e: e.memset(junk[:, :], 1.0), writes=[junk])
        PE = fw.E["pe"].h

        def filler(n):
            for _ in range(n):
                PE.matmul(pj[:, 0:256], tri[:, :], junk[:, :], start=True, stop=True)
        kdeps = [dbuf(cx, (key + "k", t, b)) for t in range(S // TT) for b in range(2)]
        qdeps = [dbuf(cx, (key + "q", t, b)) for t in range(S // TT) for b in range(2)]
        vdeps = [dbuf(cx, (key + "v", t)) for t in range(S // TT)]
        tiles = []
        g = 0
        for h in range(H):
            for T in range(nT):
                Js = list(range(4 * T + 3, -1, -1))
                for idx, J in enumerate(Js):
                    tiles.append((h, T, J, idx, len(Js), g))
                g += 1
        N = len(tiles)
        loaded = set()

        def load_head(h):
            if h in loaded or h >= H:
                return
            loaded.add(h)
            kt, qt, vh = kts[h % 2], qts[h % 2], vhs[h % 2]
            fw.dma("sp", [(kt[:, :], kT[h * 64:(h + 1) * 64, :])], kt, reads=kdeps, writes=[kt])
            fw.dma("sp", [(qt[:, :], qT[h * 64:(h + 1) * 64, :])], qt, reads=qdeps, writes=[qt])
            fw.dma("sp", [(vh[:, :, :], v[:, h * 64:(h + 1) * 64].rearrange("(b p) d -> p b d", p=128))], vh,
                   reads=vdeps, writes=[vh])

        def Amm(i):
            h, T, J, idx, n, g = tiles[i]
            load_head(h)
            kt, qt = kts[h % 2], qts[h % 2]
            z_b = pz[i % 2]
            fw.op("pe", lambda e: e.matmul(z_b[:, :], kt[:, J * 128:(J + 1) * 128], qt[:, T * 512:(T + 1) * 512],
                                           start=True, stop=True), reads=[kt, qt], writes=[z_b])

        def Aexp(i):
            h, T, J, idx, n, g = tiles[i]
            z_b, e_b = pz[i % 2], es_[i % 4]
            fw.op("act", lambda e: e.activation(out=e_b[:, :], in_=z_b[:, :], func=AF.Exp, scale=0.125),
                  reads=[z_b], writes=[e_b])
            r = J - 4 * T
            if r >= 0:
                fw.op("dve", lambda e: e.tensor_tensor(out=e_b[:, :], in0=e_b[:, :], in1=masks[:, r, :], op=ALU.mult),
                      reads=[e_b, masks], writes=[e_b])

        def Bln(i):
            e_b, sp_b = es_[i % 4], sps[i % 2]
            fw.op("act", lambda e: e.activation(out=sp_b[:, :], in_=e_b[:, :], func=AF.Ln, bias=1.0, scale=1.0),
                  reads=[e_b], writes=[sp_b])

        def C(i):
            h, T, J, idx, n, g = tiles[i]
            sp_b, s_b = sps[i % 2], psum_s[i % 2]
            l_old, l_new = lsums[i % 2], lsums[(i + 1) % 2]
            if idx == 0:
                fw.op("pe", lambda e: e.matmul(s_b[:, :], tri[:, :], sp_b[:, :], start=True, stop=True),
                      reads=[tri, sp_b], writes=[s_b])
                fw.op("dve", lambda e: e.tensor_copy(out=l_new[:, :], in_=sp_b[:, :]), reads=[sp_b], writes=[l_new])
            else:
                fw.ops("pe", [
                    lambda e: e.matmul(s_b[:, :], tri[:, :], sp_b[:, :], start=True, stop=False),
                    lambda e: e.matmul(s_b[:, :], negones[:, :], l_old[:, :], start=False, stop=True)],
                    reads=[tri, negones, sp_b, l_old], writes=[s_b])
                if idx != n - 1:
                    fw.op("dve", lambda e: e.tensor_tensor(out=l_new[:, :], in0=l_old[:, :], in1=sp_b[:, :], op=ALU.add),
                          reads=[sp_b, l_old], writes=[l_new])

        def D(i):
            s_b, e2_b = psum_s[i % 2], e2s[i % 2]
            fw.op("act", lambda e: e.activation(out=e2_b[:, :], in_=s_b[:, :], func=AF.Exp), reads=[s_b], writes=[e2_b])

        def E(i):
            e2_b, e_b, w_b = e2s[i % 2], es_[i % 4], ws_[i % 3]
            fw.op("dve", lambda e: e.tensor_tensor(out=w_b[:, :], in0=e_b[:, :], in1=e2_b[:, :], op=ALU.mult),
                  reads=[e_b, e2_b], writes=[w_b])

        def F(i):
            h, T, J, idx, n, g = tiles[i]
            vh, w_b, pout, ob = vhs[h % 2], ws_[i % 3], po[g % 2], osb[g % 2]
            fw.op("pe", lambda e: e.matmul(pout[0:64, :], vh[:, J, :], w_b[:, :], start=(idx == 0), stop=(idx == n - 1)),
                  reads=[vh, w_b], writes=[pout])
            if idx == n - 1:
                fw.op("dve", lambda e: e.tensor_copy(out=ob[:, :], in_=pout[0:64, :]), reads=[pout], writes=[ob])
                fw.dma("pool", [(oT[h * 64:(h + 1) * 64, T * 512:(T + 1) * 512], ob[:, :])], ob, reads=[ob],
                       writes=[dbuf(cx, (key + "o", h, T))])
                if T == nT - 1:
                    load_head(h + 2)

        load_head(0)
        load_head(1)
        for s_ in range(-2, N + 2):
            if 0 <= s_ + 1 < N:
                Aexp(s_ + 1)
            if 0 <= s_ - 2 < N:
                E(s_ - 2)
            if 0 <= s_ - 1 < N:
                C(s_ - 1)
            if 0 <= s_ - 2 < N:
                F(s_ - 2)
            if 0 <= s_ + 2 < N:
                Amm(s_ + 2)
            if 0 <= s_ < N:
                filler(NFILL)
                Bln(s_)
            if 0 <= s_ - 1 < N:
                D(s_ - 1)
        fw.end_phase()


def sb_out_phase(fw, cx, S, h_in, h_out, hkey_in, hkey_out, g_post, wo_dram, wo_b, oT, key):
    nt = S // TT
    with ExitStack() as st:
        tb = alloc_tile_bufs(fw, st, need_m=True)
        wo = fw.sb("wo", [128, KC, 1024], BF16, st)
        ots = [fw.sb("ot%d" % i, [128, KC, TT], BF16, st) for i in range(2)]
        pss = [fw.ps("ps_o%d" % i, stack=st) for i in range(2)]
        fw.dma("sp", [(wo[:, :, :], wo_dram)], wo, reads=[wo_b], writes=[wo])
        for t in range(nt):
            tsl = slice(t * TT, (t + 1) * TT)
            ht = load_h(fw, cx, tb, t, h_in, hkey_in)
            ot = ots[t % 2]
            odeps = [dbuf(cx, (key + "o", h, t)) for h in range(16)]
            fw.dma("sp", [(ot[:, :, :], oT[:, tsl].rearrange("(c p) t -> p c t", p=128))], ot, reads=odeps, writes=[ot])
            out_proj(fw, cx, tb, wo, ot, pss)
            epilogue(fw, cx, tb, t, ht, h_out, hkey_out, g_post)
        fw.end_phase()


def conv_phase(fw, cx, S, h_in, h_out, hkey_in, hkey_out, g_pre, g_post, wcin, wcin_b, wo_dram, wo_b, cw_off):
    nt = S // TT
    with ExitStack() as st:
        tb = alloc_tile_bufs(fw, st, need_m=True)
        wo = fw.sb("wo", [128, KC, 1024], BF16, st)
        wis = [fw.sb("wci%d" % i, [128, KC, 384], BF16, st) for i in range(2)]
        hcs = [fw.sb("hc%d" % i, [128, TT + 2], F32, st) for i in range(8)]
        cs = [fw.sb("cs%d" % i, [128, TT], F32, st) for i in range(2)]
        t0s = [fw.sb("t0%d" % i, [128, TT], F32, st) for i in range(2)]
        gated = fw.sb("gated", [128, KC, TT], BF16, st)
        pss = [fw.ps("ps_c%d" % i, stack=st) for i in range(6)]
        pso = [fw.ps("ps_o%d" % i, stack=st) for i in range(1)]
        fw.dma("sp", [(wo[:, :, :], wo_dram)], wo, reads=[wo_b], writes=[wo])
        nw = 0
        for t in range(nt):
            ht = prologue(fw, cx, tb, t, h_in, hkey_in, g_pre)
            for i in range(8):
                w = wis[nw % 2]
                pb, pc, pu = pss[(nw % 2) * 3 + 0], pss[(nw % 2) * 3 + 1], pss[(nw % 2) * 3 + 2]
                c_sb, t0 = cs[nw % 2], t0s[nw % 2]
                nw += 1
                hc = hcs[i]
                fw.dma("sp", [(w[:, :, :], wcin[i])], w, reads=[wcin_b], writes=[w])
                for gi, p in ((1, pc), (2, pu), (0, pb)):
                    fw.ops("pe", [
                        (lambda e, c=c, gi=gi, p=p, w=w: e.matmul(p[:, :], w[:, c, gi * 128:(gi + 1) * 128], tb.xn[:, c, :],
                                                                 start=(c == 0), stop=(c == KC - 1)))
                        for c in range(KC)], reads=[w, tb.xn], writes=[p])
                if t == 0:
                    fw.op("dve", lambda e, hc=hc: e.memset(hc[:, 0:2], 0.0), writes=[hc])
                else:
                    fw.op("dve", lambda e, hc=hc: e.tensor_copy(out=hc[:, 0:2], in_=hc[:, TT:TT + 2]), reads=[hc], writes=[hc])
                fw.op("act", lambda e, c_sb=c_sb, pc=pc: e.copy(out=c_sb[:, :], in_=pc[:, :]), reads=[pc], writes=[c_sb])
                fw.op("dve", lambda e, hc=hc, c_sb=c_sb, pu=pu: e.tensor_tensor(out=hc[:, 2:TT + 2], in0=c_sb[:, :], in1=pu[:, :], op=ALU.mult),
                      reads=[c_sb, pu, hc], writes=[hc])
                k0 = cw_off + 0 * 8 + i
                k1 = cw_off + 1 * 8 + i
                k2 = cw_off + 2 * 8 + i
                fw.op("dve", lambda e, hc=hc, t0=t0, k0=k0: e.tensor_scalar(out=t0[:, :], in0=hc[:, 0:TT], scalar1=cx.gains[:, k0:k0 + 1],
                                                                          scalar2=None, op0=ALU.mult),
                      reads=[hc, cx.gains], writes=[t0])
                fw.op("dve", lambda e, hc=hc, t0=t0, k1=k1: e.scalar_tensor_tensor(out=t0[:, :], in0=hc[:, 1:TT + 1], scalar=cx.gains[:, k1:k1 + 1],
                                                                                  in1=t0[:, :], op0=ALU.mult, op1=ALU.add),
                      reads=[hc, t0, cx.gains], writes=[t0])
                fw.op("dve", lambda e, hc=hc, t0=t0, k2=k2: e.scalar_tensor_tensor(out=t0[:, :], in0=hc[:, 2:TT + 2], scalar=cx.gains[:, k2:k2 + 1],
                                                                                  in1=t0[:, :], op0=ALU.mult, op1=ALU.add),
                      reads=[hc, t0, cx.gains], writes=[t0])
                fw.op("dve", lambda e, i=i, t0=t0, pb=pb: e.tensor_tensor(out=gated[:, i, :], in0=t0[:, :], in1=pb[:, :], op=ALU.mult),
                      reads=[t0, pb], writes=[gated])
            out_proj(fw, cx, tb, wo, gated, pso)
            epilogue(fw, cx, tb, t, ht, h_out, hkey_out, g_post)
        fw.end_phase()


NFILL = 6
TG = 256
CH = 64


def gla_phase(fw, cx, S, h_in, h_out, hkey_in, hkey_out, g_pre, g_post, win_d, win_b, wa_d, wa_b, wg_d, wg_b,
              wo_dram, wo_b, hn_off, gc):
    nt = S // TG
    NCH = TG // CH
    with ExitStack() as st:
        tb = alloc_tile_bufs(fw, st, need_m=True, TT=TG, nh=1)
        wo = fw.sb("wo", [128, KC, 1024], BF16, st)
        win = [fw.sb("win%d" % i, [128, KC, 512], BF16, st) for i in range(6)]
        wa = fw.sb("wa", [128, KC, 16], BF16, st)
        wg = fw.sb("wg", [32, 512], BF16, st)
        a_ext = fw.sb("a_ext", [32, TG], BF16, st)
        qT_sb = fw.sb("qT_sb", [128, 4, TG], F32, st)
        kT_sb = fw.sb("kT_sb", [128, 4, TG], F32, st)
        sg = fw.sb("sg", [128, KC, TG], BF16, st)
        o_n = fw.sb("o_n", [128, KC, TG], BF16, st)
        egs = [fw.sb("eg%d" % i, [128, TG], F32, st) for i in range(2)]
        ex = fw.sb("ex", [64, 512], F32, st)
        sp = fw.sb("sp", [64, 512], BF16, st)
        ek = fw.sb("ek", [64, 512], F32, st)
        kt_tok = fw.sb("kt_tok", [64, 512], BF16, st)
        v_tok = fw.sb("v_tok", [64, 1024], BF16, st)
        eq = fw.sb("eq", [128, 4, CH], F32, st)
        ekT = fw.sb("ekT", [128, 4, CH], F32, st)
        qtl = fw.sb("qtl", [128, 4, CH], BF16, st)
        ktl = fw.sb("ktl", [128, 4, CH], BF16, st)
        sT = fw.sb("sT", [64, 4, CH], BF16, st)
        state = fw.sb("state", [128, 4, 256], F32, st)
        state_bf = fw.sb("state_bf", [128, 4, 256], BF16, st)
        rstdh = fw.sb("rstdh", [128, 4, TG], F32, st)
        pA = [fw.ps("pA%d" % i, stack=st) for i in range(2)]
        pct = fw.ps("pct", [128, 256], stack=st)
        pst = fw.ps("pst", [128, 256], stack=st)
        po = fw.ps("po", stack=st)
        pkv = [fw.ps("pkv%d" % i, stack=st) for i in range(2)]
        o_tile = tb.tmp

        fw.dma("sp", [(wo[:, :, :], wo_dram)], wo, reads=[wo_b], writes=[wo])
        for i in range(6):
            fw.dma("sp", [(win[i][:, :, :], win_d[i])], win[i], reads=[win_b], writes=[win[i]])
        fw.dma("sp", [(wa[:, :, :], wa_d)], wa, reads=[wa_b], writes=[wa])
        fw.dma("sp", [(wg[0:17, :], wg_d)], wg, reads=[wg_b], writes=[wg])
        fw.op("dve", lambda e: e.memset(a_ext[:, :], 1.0), writes=[a_ext])
        fw.op("dve", lambda e: e.memset(state[:, :, :], 0.0), writes=[state])
        fw.op("dve", lambda e: e.memset(state_bf[:, :, :], 0.0), writes=[state_bf])
        npa = [0]

        def nextp():
            p = pA[npa[0] % 2]
            npa[0] += 1
            return p

        def proj_fm(w, j0, width, p, N=TG):
            fw.ops("pe", [
                (lambda e, c=c: e.matmul(p[0:width, 0:N], w[:, c, j0:j0 + width], tb.xn[:, c, 0:N],
                                         start=(c == 0), stop=(c == KC - 1)))
                for c in range(KC)], reads=[w, tb.xn], writes=[p])

        for t in range(nt):
            ht = prologue(fw, cx, tb, t, h_in, hkey_in, g_pre)
            for hh in range(4):
                p = nextp()
                proj_fm(win[0], hh * 128, 128, p)
                fw.op("act", lambda e, hh=hh, p=p: e.copy(out=qT_sb[:, hh, :], in_=p[:, 0:TG]), reads=[p], writes=[qT_sb])
            for hh in range(4):
                p = nextp()
                proj_fm(win[1], hh * 128, 128, p)
                fw.op("act", lambda e, hh=hh, p=p: e.copy(out=kT_sb[:, hh, :], in_=p[:, 0:TG]), reads=[p], writes=[kT_sb])
            for c8 in range(8):
                p = nextp()
                eg = egs[c8 % 2]
                proj_fm(win[4 + c8 // 4], (c8 % 4) * 128, 128, p)
                fw.op("act", lambda e, p=p, eg=eg: e.activation(out=eg[:, :], in_=p[:, 0:TG], func=AF.Exp, scale=-1.0),
                      reads=[p], writes=[eg])
                fw.op("dve", lambda e, eg=eg: e.tensor_scalar(out=eg[:, :], in0=eg[:, :], scalar1=1.0, scalar2=None, op0=ALU.add),
                      reads=[eg], writes=[eg])
                fw.op("dve", lambda e, eg=eg: e.reciprocal(out=eg[:, :], in_=eg[:, :]), reads=[eg], writes=[eg])
                fw.op("dve", lambda e, c8=c8, p=p, eg=eg: e.tensor_tensor(out=sg[:, c8, :], in0=eg[:, :], in1=p[:, 0:TG], op=ALU.mult),
                      reads=[eg, p], writes=[sg])
            p = nextp()
            proj_fm(wa, 0, 16, p)
            fw.op("act", lambda e, p=p: e.copy(out=a_ext[0:16, :], in_=p[0:16, 0:TG]), reads=[p], writes=[a_ext])
            for n in range(NCH):
                csl = slice(n * CH, (n + 1) * CH)
                px = nextp()
                fw.op("pe", lambda e, px=px: e.matmul(px[0:64, :], a_ext[0:17, csl], wg[0:17, :], start=True, stop=True),
                      reads=[a_ext, wg], writes=[px])
                fw.op("act", lambda e, px=px: e.activation(out=ex[:, :], in_=px[0:64, :], func=AF.Exp, scale=-1.0),
                      reads=[px], writes=[ex])
                fw.op("act", lambda e: e.activation(out=sp[:, :], in_=ex[:, :], func=AF.Ln, bias=1.0, scale=1.0),
                      reads=[ex], writes=[sp])
                pc = nextp()
                fw.op("pe", lambda e, pc=pc: e.matmul(pc[0:64, :], gc.btn[:, :], sp[:, :], start=True, stop=True),
                      reads=[gc.btn, sp], writes=[pc])
                fw.op("act", lambda e, pc=pc: e.activation(out=ek[:, :], in_=pc[0:64, :], func=AF.Exp, scale=-1.0),
                      reads=[pc], writes=[ek])
                pk = nextp()
                fw.ops("pe", [
                    (lambda e, c=c, pk=pk: e.matmul(pk[0:64, :], tb.xn[:, c, csl], win[1][:, c, :], start=(c == 0), stop=(c == KC - 1)))
                    for c in range(KC)], reads=[win[1], tb.xn], writes=[pk])
                fw.op("dve", lambda e, pk=pk: e.tensor_tensor(out=kt_tok[:, :], in0=ek[:, :], in1=pk[0:64, :], op=ALU.mult),
                      reads=[ek, pk], writes=[kt_tok])
                for vb in range(2):
                    pv = nextp()
                    fw.ops("pe", [
                        (lambda e, c=c, pv=pv, vb=vb: e.matmul(pv[0:64, :], tb.xn[:, c, csl], win[2 + vb][:, c, :],
                                                             start=(c == 0), stop=(c == KC - 1)))
                        for c in range(KC)], reads=[win[2 + vb], tb.xn], writes=[pv])
                    fw.op("act", lambda e, pv=pv, vb=vb: e.copy(out=v_tok[:, vb * 512:(vb + 1) * 512], in_=pv[0:64, :]),
                          reads=[pv], writes=[v_tok])
                fw.ops("pe", [
                    (lambda e, hh=hh: e.matmul(pct[:, hh * CH:(hh + 1) * CH], sp[:, hh * 128:(hh + 1) * 128], gc.btn[:, :],
                                               start=True, stop=True))
                    for hh in range(4)], reads=[sp, gc.btn], writes=[pct])
                fw.op("act", lambda e: e.activation(out=eq[:, :, :], in_=pct[:, :].rearrange("p (h c) -> p h c", h=4), func=AF.Exp),
                      reads=[pct], writes=[eq])
                fw.op("act", lambda e: e.activation(out=ekT[:, :, :], in_=pct[:, :].rearrange("p (h c) -> p h c", h=4), func=AF.Exp, scale=-1.0),
                      reads=[pct], writes=[ekT])
                fw.op("dve", lambda e: e.scalar_tensor_tensor(out=qtl[:, :, :], in0=qT_sb[:, :, csl], scalar=128 ** -0.5, in1=eq[:, :, :],
                                                             op0=ALU.mult, op1=ALU.mult),
                      reads=[qT_sb, eq], writes=[qtl])
                fw.op("dve", lambda e: e.tensor_tensor(out=ktl[:, :, :], in0=kT_sb[:, :, csl], in1=ekT[:, :, :], op=ALU.mult),
                      reads=[kT_sb, ekT], writes=[ktl])
                fw.ops("pe", [
                    (lambda e, hh=hh: e.matmul(pst[0:64, hh * CH:(hh + 1) * CH], ktl[:, hh, :], qtl[:, hh, :], start=True, stop=True))
                    for hh in range(4)], reads=[ktl, qtl], writes=[pst])
                fw.op("dve", lambda e: e.tensor_tensor(out=sT[:, :, :], in0=pst[0:64, :].rearrange("p (h c) -> p h c", h=4),
                                                       in1=gc.mask4[:, :, :], op=ALU.mult),
                      reads=[pst, gc.mask4], writes=[sT])
                mm = []
                for hh in range(4):
                    for ec in range(2):
                        col = (hh * 2 + ec) * CH
                        mm.append(lambda e, hh=hh, ec=ec, col=col: e.matmul(
                            po[:, col:col + CH], state_bf[:, hh, ec * 128:(ec + 1) * 128], qtl[:, hh, :], start=True, stop=False))
                        mm.append(lambda e, hh=hh, ec=ec, col=col: e.matmul(
                            po[:, col:col + CH], v_tok[:, hh * 256 + ec * 128:hh * 256 + (ec + 1) * 128], sT[:, hh, :],
                            start=False, stop=True))
                fw.ops("pe", mm, reads=[state_bf, qtl, v_tok, sT], writes=[po])
                fw.op("act", lambda e: e.copy(out=o_tile[:, :, csl], in_=po[:, :].rearrange("p (k c) -> p k c", k=8)),
                      reads=[po], writes=[o_tile])
                for hh in range(4):
                    pkvb = pkv[hh // 2]
                    fw.op("pe", lambda e, hh=hh, pkvb=pkvb: e.matmul(
                        pkvb[:, (hh % 2) * 256:(hh % 2 + 1) * 256], kt_tok[:, hh * 128:(hh + 1) * 128], v_tok[:, hh * 256:(hh + 1) * 256],
                        start=True, stop=True), reads=[kt_tok, v_tok], writes=[pkvb])
                for hh in range(4):
                    pkvb = pkv[hh // 2]
                    fw.op("dve", lambda e, hh=hh: e.tensor_scalar(out=state[:, hh, :], in0=state[:, hh, :], scalar1=eq[:, hh, CH - 1:CH],
                                                                 scalar2=None, op0=ALU.mult),
                          reads=[state, eq], writes=[state])
                    fw.op("dve", lambda e, hh=hh, pkvb=pkvb: e.scalar_tensor_tensor(
                        out=state[:, hh, :], in0=pkvb[:, (hh % 2) * 256:(hh % 2 + 1) * 256], scalar=eq[:, hh, CH - 1:CH],
                        in1=state[:, hh, :], op0=ALU.mult, op1=ALU.add),
                        reads=[pkvb, eq, state], writes=[state])
                fw.op("act", lambda e: e.copy(out=state_bf[:, :, :], in_=state[:, :, :]), reads=[state], writes=[state_bf])
            fw.op("act", lambda e: e.activation(out=tb.sq[:, :, :], in_=o_tile[:, :, :], func=AF.Square), reads=[o_tile], writes=[tb.sq])
            for hh in range(4):
                pkvb = pkv[hh // 2]
                fw.ops("pe", [
                    (lambda e, hh=hh, ec=ec, pkvb=pkvb: e.matmul(pkvb[:, (hh % 2) * 256:(hh % 2 + 1) * 256], cx.ones256[:, :],
                                                                 tb.sq[:, hh * 2 + ec, :], start=(ec == 0), stop=(ec == 1)))
                    for ec in range(2)], reads=[cx.ones256, tb.sq], writes=[pkvb])
            for half in range(2):
                fw.op("act", lambda e, half=half: e.activation(
                    out=rstdh[:, 2 * half:2 * half + 2, :], in_=pkv[half][:, :].rearrange("p (h c) -> p h c", h=2),
                    func=AF.Ln, bias=EPS, scale=1.0), reads=[pkv[half]], writes=[rstdh])
            fw.op("act", lambda e: e.activation(out=rstdh[:, :, :], in_=rstdh[:, :, :], func=AF.Exp, scale=-0.5),
                  reads=[rstdh], writes=[rstdh])
            for k8 in range(8):
                fw.op("dve", lambda e, k8=k8: e.scalar_tensor_tensor(
                    out=o_tile[:, k8, :], in0=o_tile[:, k8, :], scalar=cx.gains[:, hn_off + k8:hn_off + k8 + 1],
                    in1=rstdh[:, k8 // 2, :], op0=ALU.mult, op1=ALU.mult),
                    reads=[o_tile, rstdh, cx.gains], writes=[o_tile])
            fw.op("dve", lambda e: e.tensor_tensor(out=o_n[:, :, :], in0=o_tile[:, :, :], in1=sg[:, :, :], op=ALU.mult),
                  reads=[o_tile, sg], writes=[o_n])
            out_proj(fw, cx, tb, wo, o_n, pA)
            epilogue(fw, cx, tb, t, ht, h_out, hkey_out, g_post)
        fw.end_phase()


S_FULL = 4096
NSMALL = 160
G_OFF, CW_OFF, HN_OFF = 0, 128, 152


def build_program(S=S_FULL):
    nc = bass.Bass("TRN2", target_bir_lowering=False)

    def din(name, shape, dt=F32):
        return nc.dram_tensor(name, list(shape), dt, kind="ExternalInput").ap()

    def dscr(name, shape, dt=BF16):
        return nc.dram_tensor(name, list(shape), dt).ap()

    x = din("x", [128, 8, S])
    y = nc.dram_tensor("y", [128, 8, S], F32, kind="ExternalOutput").ap()
    small = din("small", [128, NSMALL])
    tri_d = din("tri", [128, 128])
    neg_d = din("negones", [128, 128])
    masks_d = din("masks", [128, 4, 512])
    btn_d = din("btn", [64, 64])
    mask4_d = din("mask4", [64, 4, 64])
    w32 = {}
    wbf = {}

    def wpair(name, shape):
        w32[name] = din(name + "_f32", shape)
        wbf[name] = dscr(name + "_bf", shape)

    for j in range(2):
        wpair("sbqkv%d" % j, [6, 128, 8, 512])
        wpair("sbo%d" % j, [128, 8, 1024])
    wpair("cin", [8, 128, 8, 384])
    wpair("cout", [128, 8, 1024])
    wpair("gin", [6, 128, 8, 512])
    wpair("ga", [128, 8, 16])
    wpair("gg", [17, 512])
    wpair("go", [128, 8, 1024])
    for l in range(4):
        wpair("up%d" % l, [8, 128, 8, 512])
        wpair("dn%d" % l, [8, 128, 32, 128])
    qT = dscr("qT", [1024, S])
    kT = dscr("kT", [1024, S])
    v = dscr("v", [S, 1024])
    oT = dscr("oT", [1024, S])

    with ExitStack() as es:
        fw = FW(nc, es)
        cx = Ctx()
        setup_common(fw, cx, small, NSMALL)
        sbc = Ctx()
        sbc.tri = fw.sb("tri", [128, 128], BF16)
        sbc.negones = fw.sb("negones", [128, 128], BF16)
        sbc.masks = fw.sb("masks", [128, 4, 512], F32)
        gc = Ctx()
        gc.btn = fw.sb("btn", [64, 64], BF16)
        gc.mask4 = fw.sb("mask4", [64, 4, 64], F32)
        fw.dma("pool", [(sbc.tri[:, :], tri_d)], sbc.tri, writes=[sbc.tri])
        fw.dma("pool", [(sbc.negones[:, :], neg_d)], sbc.negones, writes=[sbc.negones])
        fw.dma("pool", [(gc.btn[:, :], btn_d)], gc.btn, writes=[gc.btn])
        fw.dma("sp", [(sbc.masks[:, :, :], masks_d)], sbc.masks, writes=[sbc.masks])
        fw.dma("sp", [(gc.mask4[:, :, :], mask4_d)], gc.mask4, writes=[gc.mask4])

        wb = {}

        def cast(name):
            shape = wbf[name].shape
            n = shape[0] if len(shape) == 4 else 1
            wb[name] = cast_weight(fw, cx, name, wbf[name], w32[name], n)

        order = ["sbqkv0", "sbo0", "up0", "dn0", "cin", "cout", "up1", "dn1",
                 "gin", "ga", "gg", "go", "up2", "dn2", "sbqkv1", "sbo1", "up3", "dn3"]
        for name in order:
            cast(name)

        def gi(l, k):
            return l * 4 + k

        def ffn(l, hin, hkin):
            ffn_phase(fw, cx, S, hin, y, hkin, "y", gi(l, 2), gi(l, 3), wbf["up%d" % l], wbf["dn%d" % l],
                      wb["up%d" % l], wb["dn%d" % l])

        def sb(l, j, hin, hkin):
            key = "sb%d" % j
            sb_proj_phase(fw, cx, S, hin, hkin, gi(l, 0), wbf["sbqkv%d" % j], wb["sbqkv%d" % j], qT, kT, v, key)
            sb_attn_phase(fw, cx, S, qT, kT, v, oT, key, sbc)
            sb_out_phase(fw, cx, S, hin, y, hkin, "y", gi(l, 1), wbf["sbo%d" % j], wb["sbo%d" % j], oT, key)

        sb(0, 0, x, "x")
        ffn(0, y, "y")
        conv_phase(fw, cx, S, y, y, "y", "y", gi(1, 0), gi(1, 1), wbf["cin"], wb["cin"], wbf["cout"], wb["cout"], CW_OFF)
        ffn(1, y, "y")
        gla_phase(fw, cx, S, y, y, "yg", "yg", gi(2, 0), gi(2, 1), wbf["gin"], wb["gin"], wbf["ga"], wb["ga"],
                  wbf["gg"], wb["gg"], wbf["go"], wb["go"], HN_OFF, gc)
        ffn(2, y, "y")
        sb(3, 1, y, "y")
        ffn(3, y, "y")
        fw.finish([cx.dram[("y", t)] for t in range(S // TT)])
        build_program.stats = fw.stats()
    return nc


def _fm(a):
    return np.ascontiguousarray(a.T.reshape(8, 128, -1).transpose(1, 0, 2))


def _blk512(W, nb):
    return np.ascontiguousarray(W.reshape(8, 128, nb, 512).transpose(2, 1, 0, 3))


def _rows(W):
    return np.ascontiguousarray(W.reshape(8, 128, W.shape[1]).transpose(1, 0, 2))


def prepare_inputs(x, norm_gains, sb_w_qkv, sb_w_o, conv_w_in, conv_w, conv_w_out,
                   gla_w_in, gla_w_gate_up, gla_b_gate, gla_head_norm, gla_w_o, ffn_w_up, ffn_w_down):
    f = np.float32
    shared = {}
    small = np.zeros((128, NSMALL), f)
    ng = np.asarray(norm_gains, f)
    small[:, 0:128] = ng.reshape(16, 8, 128).transpose(2, 0, 1).reshape(128, 128)
    small[:, 128:152] = np.asarray(conv_w, f)[0].reshape(3, 8, 128).transpose(2, 0, 1).reshape(128, 24)
    small[:, 152:160] = np.asarray(gla_head_norm, f)[0].reshape(8, 128).T
    shared["small"] = small
    k = np.arange(128)
    shared["tri"] = -(k[:, None] >= k[None, :]).astype(f)
    shared["negones"] = -np.ones((128, 128), f)
    col = np.arange(512)
    shared["masks"] = np.ascontiguousarray(
        np.stack([(col[None, :] > (r * 128 + k[:, None])).astype(f) for r in range(4)], 1))
    s = np.arange(64)
    shared["btn"] = np.ascontiguousarray(-(s[:, None] <= s[None, :]).astype(f) / 16.0)
    shared["mask4"] = np.ascontiguousarray(np.repeat((s[:, None] <= s[None, :]).astype(f)[:, None, :], 4, 1))
    for j in range(2):
        shared["sbqkv%d_f32" % j] = _blk512(np.asarray(sb_w_qkv[j], f), 6)
        shared["sbo%d_f32" % j] = _rows(np.asarray(sb_w_o[j], f))
    Win = np.asarray(conv_w_in[0], f)
    shared["cin_f32"] = np.ascontiguousarray(
        Win.reshape(8, 128, 3, 8, 128).transpose(3, 1, 0, 2, 4).reshape(8, 128, 8, 384))
    shared["cout_f32"] = _rows(np.asarray(conv_w_out[0], f))
    Wg = np.asarray(gla_w_in[0], f)
    shared["gin_f32"] = _blk512(np.ascontiguousarray(Wg[:, :3072]), 6)
    shared["ga_f32"] = _rows(np.ascontiguousarray(Wg[:, 3072:]))
    shared["gg_f32"] = np.ascontiguousarray(
        np.concatenate([np.asarray(gla_w_gate_up[0], f), np.asarray(gla_b_gate[0], f)[None]], 0))
    shared["go_f32"] = _rows(np.asarray(gla_w_o[0], f))
    for l in range(4):
        shared["up%d_f32" % l] = _blk512(np.asarray(ffn_w_up[l], f), 8)
        Wd = np.asarray(ffn_w_down[l], f)
        shared["dn%d_f32" % l] = np.ascontiguousarray(Wd.reshape(32, 128, 8, 128).transpose(2, 1, 0, 3))
    xs = np.asarray(x, f)
    in_maps = []
    for b in range(8):
        m = dict(shared)
        m["x"] = _fm(xs[b])
        in_maps.append(m)
    return in_maps


_NC_CACHE = {}


def kernel(**inputs):
    in_maps = prepare_inputs(**inputs)
    if "nc" not in _NC_CACHE:
        _NC_CACHE["nc"] = build_program(S_FULL)
    nc = _NC_CACHE["nc"]
    res = run_bass_kernel_spmd(nc, in_maps, core_ids=list(range(8)))
    out = np.empty((8, S_FULL, 1024), np.float32)
    for b in range(8):
        yb = np.asarray(res.results[b]["y"])
        out[b] = yb.transpose(1, 0, 2).reshape(1024, S_FULL).T
    return out
```

```python
import numpy as np
from contextlib import ExitStack
import concourse.bass as bass
import concourse.mybir as mybir
from concourse.bass_utils import run_bass_kernel_spmd

F32 = mybir.dt.float32
BF16 = mybir.dt.bfloat16
AF = mybir.ActivationFunctionType
ALU = mybir.AluOpType


class Buf:
    __slots__ = ("name", "t", "w", "r", "sem", "tot")

    def __init__(self, name, t=None):
        self.name = name
        self.t = t
        self.w = None
        self.r = {}
        self.sem = None
        self.tot = 0

    def __getitem__(self, idx):
        return self.t[idx]


class Eng:
    def __init__(self, name, h, sem):
        self.name = name
        self.h = h
        self.sem = sem
        self.count = 0
        self.waited = {}
        self.nwaits = 0
        self.nins = 0


class FW:
    def __init__(self, nc, es):
        self.nc = nc
        self.es = es
        self.E = {}
        for name, h in (("pe", nc.tensor), ("act", nc.scalar), ("dve", nc.vector),
                        ("pool", nc.gpsimd), ("sp", nc.sync)):
            sem = es.enter_context(nc.semaphore("sem_" + name))
            self.E[name] = Eng(name, h, sem)
        self.nsem = 5
        self.uid = 0
        self.dma_slots = []
        self.free_sems = []
        self.phase_bufs = []

    def sb(self, name, shape, dt, stack=None):
        self.uid += 1
        t = (stack or self.es).enter_context(self.nc.sbuf_tensor(f"{name}_{self.uid}", list(shape), dt))
        b = Buf(name, t)
        if stack is not None:
            self.phase_bufs.append(b)
        return b

    def ps(self, name, shape=(128, 512), dt=F32, stack=None):
        self.uid += 1
        t = (stack or self.es).enter_context(self.nc.psum_tensor(f"{name}_{self.uid}", list(shape), dt))
        return Buf(name, t)

    def dsem(self, b):
        if b.sem is None:
            if b.t is not None and self.free_sems:
                b.sem, b.tot = self.free_sems.pop()
            else:
                b.sem = self.es.enter_context(self.nc.semaphore(f"dq_{b.name}_{self.nsem}"))
                self.nsem += 1
            if b.t is not None:
                self.dma_slots.append(b)
        return b.sem

    def end_phase(self):
        self.barrier()
        for b in self.phase_bufs:
            if b.sem is not None:
                self.free_sems.append((b.sem, b.tot))
                self.dma_slots.remove(b)
                b.sem = None
        self.phase_bufs = []

    def _waits(self, E, reads, writes):
        need = {}

        def add(ev, same_ok):
            if ev is None:
                return
            sem, val = ev
            if sem is E.sem and not same_ok:
                return
            k = id(sem)
            if k not in need or need[k][1] < val:
                need[k] = (sem, val)

        for b in reads:
            add(b.w, True)
        for b in writes:
            add(b.w, False)
            for sem_id, (sem, val) in b.r.items():
                add((sem, val), False)
        for k, (sem, val) in need.items():
            if E.waited.get(k, 0) >= val:
                continue
            E.h.wait_ge(sem, val)
            E.waited[k] = val
            E.nwaits += 1

    def _commit(self, ev, reads, writes):
        sem, val = ev
        k = id(sem)
        for b in reads:
            if k not in b.r or b.r[k][1] < val:
                b.r[k] = (sem, val)
        for b in writes:
            b.w = ev
            b.r = {}

    def op(self, eng, fn, reads=(), writes=()):
        E = self.E[eng]
        self._waits(E, reads, writes)
        ins = fn(E.h)
        E.count += 1
        E.nins += 1
        ins.then_inc(E.sem, 1)
        self._commit((E.sem, E.count), reads, writes)

    def ops(self, eng, fns, reads=(), writes=()):
        E = self.E[eng]
        self._waits(E, reads, writes)
        ins = None
        for fn in fns:
            ins = fn(E.h)
            E.nins += 1
        E.count += 1
        ins.then_inc(E.sem, 1)
        self._commit((E.sem, E.count), reads, writes)

    def dma(self, q, pairs, slot, reads=(), writes=()):
        E = self.E[q]
        sem = self.dsem(slot)
        self._waits(E, reads, writes)
        k = id(sem)
        if slot.tot > 0 and E.waited.get(k, 0) < slot.tot:
            E.h.wait_ge(sem, slot.tot)
            E.waited[k] = slot.tot
            E.nwaits += 1
        for (o, i) in pairs:
            E.h.dma_start(out=o, in_=i).then_inc(sem, 16)
            slot.tot += 16
            E.nins += 1
        self._commit((sem, slot.tot), reads, writes)

    def finish(self, bufs):
        E = self.E["sp"]
        self._waits(E, bufs, ())

    def barrier(self):
        SP = self.E["sp"]
        for b in self.dma_slots:
            k = id(b.sem)
            if b.tot > 0 and SP.waited.get(k, 0) < b.tot:
                SP.h.wait_ge(b.sem, b.tot)
                SP.waited[k] = b.tot
        SP.count += 1
        SP.h.nop().then_inc(SP.sem, 1)
        evs = []
        for E in self.E.values():
            if E.count > 0:
                evs.append((E.sem, E.count))
        for E in self.E.values():
            for sem, val in evs:
                if sem is E.sem:
                    continue
                k = id(sem)
                if E.waited.get(k, 0) < val:
                    E.h.wait_ge(sem, val)
                    E.waited[k] = val
                    E.nwaits += 1

    def stats(self):
        return {n: (e.nins, e.nwaits) for n, e in self.E.items()}


D = 1024
KC = 8
TT = 512
EPS = 1e-6


class Ctx:
    pass


def setup_common(fw, cx, gains_dram, ngains):
    nc = fw.nc
    cx.onesD = fw.sb("onesD", [128, 128], BF16)
    cx.ones256 = fw.sb("ones256", [128, 128], BF16)
    cx.gains = fw.sb("gains", [128, ngains], F32)
    fw.op("pool", lambda e: e.memset(cx.onesD[:, :], 1.0 / 1024), writes=[cx.onesD])
    fw.op("pool", lambda e: e.memset(cx.ones256[:, :], 1.0 / 256), writes=[cx.ones256])
    fw.dma("sp", [(cx.gains[:, :], gains_dram)], cx.gains, writes=[cx.gains])
    cx.dram = {}


def dbuf(cx, key):
    if key not in cx.dram:
        cx.dram[key] = Buf("dram_%s" % (key,))
    return cx.dram[key]


def cast_weight(fw, cx, key, dst_ap, src_ap, nsplit=1):
    b = dbuf(cx, key)
    if nsplit == 1:
        pairs = [(dst_ap, src_ap)]
    else:
        pairs = [(dst_ap[i], src_ap[i]) for i in range(nsplit)]
    fw.dma("pool", pairs, b, writes=[b])
    return b


def rms_rstd(fw, cx, src, nch, T, ones, sq, ps, rstd):
    fw.op("act", lambda e: e.activation(out=sq[:, 0:nch, 0:T], in_=src[:, 0:nch, 0:T], func=AF.Square),
          reads=[src], writes=[sq])
    fw.ops("pe", [
        (lambda e, c=c: e.matmul(ps[:, 0:T], ones[:, :], sq[:, c, 0:T], start=(c == 0), stop=(c == nch - 1)))
        for c in range(nch)], reads=[sq, ones], writes=[ps])
    fw.op("act", lambda e: e.activation(out=rstd[:, 0:T], in_=ps[:, 0:T], func=AF.Ln, bias=EPS, scale=1.0),
          reads=[ps], writes=[rstd])
    fw.op("act", lambda e: e.activation(out=rstd[:, 0:T], in_=rstd[:, 0:T], func=AF.Exp, scale=-0.5),
          reads=[rstd], writes=[rstd])


def ffn_phase(fw, cx, S, h_in, h_out, hkey_in, hkey_out, g_pre, g_post, wup, wdn, wup_b, wdn_b):
    nt = S // TT
    with ExitStack() as st:
        hts = [fw.sb("hT%d" % i, [128, KC, TT], F32, st) for i in range(2)]
        sqA = fw.sb("sqA", [128, KC, TT], BF16, st)
        sqB = fw.sb("sqB", [128, KC, TT], BF16, st)
        xns = [fw.sb("xn%d" % i, [128, KC, TT], BF16, st) for i in range(2)]
        act = fw.sb("act", [128, 32, TT], BF16, st)
        mT = fw.sb("mT", [128, KC, TT], F32, st)
        tmp = fw.sb("tmp", [128, KC, TT], F32, st)
        rstd = fw.sb("rstd", [128, TT], F32, st)
        rstd2 = fw.sb("rstd2", [128, TT], F32, st)
        relus = [fw.sb("relu%d" % i, [128, TT], F32, st) for i in range(2)]
        wus = [fw.sb("wu%d" % i, [128, KC, 512], BF16, st) for i in range(2)]
        wds = [fw.sb("wd%d" % i, [128, 32, 128], BF16, st) for i in range(2)]
        ps_statA = fw.ps("ps_statA", stack=st)
        ps_statB = fw.ps("ps_statB", stack=st)
        ps_up = [fw.ps("ps_up%d" % i, stack=st) for i in range(4)]
        ps_dn = [fw.ps("ps_dn%d" % i, stack=st) for i in range(2)]
        cnt = {"u": 0, "d": 0}

        def pre(t):
            ht, xn = hts[t % 2], xns[t % 2]
            tsl = slice(t * TT, (t + 1) * TT)
            fw.dma("sp", [(ht[:, :, :], h_in[:, :, tsl])], ht, reads=[dbuf(cx, (hkey_in, t))], writes=[ht])
            rms_rstd(fw, cx, ht, KC, TT, cx.onesD, sqA, ps_statA, rstd)
            for c in range(KC):
                fw.op("dve", lambda e, c=c: e.scalar_tensor_tensor(
                    out=xn[:, c, :], in0=ht[:, c, :], scalar=cx.gains[:, g_pre * KC + c:g_pre * KC + c + 1],
                    in1=rstd[:, :], op0=ALU.mult, op1=ALU.mult),
                    reads=[ht, rstd, cx.gains], writes=[xn])

        def up(t):
            xn = xns[t % 2]
            for blk in range(8):
                wu = wus[cnt["u"] % 2]
                cnt["u"] += 1
                fw.dma("sp", [(wu[:, :, :], wup[blk])], wu, reads=[wup_b], writes=[wu])
                for j in range(4):
                    f = blk * 4 + j
                    pu = ps_up[f % 4]
                    fw.ops("pe", [
                        (lambda e, c=c, j=j, pu=pu, wu=wu: e.matmul(pu[:, :], wu[:, c, j * 128:(j + 1) * 128], xn[:, c, :],
                                                                   start=(c == 0), stop=(c == KC - 1)))
                        for c in range(KC)], reads=[wu, xn], writes=[pu])
                    rl = relus[f % 2]
                    fw.op("act", lambda e, pu=pu, rl=rl: e.activation(out=rl[:, :], in_=pu[:, :], func=AF.Relu),
                          reads=[pu], writes=[rl])
                    fw.op("dve", lambda e, f=f, rl=rl: e.tensor_tensor(out=act[:, f, :], in0=rl[:, :], in1=rl[:, :], op=ALU.mult),
                          reads=[rl], writes=[act])

        def down(t):
            for d in range(KC):
                wd = wds[cnt["d"] % 2]
                cnt["d"] += 1
                fw.dma("sp", [(wd[:, :, :], wdn[d])], wd, reads=[wdn_b], writes=[wd])
                pd = ps_dn[d % 2]
                fw.ops("pe", [
                    (lambda e, f=f, pd=pd, wd=wd: e.matmul(pd[:, :], wd[:, f, :], act[:, f, :],
                                                          start=(f == 0), stop=(f == 31)))
                    for f in range(32)], reads=[wd, act], writes=[pd])
                fw.op("act", lambda e, d=d, pd=pd: e.copy(out=mT[:, d, :], in_=pd[:, :]), reads=[pd], writes=[mT])

        def post(t):
            ht = hts[t % 2]
            tsl = slice(t * TT, (t + 1) * TT)
            rms_rstd(fw, cx, mT, KC, TT, cx.onesD, sqB, ps_statB, rstd2)
            for c in range(KC):
                fw.op("dve", lambda e, c=c: e.scalar_tensor_tensor(
                    out=tmp[:, c, :], in0=mT[:, c, :], scalar=cx.gains[:, g_post * KC + c:g_post * KC + c + 1],
                    in1=rstd2[:, :], op0=ALU.mult, op1=ALU.mult),
                    reads=[mT, rstd2, cx.gains], writes=[tmp])
            fw.op("dve", lambda e: e.tensor_tensor(out=tmp[:, :, :], in0=tmp[:, :, :], in1=ht[:, :, :], op=ALU.add),
                  reads=[tmp, ht], writes=[tmp])
            fw.dma("pool", [(h_out[:, :, tsl], tmp[:, :, :])], tmp, reads=[tmp], writes=[dbuf(cx, (hkey_out, t))])

        pre(0)
        up(0)
        for t in range(nt):
            if t + 1 < nt:
                pre(t + 1)
            down(t)
            if t + 1 < nt:
                up(t + 1)
            post(t)
        fw.end_phase()


class TileBufs:
    pass


def alloc_tile_bufs(fw, st, need_m=True, TT=TT, nh=2):
    tb = TileBufs()
    tb.TT = TT
    tb.hts = [fw.sb("hT%d" % i, [128, KC, TT], F32, st) for i in range(nh)]
    tb.sq = fw.sb("sq", [128, KC, TT], BF16, st)
    tb.xn = fw.sb("xn", [128, KC, TT], BF16, st)
    tb.rstd = fw.sb("rstd", [128, TT], F32, st)
    tb.ps_stat = fw.ps("ps_stat", stack=st)
    if need_m:
        tb.mT = fw.sb("mT", [128, KC, TT], F32, st)
        tb.tmp = fw.sb("tmp", [128, KC, TT], F32, st)
        tb.rstd2 = fw.sb("rstd2", [128, TT], F32, st)
    return tb


def prologue(fw, cx, tb, t, h_in, hkey_in, g_pre):
    TT = tb.TT
    ht = tb.hts[t % len(tb.hts)]
    tsl = slice(t * TT, (t + 1) * TT)
    fw.dma("sp", [(ht[:, :, :], h_in[:, :, tsl])], ht, reads=[dbuf(cx, (hkey_in, t))], writes=[ht])
    rms_rstd(fw, cx, ht, KC, TT, cx.onesD, tb.sq, tb.ps_stat, tb.rstd)
    for c in range(KC):
        fw.op("dve", lambda e, c=c: e.scalar_tensor_tensor(
            out=tb.xn[:, c, :], in0=ht[:, c, :], scalar=cx.gains[:, g_pre * KC + c:g_pre * KC + c + 1],
            in1=tb.rstd[:, :], op0=ALU.mult, op1=ALU.mult),
            reads=[ht, tb.rstd, cx.gains], writes=[tb.xn])
    return ht


def load_h(fw, cx, tb, t, h_in, hkey_in):
    TT = tb.TT
    ht = tb.hts[t % len(tb.hts)]
    tsl = slice(t * TT, (t + 1) * TT)
    fw.dma("sp", [(ht[:, :, :], h_in[:, :, tsl])], ht, reads=[dbuf(cx, (hkey_in, t))], writes=[ht])
    return ht


def epilogue(fw, cx, tb, t, ht, h_out, hkey_out, g_post):
    TT = tb.TT
    tsl = slice(t * TT, (t + 1) * TT)
    rms_rstd(fw, cx, tb.mT, KC, TT, cx.onesD, tb.sq, tb.ps_stat, tb.rstd2)
    for c in range(KC):
        fw.op("dve", lambda e, c=c: e.scalar_tensor_tensor(
            out=tb.tmp[:, c, :], in0=tb.mT[:, c, :], scalar=cx.gains[:, g_post * KC + c:g_post * KC + c + 1],
            in1=tb.rstd2[:, :], op0=ALU.mult, op1=ALU.mult),
            reads=[tb.mT, tb.rstd2, cx.gains], writes=[tb.tmp])
    fw.op("dve", lambda e: e.tensor_tensor(out=tb.tmp[:, :, :], in0=tb.tmp[:, :, :], in1=ht[:, :, :], op=ALU.add),
          reads=[tb.tmp, ht], writes=[tb.tmp])
    fw.dma("pool", [(h_out[:, :, tsl], tb.tmp[:, :, :])], tb.tmp, reads=[tb.tmp], writes=[dbuf(cx, (hkey_out, t))])


def out_proj(fw, cx, tb, wo, src, ps_list):
    for d in range(KC):
        pd = ps_list[d % len(ps_list)]
        fw.ops("pe", [
            (lambda e, c=c, d=d, pd=pd: e.matmul(pd[:, 0:tb.TT], wo[:, c, d * 128:(d + 1) * 128], src[:, c, :],
                                                 start=(c == 0), stop=(c == KC - 1)))
            for c in range(KC)], reads=[wo, src], writes=[pd])
        fw.op("act", lambda e, d=d, pd=pd: e.copy(out=tb.mT[:, d, :], in_=pd[:, 0:tb.TT]), reads=[pd], writes=[tb.mT])


def sb_proj_phase(fw, cx, S, h_in, hkey_in, g_pre, wqkv, wqkv_b, qT, kT, v, key):
    nt = S // TT
    with ExitStack() as st:
        tb = alloc_tile_bufs(fw, st, need_m=False)
        ws = [fw.sb("wqkv%d" % i, [128, KC, 512], BF16, st) for i in range(2)]
        qk_sb = [fw.sb("qk_sb%d" % i, [128, 4, TT], BF16, st) for i in range(2)]
        v_sb = fw.sb("v_sb", [128, 4, 1024], BF16, st)
        pss = [fw.ps("ps_p%d" % i, stack=st) for i in range(4)]
        nw = 0
        npz = 0
        for t in range(nt):
            tsl = slice(t * TT, (t + 1) * TT)
            prologue(fw, cx, tb, t, h_in, hkey_in, g_pre)
            for blk in range(6):
                w = ws[nw % 2]
                nw += 1
                fw.dma("sp", [(w[:, :, :], wqkv[blk])], w, reads=[wqkv_b], writes=[w])
                if blk < 4:
                    dst = qT if blk < 2 else kT
                    stage = qk_sb[blk % 2]
                    for j in range(4):
                        p = pss[npz % 4]
                        npz += 1
                        fw.ops("pe", [
                            (lambda e, c=c, j=j, p=p, w=w: e.matmul(p[:, :], w[:, c, j * 128:(j + 1) * 128], tb.xn[:, c, :],
                                                                   start=(c == 0), stop=(c == KC - 1)))
                            for c in range(KC)], reads=[w, tb.xn], writes=[p])
                        fw.op("act", lambda e, j=j, p=p, stage=stage: e.copy(out=stage[:, j, :], in_=p[:, :]),
                              reads=[p], writes=[stage])
                    r0 = (blk % 2) * 512
                    fw.dma("pool", [(dst[r0:r0 + 512, tsl].rearrange("(j p) t -> p j t", p=128), stage[:, :, :])],
                           stage, reads=[stage], writes=[dbuf(cx, (key + ("q" if blk < 2 else "k"), t, blk % 2))])
                else:
                    vb = blk - 4
                    for tk in range(4):
                        p = pss[npz % 4]
                        npz += 1
                        fw.ops("pe", [
                            (lambda e, c=c, tk=tk, p=p, w=w: e.matmul(p[:, :], tb.xn[:, c, tk * 128:(tk + 1) * 128], w[:, c, :],
                                                                     start=(c == 0), stop=(c == KC - 1)))
                            for c in range(KC)], reads=[w, tb.xn], writes=[p])
                        fw.op("act", lambda e, tk=tk, p=p, vb=vb: e.copy(out=v_sb[:, tk, vb * 512:(vb + 1) * 512], in_=p[:, :]),
                              reads=[p], writes=[v_sb])
            fw.dma("pool", [(v[tsl, :].rearrange("(b p) f -> p b f", p=128), v_sb[:, :, :])], v_sb,
                   reads=[v_sb], writes=[dbuf(cx, (key + "v", t))])
        fw.end_phase()


def sb_attn_phase(fw, cx, S, qT, kT, v, oT, key, consts):
    nT = S // 512
    nB = S // 128
    H = 16
    with ExitStack() as st:
        kts = [fw.sb("kt%d" % i, [64, S], BF16, st) for i in range(2)]
        qts = [fw.sb("qt%d" % i, [64, S], BF16, st) for i in range(2)]
        vhs = [fw.sb("vh%d" % i, [128, nB, 64], BF16, st) for i in range(2)]
        es_ = [fw.sb("e%d" % i, [128, 512], F32, st) for i in range(4)]
        sps = [fw.sb("sp%d" % i, [128, 512], BF16, st) for i in range(2)]
        lsums = [fw.sb("lsum%d" % i, [128, 512], BF16, st) for i in range(2)]
        e2s = [fw.sb("e2%d" % i, [128, 512], F32, st) for i in range(2)]
        ws_ = [fw.sb("w%d" % i, [128, 512], BF16, st) for i in range(3)]
        osb = [fw.sb("osb%d" % i, [64, 512], BF16, st) for i in range(2)]
        pz = [fw.ps("pz%d" % i, stack=st) for i in range(2)]
        psum_s = [fw.ps("pss%d" % i, stack=st) for i in range(2)]
        po = [fw.ps("po%d" % i, stack=st) for i in range(2)]
        pj = fw.ps("pj", stack=st)
        junk = fw.sb("junk", [128, 256], BF16, st)
        tri, negones, masks = consts.tri, consts.negones, consts.masks
        fw.op("dve", lambda e: e.memset(junk[:, :], 1.0), writes=[junk])
        PE = fw.E["pe"].h

        def filler(n):
            for _ in range(n):
                PE.matmul(pj[:, 0:256], tri[:, :], junk[:, :], start=True, stop=True)
        kdeps = [dbuf(cx, (key + "k", t, b)) for t in range(S // TT) for b in range(2)]
        qdeps = [dbuf(cx, (key + "q", t, b)) for t in range(S // TT) for b in range(2)]
        vdeps = [dbuf(cx, (key + "v", t)) for t in range(S // TT)]
        tiles = []
        g = 0
        for h in range(H):
            for T in range(nT):
                Js = list(range(4 * T + 3, -1, -1))
                for idx, J in enumerate(Js):
                    tiles.append((h, T, J, idx, len(Js), g))
                g += 1
        N = len(tiles)
        loaded = set()

        def load_head(h):
            if h in loaded or h >= H:
                return
            loaded.add(h)
            kt, qt, vh = kts[h % 2], qts[h % 2], vhs[h % 2]
            fw.dma("sp", [(kt[:, :], kT[h * 64:(h + 1) * 64, :])], kt, reads=kdeps, writes=[kt])
            fw.dma("sp", [(qt[:, :], qT[h * 64:(h + 1) * 64, :])], qt, reads=qdeps, writes=[qt])
            fw.dma("sp", [(vh[:, :, :], v[:, h * 64:(h + 1) * 64].rearrange("(b p) d -> p b d", p=128))], vh,
                   reads=vdeps, writes=[vh])

        def c0_of(i):
            h, T, J, idx, n, g = tiles[i]
            r = J - 4 * T
            return (r * 128) if r > 0 else 0

        def Amm(i):
            h, T, J, idx, n, g = tiles[i]
            load_head(h)
            kt, qt = kts[h % 2], qts[h % 2]
            z_b = pz[i % 2]
            c0 = c0_of(i)
            fw.op("pe", lambda e: e.matmul(z_b[:, c0:512], kt[:, J * 128:(J + 1) * 128], qt[:, T * 512 + c0:(T + 1) * 512],
                                           start=True, stop=True), reads=[kt, qt], writes=[z_b])

        def Aexp(i):
            h, T, J, idx, n, g = tiles[i]
            z_b, e_b = pz[i % 2], es_[i % 4]
            c0 = c0_of(i)
            fw.op("act", lambda e: e.activation(out=e_b[:, c0:512], in_=z_b[:, c0:512], func=AF.Exp, scale=0.125),
                  reads=[z_b], writes=[e_b])
            r = J - 4 * T
            if r >= 0:
                fw.op("dve", lambda e: e.tensor_tensor(out=e_b[:, c0:c0 + 128], in0=e_b[:, c0:c0 + 128],
                                                       in1=masks[:, 0, 0:128], op=ALU.mult),
                      reads=[e_b, masks], writes=[e_b])

        def Bln(i):
            e_b, sp_b = es_[i % 4], sps[i % 2]
            c0 = c0_of(i)
            fw.op("act", lambda e: e.activation(out=sp_b[:, c0:512], in_=e_b[:, c0:512], func=AF.Ln, bias=1.0, scale=1.0),
                  reads=[e_b], writes=[sp_b])

        def C(i):
            h, T, J, idx, n, g = tiles[i]
            sp_b, s_b = sps[i % 2], psum_s[i % 2]
            l_old, l_new = lsums[i % 2], lsums[(i + 1) % 2]
            c0 = c0_of(i)
            if idx == 0:
                fw.op("pe", lambda e: e.matmul(s_b[:, c0:512], tri[:, :], sp_b[:, c0:512], start=True, stop=True),
                      reads=[tri, sp_b], writes=[s_b])
                fw.op("dve", lambda e: e.tensor_copy(out=l_new[:, c0:512], in_=sp_b[:, c0:512]), reads=[sp_b], writes=[l_new])
            else:
                fw.ops("pe", [
                    lambda e: e.matmul(s_b[:, c0:512], tri[:, :], sp_b[:, c0:512], start=True, stop=False),
                    lambda e: e.matmul(s_b[:, c0:512], negones[:, :], l_old[:, c0:512], start=False, stop=True)],
                    reads=[tri, negones, sp_b, l_old], writes=[s_b])
                if idx != n - 1:
                    fw.op("dve", lambda e: e.tensor_tensor(out=l_new[:, c0:512], in0=l_old[:, c0:512], in1=sp_b[:, c0:512],
                                                           op=ALU.add),
                          reads=[sp_b, l_old], writes=[l_new])
            if c0 > 0:
                fw.op("dve", lambda e: e.memset(l_new[:, c0 - 128:c0], 0.0), writes=[l_new])

        def D(i):
            s_b, e2_b = psum_s[i % 2], e2s[i % 2]
            c0 = c0_of(i)
            fw.op("act", lambda e: e.activation(out=e2_b[:, c0:512], in_=s_b[:, c0:512], func=AF.Exp), reads=[s_b], writes=[e2_b])

        def E(i):
            e2_b, e_b, w_b = e2s[i % 2], es_[i % 4], ws_[i % 3]
            c0 = c0_of(i)
            fw.op("dve", lambda e: e.tensor_tensor(out=w_b[:, c0:512], in0=e_b[:, c0:512], in1=e2_b[:, c0:512], op=ALU.mult),
                  reads=[e_b, e2_b], writes=[w_b])

        def F(i):
            h, T, J, idx, n, g = tiles[i]
            vh, w_b, pout, ob = vhs[h % 2], ws_[i % 3], po[g % 2], osb[g % 2]
            c0 = c0_of(i)
            fw.op("pe", lambda e: e.matmul(pout[0:64, c0:512], vh[:, J, :], w_b[:, c0:512], start=(idx == 0), stop=(idx == n - 1)),
                  reads=[vh, w_b], writes=[pout])
            if idx == n - 1:
                fw.op("dve", lambda e: e.tensor_copy(out=ob[:, :], in_=pout[0:64, :]), reads=[pout], writes=[ob])
                fw.dma("pool", [(oT[h * 64:(h + 1) * 64, T * 512:(T + 1) * 512], ob[:, :])], ob, reads=[ob],
                       writes=[dbuf(cx, (key + "o", h, T))])
                if T == nT - 1:
                    load_head(h + 2)

        load_head(0)
        load_head(1)
        for s_ in range(-2, N + 2):
            if 0 <= s_ + 1 < N:
                Aexp(s_ + 1)
            if 0 <= s_ - 2 < N:
                E(s_ - 2)
            if 0 <= s_ - 1 < N:
                C(s_ - 1)
            if 0 <= s_ - 2 < N:
                F(s_ - 2)
            if 0 <= s_ + 2 < N:
                Amm(s_ + 2)
            if 0 <= s_ < N:
                filler(NFILL)
                Bln(s_)
            if 0 <= s_ - 1 < N:
                D(s_ - 1)
        fw.end_phase()


def sb_out_phase(fw, cx, S, h_in, h_out, hkey_in, hkey_out, g_post, wo_dram, wo_b, oT, key):
    nt = S // TT
    with ExitStack() as st:
        tb = alloc_tile_bufs(fw, st, need_m=True)
        wo = fw.sb("wo", [128, KC, 1024], BF16, st)
        ots = [fw.sb("ot%d" % i, [128, KC, TT], BF16, st) for i in range(2)]
        pss = [fw.ps("ps_o%d" % i, stack=st) for i in range(2)]
        fw.dma("sp", [(wo[:, :, :], wo_dram)], wo, reads=[wo_b], writes=[wo])
        for t in range(nt):
            tsl = slice(t * TT, (t + 1) * TT)
            ht = load_h(fw, cx, tb, t, h_in, hkey_in)
            ot = ots[t % 2]
            odeps = [dbuf(cx, (key + "o", h, t)) for h in range(16)]
            fw.dma("sp", [(ot[:, :, :], oT[:, tsl].rearrange("(c p) t -> p c t", p=128))], ot, reads=odeps, writes=[ot])
            out_proj(fw, cx, tb, wo, ot, pss)
            epilogue(fw, cx, tb, t, ht, h_out, hkey_out, g_post)
        fw.end_phase()


def conv_phase(fw, cx, S, h_in, h_out, hkey_in, hkey_out, g_pre, g_post, wcin, wcin_b, wo_dram, wo_b, cw_off):
    nt = S // TT
    with ExitStack() as st:
        tb = alloc_tile_bufs(fw, st, need_m=True)
        wo = fw.sb("wo", [128, KC, 1024], BF16, st)
        wis = [fw.sb("wci%d" % i, [128, KC, 384], BF16, st) for i in range(2)]
        hcs = [fw.sb("hc%d" % i, [128, TT + 2], F32, st) for i in range(8)]
        cs = [fw.sb("cs%d" % i, [128, TT], F32, st) for i in range(2)]
        t0s = [fw.sb("t0%d" % i, [128, TT], F32, st) for i in range(2)]
        gated = fw.sb("gated", [128, KC, TT], BF16, st)
        pss = [fw.ps("ps_c%d" % i, stack=st) for i in range(6)]
        pso = [fw.ps("ps_o%d" % i, stack=st) for i in range(1)]
        fw.dma("sp", [(wo[:, :, :], wo_dram)], wo, reads=[wo_b], writes=[wo])
        nw = 0
        for t in range(nt):
            ht = prologue(fw, cx, tb, t, h_in, hkey_in, g_pre)
            for i in range(8):
                w = wis[nw % 2]
                pb, pc, pu = pss[(nw % 2) * 3 + 0], pss[(nw % 2) * 3 + 1], pss[(nw % 2) * 3 + 2]
                c_sb, t0 = cs[nw % 2], t0s[nw % 2]
                nw += 1
                hc = hcs[i]
                fw.dma("sp", [(w[:, :, :], wcin[i])], w, reads=[wcin_b], writes=[w])
                for gi, p in ((1, pc), (2, pu), (0, pb)):
                    fw.ops("pe", [
                        (lambda e, c=c, gi=gi, p=p, w=w: e.matmul(p[:, :], w[:, c, gi * 128:(gi + 1) * 128], tb.xn[:, c, :],
                                                                 start=(c == 0), stop=(c == KC - 1)))
                        for c in range(KC)], reads=[w, tb.xn], writes=[p])
                if t == 0:
                    fw.op("dve", lambda e, hc=hc: e.memset(hc[:, 0:2], 0.0), writes=[hc])
                else:
                    fw.op("dve", lambda e, hc=hc: e.tensor_copy(out=hc[:, 0:2], in_=hc[:, TT:TT + 2]), reads=[hc], writes=[hc])
                fw.op("act", lambda e, c_sb=c_sb, pc=pc: e.copy(out=c_sb[:, :], in_=pc[:, :]), reads=[pc], writes=[c_sb])
                fw.op("dve", lambda e, hc=hc, c_sb=c_sb, pu=pu: e.tensor_tensor(out=hc[:, 2:TT + 2], in0=c_sb[:, :], in1=pu[:, :], op=ALU.mult),
                      reads=[c_sb, pu, hc], writes=[hc])
                k0 = cw_off + 0 * 8 + i
                k1 = cw_off + 1 * 8 + i
                k2 = cw_off + 2 * 8 + i
                fw.op("dve", lambda e, hc=hc, t0=t0, k0=k0: e.tensor_scalar(out=t0[:, :], in0=hc[:, 0:TT], scalar1=cx.gains[:, k0:k0 + 1],
                                                                          scalar2=None, op0=ALU.mult),
                      reads=[hc, cx.gains], writes=[t0])
                fw.op("dve", lambda e, hc=hc, t0=t0, k1=k1: e.scalar_tensor_tensor(out=t0[:, :], in0=hc[:, 1:TT + 1], scalar=cx.gains[:, k1:k1 + 1],
                                                                                  in1=t0[:, :], op0=ALU.mult, op1=ALU.add),
                      reads=[hc, t0, cx.gains], writes=[t0])
                fw.op("dve", lambda e, hc=hc, t0=t0, k2=k2: e.scalar_tensor_tensor(out=t0[:, :], in0=hc[:, 2:TT + 2], scalar=cx.gains[:, k2:k2 + 1],
                                                                                  in1=t0[:, :], op0=ALU.mult, op1=ALU.add),
                      reads=[hc, t0, cx.gains], writes=[t0])
                fw.op("dve", lambda e, i=i, t0=t0, pb=pb: e.tensor_tensor(out=gated[:, i, :], in0=t0[:, :], in1=pb[:, :], op=ALU.mult),
                      reads=[t0, pb], writes=[gated])
            out_proj(fw, cx, tb, wo, gated, pso)
            epilogue(fw, cx, tb, t, ht, h_out, hkey_out, g_post)
        fw.end_phase()


NFILL = 6
TG = 256
CH = 64


def gla_phase(fw, cx, S, h_in, h_out, hkey_in, hkey_out, g_pre, g_post, win_d, win_b, wa_d, wa_b, wg_d, wg_b,
              wo_dram, wo_b, hn_off, gc):
    nt = S // TG
    NCH = TG // CH
    with ExitStack() as st:
        tb = alloc_tile_bufs(fw, st, need_m=True, TT=TG, nh=1)
        wo = fw.sb("wo", [128, KC, 1024], BF16, st)
        win = [fw.sb("win%d" % i, [128, KC, 512], BF16, st) for i in range(6)]
        wa = fw.sb("wa", [128, KC, 16], BF16, st)
        wg = fw.sb("wg", [32, 512], BF16, st)
        a_ext = fw.sb("a_ext", [32, TG], BF16, st)
        qT_sb = fw.sb("qT_sb", [128, 4, TG], F32, st)
        kT_sb = fw.sb("kT_sb", [128, 4, TG], F32, st)
        sg = fw.sb("sg", [128, KC, TG], BF16, st)
        o_n = fw.sb("o_n", [128, KC, TG], BF16, st)
        egs = [fw.sb("eg%d" % i, [128, TG], F32, st) for i in range(2)]
        ex = fw.sb("ex", [64, 512], F32, st)
        sp = fw.sb("sp", [64, 512], BF16, st)
        ek = fw.sb("ek", [64, 512], F32, st)
        kt_tok = fw.sb("kt_tok", [64, 512], BF16, st)
        v_tok = fw.sb("v_tok", [64, 1024], BF16, st)
        eq = fw.sb("eq", [128, 4, CH], F32, st)
        ekT = fw.sb("ekT", [128, 4, CH], F32, st)
        qtl = fw.sb("qtl", [128, 4, CH], BF16, st)
        ktl = fw.sb("ktl", [128, 4, CH], BF16, st)
        sT = fw.sb("sT", [64, 4, CH], BF16, st)
        state = fw.sb("state", [128, 4, 256], F32, st)
        state_bf = fw.sb("state_bf", [128, 4, 256], BF16, st)
        rstdh = fw.sb("rstdh", [128, 4, TG], F32, st)
        pA = [fw.ps("pA%d" % i, stack=st) for i in range(2)]
        pct = fw.ps("pct", [128, 256], stack=st)
        pst = fw.ps("pst", [128, 256], stack=st)
        po = fw.ps("po", stack=st)
        pkv = [fw.ps("pkv%d" % i, stack=st) for i in range(2)]
        o_tile = tb.tmp

        fw.dma("sp", [(wo[:, :, :], wo_dram)], wo, reads=[wo_b], writes=[wo])
        for i in range(6):
            fw.dma("sp", [(win[i][:, :, :], win_d[i])], win[i], reads=[win_b], writes=[win[i]])
        fw.dma("sp", [(wa[:, :, :], wa_d)], wa, reads=[wa_b], writes=[wa])
        fw.dma("sp", [(wg[0:17, :], wg_d)], wg, reads=[wg_b], writes=[wg])
        fw.op("dve", lambda e: e.memset(a_ext[:, :], 1.0), writes=[a_ext])
        fw.op("dve", lambda e: e.memset(state[:, :, :], 0.0), writes=[state])
        fw.op("dve", lambda e: e.memset(state_bf[:, :, :], 0.0), writes=[state_bf])
        npa = [0]

        def nextp():
            p = pA[npa[0] % 2]
            npa[0] += 1
            return p

        def proj_fm(w, j0, width, p, N=TG):
            fw.ops("pe", [
                (lambda e, c=c: e.matmul(p[0:width, 0:N], w[:, c, j0:j0 + width], tb.xn[:, c, 0:N],
                                         start=(c == 0), stop=(c == KC - 1)))
                for c in range(KC)], reads=[w, tb.xn], writes=[p])

        for t in range(nt):
            ht = prologue(fw, cx, tb, t, h_in, hkey_in, g_pre)
            for hh in range(4):
                p = nextp()
                proj_fm(win[0], hh * 128, 128, p)
                fw.op("act", lambda e, hh=hh, p=p: e.copy(out=qT_sb[:, hh, :], in_=p[:, 0:TG]), reads=[p], writes=[qT_sb])
            for hh in range(4):
                p = nextp()
                proj_fm(win[1], hh * 128, 128, p)
                fw.op("act", lambda e, hh=hh, p=p: e.copy(out=kT_sb[:, hh, :], in_=p[:, 0:TG]), reads=[p], writes=[kT_sb])
            for c8 in range(8):
                p = nextp()
                eg = egs[c8 % 2]
                proj_fm(win[4 + c8 // 4], (c8 % 4) * 128, 128, p)
                fw.op("act", lambda e, p=p, eg=eg: e.activation(out=eg[:, :], in_=p[:, 0:TG], func=AF.Exp, scale=-1.0),
                      reads=[p], writes=[eg])
                fw.op("dve", lambda e, eg=eg: e.tensor_scalar(out=eg[:, :], in0=eg[:, :], scalar1=1.0, scalar2=None, op0=ALU.add),
                      reads=[eg], writes=[eg])
                fw.op("dve", lambda e, eg=eg: e.reciprocal(out=eg[:, :], in_=eg[:, :]), reads=[eg], writes=[eg])
                fw.op("dve", lambda e, c8=c8, p=p, eg=eg: e.tensor_tensor(out=sg[:, c8, :], in0=eg[:, :], in1=p[:, 0:TG], op=ALU.mult),
                      reads=[eg, p], writes=[sg])
            p = nextp()
            proj_fm(wa, 0, 16, p)
            fw.op("act", lambda e, p=p: e.copy(out=a_ext[0:16, :], in_=p[0:16, 0:TG]), reads=[p], writes=[a_ext])
            for n in range(NCH):
                csl = slice(n * CH, (n + 1) * CH)
                px = nextp()
                fw.op("pe", lambda e, px=px: e.matmul(px[0:64, :], a_ext[0:17, csl], wg[0:17, :], start=True, stop=True),
                      reads=[a_ext, wg], writes=[px])
                fw.op("act", lambda e, px=px: e.activation(out=ex[:, :], in_=px[0:64, :], func=AF.Exp, scale=-1.0),
                      reads=[px], writes=[ex])
                fw.op("act", lambda e: e.activation(out=sp[:, :], in_=ex[:, :], func=AF.Ln, bias=1.0, scale=1.0),
                      reads=[ex], writes=[sp])
                pc = nextp()
                fw.op("pe", lambda e, pc=pc: e.matmul(pc[0:64, :], gc.btn[:, :], sp[:, :], start=True, stop=True),
                      reads=[gc.btn, sp], writes=[pc])
                fw.op("act", lambda e, pc=pc: e.activation(out=ek[:, :], in_=pc[0:64, :], func=AF.Exp, scale=-1.0),
                      reads=[pc], writes=[ek])
                pk = nextp()
                fw.ops("pe", [
                    (lambda e, c=c, pk=pk: e.matmul(pk[0:64, :], tb.xn[:, c, csl], win[1][:, c, :], start=(c == 0), stop=(c == KC - 1)))
                    for c in range(KC)], reads=[win[1], tb.xn], writes=[pk])
                fw.op("dve", lambda e, pk=pk: e.tensor_tensor(out=kt_tok[:, :], in0=ek[:, :], in1=pk[0:64, :], op=ALU.mult),
                      reads=[ek, pk], writes=[kt_tok])
                for vb in range(2):
                    pv = nextp()
                    fw.ops("pe", [
                        (lambda e, c=c, pv=pv, vb=vb: e.matmul(pv[0:64, :], tb.xn[:, c, csl], win[2 + vb][:, c, :],
                                                             start=(c == 0), stop=(c == KC - 1)))
                        for c in range(KC)], reads=[win[2 + vb], tb.xn], writes=[pv])
                    fw.op("act", lambda e, pv=pv, vb=vb: e.copy(out=v_tok[:, vb * 512:(vb + 1) * 512], in_=pv[0:64, :]),
                          reads=[pv], writes=[v_tok])
                fw.ops("pe", [
                    (lambda e, hh=hh: e.matmul(pct[:, hh * CH:(hh + 1) * CH], sp[:, hh * 128:(hh + 1) * 128], gc.btn[:, :],
                                               start=True, stop=True))
                    for hh in range(4)], reads=[sp, gc.btn], writes=[pct])
                fw.op("act", lambda e: e.activation(out=eq[:, :, :], in_=pct[:, :].rearrange("p (h c) -> p h c", h=4), func=AF.Exp),
                      reads=[pct], writes=[eq])
                fw.op("act", lambda e: e.activation(out=ekT[:, :, :], in_=pct[:, :].rearrange("p (h c) -> p h c", h=4), func=AF.Exp, scale=-1.0),
                      reads=[pct], writes=[ekT])
                fw.op("dve", lambda e: e.scalar_tensor_tensor(out=qtl[:, :, :], in0=qT_sb[:, :, csl], scalar=128 ** -0.5, in1=eq[:, :, :],
                                                             op0=ALU.mult, op1=ALU.mult),
                      reads=[qT_sb, eq], writes=[qtl])
                fw.op("dve", lambda e: e.tensor_tensor(out=ktl[:, :, :], in0=kT_sb[:, :, csl], in1=ekT[:, :, :], op=ALU.mult),
                      reads=[kT_sb, ekT], writes=[ktl])
                fw.ops("pe", [
                    (lambda e, hh=hh: e.matmul(pst[0:64, hh * CH:(hh + 1) * CH], ktl[:, hh, :], qtl[:, hh, :], start=True, stop=True))
                    for hh in range(4)], reads=[ktl, qtl], writes=[pst])
                fw.op("dve", lambda e: e.tensor_tensor(out=sT[:, :, :], in0=pst[0:64, :].rearrange("p (h c) -> p h c", h=4),
                                                       in1=gc.mask4[:, :, :], op=ALU.mult),
                      reads=[pst, gc.mask4], writes=[sT])
                mm = []
                for hh in range(4):
                    for ec in range(2):
                        col = (hh * 2 + ec) * CH
                        mm.append(lambda e, hh=hh, ec=ec, col=col: e.matmul(
                            po[:, col:col + CH], state_bf[:, hh, ec * 128:(ec + 1) * 128], qtl[:, hh, :], start=True, stop=False))
                        mm.append(lambda e, hh=hh, ec=ec, col=col: e.matmul(
                            po[:, col:col + CH], v_tok[:, hh * 256 + ec * 128:hh * 256 + (ec + 1) * 128], sT[:, hh, :],
                            start=False, stop=True))
                fw.ops("pe", mm, reads=[state_bf, qtl, v_tok, sT], writes=[po])
                fw.op("act", lambda e: e.copy(out=o_tile[:, :, csl], in_=po[:, :].rearrange("p (k c) -> p k c", k=8)),
                      reads=[po], writes=[o_tile])
                for hh in range(4):
                    pkvb = pkv[hh // 2]
                    fw.op("pe", lambda e, hh=hh, pkvb=pkvb: e.matmul(
                        pkvb[:, (hh % 2) * 256:(hh % 2 + 1) * 256], kt_tok[:, hh * 128:(hh + 1) * 128], v_tok[:, hh * 256:(hh + 1) * 256],
                        start=True, stop=True), reads=[kt_tok, v_tok], writes=[pkvb])
                for hh in range(4):
                    pkvb = pkv[hh // 2]
                    fw.op("dve", lambda e, hh=hh: e.tensor_scalar(out=state[:, hh, :], in0=state[:, hh, :], scalar1=eq[:, hh, CH - 1:CH],
                                                                 scalar2=None, op0=ALU.mult),
                          reads=[state, eq], writes=[state])
                    fw.op("dve", lambda e, hh=hh, pkvb=pkvb: e.scalar_tensor_tensor(
                        out=state[:, hh, :], in0=pkvb[:, (hh % 2) * 256:(hh % 2 + 1) * 256], scalar=eq[:, hh, CH - 1:CH],
                        in1=state[:, hh, :], op0=ALU.mult, op1=ALU.add),
                        reads=[pkvb, eq, state], writes=[state])
                fw.op("act", lambda e: e.copy(out=state_bf[:, :, :], in_=state[:, :, :]), reads=[state], writes=[state_bf])
            fw.op("act", lambda e: e.activation(out=tb.sq[:, :, :], in_=o_tile[:, :, :], func=AF.Square), reads=[o_tile], writes=[tb.sq])
            for hh in range(4):
                pkvb = pkv[hh // 2]
                fw.ops("pe", [
                    (lambda e, hh=hh, ec=ec, pkvb=pkvb: e.matmul(pkvb[:, (hh % 2) * 256:(hh % 2 + 1) * 256], cx.ones256[:, :],
                                                                 tb.sq[:, hh * 2 + ec, :], start=(ec == 0), stop=(ec == 1)))
                    for ec in range(2)], reads=[cx.ones256, tb.sq], writes=[pkvb])
            for half in range(2):
                fw.op("act", lambda e, half=half: e.activation(
                    out=rstdh[:, 2 * half:2 * half + 2, :], in_=pkv[half][:, :].rearrange("p (h c) -> p h c", h=2),
                    func=AF.Ln, bias=EPS, scale=1.0), reads=[pkv[half]], writes=[rstdh])
            fw.op("act", lambda e: e.activation(out=rstdh[:, :, :], in_=rstdh[:, :, :], func=AF.Exp, scale=-0.5),
                  reads=[rstdh], writes=[rstdh])
            for k8 in range(8):
                fw.op("dve", lambda e, k8=k8: e.scalar_tensor_tensor(
                    out=o_tile[:, k8, :], in0=o_tile[:, k8, :], scalar=cx.gains[:, hn_off + k8:hn_off + k8 + 1],
                    in1=rstdh[:, k8 // 2, :], op0=ALU.mult, op1=ALU.mult),
                    reads=[o_tile, rstdh, cx.gains], writes=[o_tile])
            fw.op("dve", lambda e: e.tensor_tensor(out=o_n[:, :, :], in0=o_tile[:, :, :], in1=sg[:, :, :], op=ALU.mult),
                  reads=[o_tile, sg], writes=[o_n])
            out_proj(fw, cx, tb, wo, o_n, pA)
            epilogue(fw, cx, tb, t, ht, h_out, hkey_out, g_post)
        fw.end_phase()


S_FULL = 4096
NSMALL = 160
G_OFF, CW_OFF, HN_OFF = 0, 128, 152


def build_program(S=S_FULL):
    nc = bass.Bass("TRN2", target_bir_lowering=False)

    def din(name, shape, dt=F32):
        return nc.dram_tensor(name, list(shape), dt, kind="ExternalInput").ap()

    def dscr(name, shape, dt=BF16):
        return nc.dram_tensor(name, list(shape), dt).ap()

    x = din("x", [128, 8, S])
    y = nc.dram_tensor("y", [128, 8, S], F32, kind="ExternalOutput").ap()
    small = din("small", [128, NSMALL])
    tri_d = din("tri", [128, 128])
    neg_d = din("negones", [128, 128])
    masks_d = din("masks", [128, 4, 512])
    btn_d = din("btn", [64, 64])
    mask4_d = din("mask4", [64, 4, 64])
    w32 = {}
    wbf = {}

    def wpair(name, shape):
        w32[name] = din(name + "_f32", shape)
        wbf[name] = dscr(name + "_bf", shape)

    for j in range(2):
        wpair("sbqkv%d" % j, [6, 128, 8, 512])
        wpair("sbo%d" % j, [128, 8, 1024])
    wpair("cin", [8, 128, 8, 384])
    wpair("cout", [128, 8, 1024])
    wpair("gin", [6, 128, 8, 512])
    wpair("ga", [128, 8, 16])
    wpair("gg", [17, 512])
    wpair("go", [128, 8, 1024])
    for l in range(4):
        wpair("up%d" % l, [8, 128, 8, 512])
        wpair("dn%d" % l, [8, 128, 32, 128])
    qT = dscr("qT", [1024, S])
    kT = dscr("kT", [1024, S])
    v = dscr("v", [S, 1024])
    oT = dscr("oT", [1024, S])

    with ExitStack() as es:
        fw = FW(nc, es)
        cx = Ctx()
        setup_common(fw, cx, small, NSMALL)
        sbc = Ctx()
        sbc.tri = fw.sb("tri", [128, 128], BF16)
        sbc.negones = fw.sb("negones", [128, 128], BF16)
        sbc.masks = fw.sb("masks", [128, 4, 512], F32)
        gc = Ctx()
        gc.btn = fw.sb("btn", [64, 64], BF16)
        gc.mask4 = fw.sb("mask4", [64, 4, 64], F32)
        fw.dma("pool", [(sbc.tri[:, :], tri_d)], sbc.tri, writes=[sbc.tri])
        fw.dma("pool", [(sbc.negones[:, :], neg_d)], sbc.negones, writes=[sbc.negones])
        fw.dma("pool", [(gc.btn[:, :], btn_d)], gc.btn, writes=[gc.btn])
        fw.dma("sp", [(sbc.masks[:, :, :], masks_d)], sbc.masks, writes=[sbc.masks])
        fw.dma("sp", [(gc.mask4[:, :, :], mask4_d)], gc.mask4, writes=[gc.mask4])

        wb = {}

        def cast(name):
            shape = wbf[name].shape
            n = shape[0] if len(shape) == 4 else 1
            wb[name] = cast_weight(fw, cx, name, wbf[name], w32[name], n)

        order = ["sbqkv0", "sbo0", "up0", "dn0", "cin", "cout", "up1", "dn1",
                 "gin", "ga", "gg", "go", "up2", "dn2", "sbqkv1", "sbo1", "up3", "dn3"]
        for name in order:
            cast(name)

        def gi(l, k):
            return l * 4 + k

        def ffn(l, hin, hkin):
            ffn_phase(fw, cx, S, hin, y, hkin, "y", gi(l, 2), gi(l, 3), wbf["up%d" % l], wbf["dn%d" % l],
                      wb["up%d" % l], wb["dn%d" % l])

        def sb(l, j, hin, hkin):
            key = "sb%d" % j
            sb_proj_phase(fw, cx, S, hin, hkin, gi(l, 0), wbf["sbqkv%d" % j], wb["sbqkv%d" % j], qT, kT, v, key)
            sb_attn_phase(fw, cx, S, qT, kT, v, oT, key, sbc)
            sb_out_phase(fw, cx, S, hin, y, hkin, "y", gi(l, 1), wbf["sbo%d" % j], wb["sbo%d" % j], oT, key)

        sb(0, 0, x, "x")
        ffn(0, y, "y")
        conv_phase(fw, cx, S, y, y, "y", "y", gi(1, 0), gi(1, 1), wbf["cin"], wb["cin"], wbf["cout"], wb["cout"], CW_OFF)
        ffn(1, y, "y")
        gla_phase(fw, cx, S, y, y, "yg", "yg", gi(2, 0), gi(2, 1), wbf["gin"], wb["gin"], wbf["ga"], wb["ga"],
                  wbf["gg"], wb["gg"], wbf["go"], wb["go"], HN_OFF, gc)
        ffn(2, y, "y")
        sb(3, 1, y, "y")
        ffn(3, y, "y")
        fw.finish([cx.dram[("y", t)] for t in range(S // TT)])
        build_program.stats = fw.stats()
    return nc


def _fm(a):
    return np.ascontiguousarray(a.T.reshape(8, 128, -1).transpose(1, 0, 2))


def _blk512(W, nb):
    return np.ascontiguousarray(W.reshape(8, 128, nb, 512).transpose(2, 1, 0, 3))


def _rows(W):
    return np.ascontiguousarray(W.reshape(8, 128, W.shape[1]).transpose(1, 0, 2))


def prepare_inputs(x, norm_gains, sb_w_qkv, sb_w_o, conv_w_in, conv_w, conv_w_out,
                   gla_w_in, gla_w_gate_up, gla_b_gate, gla_head_norm, gla_w_o, ffn_w_up, ffn_w_down):
    f = np.float32
    shared = {}
    small = np.zeros((128, NSMALL), f)
    ng = np.asarray(norm_gains, f)
    small[:, 0:128] = ng.reshape(16, 8, 128).transpose(2, 0, 1).reshape(128, 128)
    small[:, 128:152] = np.asarray(conv_w, f)[0].reshape(3, 8, 128).transpose(2, 0, 1).reshape(128, 24)
    small[:, 152:160] = np.asarray(gla_head_norm, f)[0].reshape(8, 128).T
    shared["small"] = small
    k = np.arange(128)
    shared["tri"] = -(k[:, None] >= k[None, :]).astype(f)
    shared["negones"] = -np.ones((128, 128), f)
    col = np.arange(512)
    shared["masks"] = np.ascontiguousarray(
        np.stack([(col[None, :] > (r * 128 + k[:, None])).astype(f) for r in range(4)], 1))
    s = np.arange(64)
    shared["btn"] = np.ascontiguousarray(-(s[:, None] <= s[None, :]).astype(f) / 16.0)
    shared["mask4"] = np.ascontiguousarray(np.repeat((s[:, None] <= s[None, :]).astype(f)[:, None, :], 4, 1))
    for j in range(2):
        shared["sbqkv%d_f32" % j] = _blk512(np.asarray(sb_w_qkv[j], f), 6)
        shared["sbo%d_f32" % j] = _rows(np.asarray(sb_w_o[j], f))
    Win = np.asarray(conv_w_in[0], f)
    shared["cin_f32"] = np.ascontiguousarray(
        Win.reshape(8, 128, 3, 8, 128).transpose(3, 1, 0, 2, 4).reshape(8, 128, 8, 384))
    shared["cout_f32"] = _rows(np.asarray(conv_w_out[0], f))
    Wg = np.asarray(gla_w_in[0], f)
    shared["gin_f32"] = _blk512(np.ascontiguousarray(Wg[:, :3072]), 6)
    shared["ga_f32"] = _rows(np.ascontiguousarray(Wg[:, 3072:]))
    shared["gg_f32"] = np.ascontiguousarray(
        np.concatenate([np.asarray(gla_w_gate_up[0], f), np.asarray(gla_b_gate[0], f)[None]], 0))
    shared["go_f32"] = _rows(np.asarray(gla_w_o[0], f))
    for l in range(4):
        shared["up%d_f32" % l] = _blk512(np.asarray(ffn_w_up[l], f), 8)
        Wd = np.asarray(ffn_w_down[l], f)
        shared["dn%d_f32" % l] = np.ascontiguousarray(Wd.reshape(32, 128, 8, 128).transpose(2, 1, 0, 3))
    xs = np.asarray(x, f)
    in_maps = []
    for b in range(8):
        m = dict(shared)
        m["x"] = _fm(xs[b])
        in_maps.append(m)
    return in_maps


_NC_CACHE = {}


def kernel(**inputs):
    in_maps = prepare_inputs(**inputs)
    if "nc" not in _NC_CACHE:
        _NC_CACHE["nc"] = build_program(S_FULL)
    nc = _NC_CACHE["nc"]
    res = run_bass_kernel_spmd(nc, in_maps, core_ids=list(range(8)))
    out = np.empty((8, S_FULL, 1024), np.float32)
    for b in range(8):
        yb = np.asarray(res.results[b]["y"])
        out[b] = yb.transpose(1, 0, 2).reshape(1024, S_FULL).T
    return out
```

```python
import numpy as np
from contextlib import ExitStack
import concourse.bass as bass
import concourse.mybir as mybir
from concourse.bass_utils import run_bass_kernel_spmd

F32 = mybir.dt.float32
BF16 = mybir.dt.bfloat16
AF = mybir.ActivationFunctionType
ALU = mybir.AluOpType


class Buf:
    __slots__ = ("name", "t", "w", "r", "sem", "tot")

    def __init__(self, name, t=None):
        self.name = name
        self.t = t
        self.w = None
        self.r = {}
        self.sem = None
        self.tot = 0

    def __getitem__(self, idx):
        return self.t[idx]


class Eng:
    def __init__(self, name, h, sem):
        self.name = name
        self.h = h
        self.sem = sem
        self.count = 0
        self.waited = {}
        self.nwaits = 0
        self.nins = 0


class FW:
    def __init__(self, nc, es):
        self.nc = nc
        self.es = es
        self.E = {}
        for name, h in (("pe", nc.tensor), ("act", nc.scalar), ("dve", nc.vector),
                        ("pool", nc.gpsimd), ("sp", nc.sync)):
            sem = es.enter_context(nc.semaphore("sem_" + name))
            self.E[name] = Eng(name, h, sem)
        self.nsem = 5
        self.uid = 0
        self.dma_slots = []
        self.free_sems = []
        self.phase_bufs = []

    def sb(self, name, shape, dt, stack=None):
        self.uid += 1
        t = (stack or self.es).enter_context(self.nc.sbuf_tensor(f"{name}_{self.uid}", list(shape), dt))
        b = Buf(name, t)
        if stack is not None:
            self.phase_bufs.append(b)
        return b

    def ps(self, name, shape=(128, 512), dt=F32, stack=None):
        self.uid += 1
        t = (stack or self.es).enter_context(self.nc.psum_tensor(f"{name}_{self.uid}", list(shape), dt))
        return Buf(name, t)

    def dsem(self, b):
        if b.sem is None:
            if b.t is not None and self.free_sems:
                b.sem, b.tot = self.free_sems.pop()
            else:
                b.sem = self.es.enter_context(self.nc.semaphore(f"dq_{b.name}_{self.nsem}"))
                self.nsem += 1
            if b.t is not None:
                self.dma_slots.append(b)
        return b.sem

    def end_phase(self):
        self.barrier()
        for b in self.phase_bufs:
            if b.sem is not None:
                self.free_sems.append((b.sem, b.tot))
                self.dma_slots.remove(b)
                b.sem = None
        self.phase_bufs = []

    def _waits(self, E, reads, writes):
        need = {}

        def add(ev, same_ok):
            if ev is None:
                return
            sem, val = ev
            if sem is E.sem and not same_ok:
                return
            k = id(sem)
            if k not in need or need[k][1] < val:
                need[k] = (sem, val)

        for b in reads:
            add(b.w, True)
        for b in writes:
            add(b.w, False)
            for sem_id, (sem, val) in b.r.items():
                add((sem, val), False)
        for k, (sem, val) in need.items():
            if E.waited.get(k, 0) >= val:
                continue
            E.h.wait_ge(sem, val)
            E.waited[k] = val
            E.nwaits += 1

    def _commit(self, ev, reads, writes):
        sem, val = ev
        k = id(sem)
        for b in reads:
            if k not in b.r or b.r[k][1] < val:
                b.r[k] = (sem, val)
        for b in writes:
            b.w = ev
            b.r = {}

    def op(self, eng, fn, reads=(), writes=()):
        E = self.E[eng]
        self._waits(E, reads, writes)
        ins = fn(E.h)
        E.count += 1
        E.nins += 1
        ins.then_inc(E.sem, 1)
        self._commit((E.sem, E.count), reads, writes)

    def ops(self, eng, fns, reads=(), writes=()):
        E = self.E[eng]
        self._waits(E, reads, writes)
        ins = None
        for fn in fns:
            ins = fn(E.h)
            E.nins += 1
        E.count += 1
        ins.then_inc(E.sem, 1)
        self._commit((E.sem, E.count), reads, writes)

    def dma(self, q, pairs, slot, reads=(), writes=()):
        E = self.E[q]
        sem = self.dsem(slot)
        self._waits(E, reads, writes)
        k = id(sem)
        if slot.tot > 0 and E.waited.get(k, 0) < slot.tot:
            E.h.wait_ge(sem, slot.tot)
            E.waited[k] = slot.tot
            E.nwaits += 1
        for (o, i) in pairs:
            E.h.dma_start(out=o, in_=i).then_inc(sem, 16)
            slot.tot += 16
            E.nins += 1
        self._commit((sem, slot.tot), reads, writes)

    def finish(self, bufs):
        E = self.E["sp"]
        self._waits(E, bufs, ())

    def barrier(self):
        SP = self.E["sp"]
        for b in self.dma_slots:
            k = id(b.sem)
            if b.tot > 0 and SP.waited.get(k, 0) < b.tot:
                SP.h.wait_ge(b.sem, b.tot)
                SP.waited[k] = b.tot
        SP.count += 1
        SP.h.nop().then_inc(SP.sem, 1)
        evs = []
        for E in self.E.values():
            if E.count > 0:
                evs.append((E.sem, E.count))
        for E in self.E.values():
            for sem, val in evs:
                if sem is E.sem:
                    continue
                k = id(sem)
                if E.waited.get(k, 0) < val:
                    E.h.wait_ge(sem, val)
                    E.waited[k] = val
                    E.nwaits += 1

    def stats(self):
        return {n: (e.nins, e.nwaits) for n, e in self.E.items()}


D = 1024
KC = 8
TT = 512
EPS = 1e-6


class Ctx:
    pass


def setup_common(fw, cx, gains_dram, ngains):
    nc = fw.nc
    cx.onesD = fw.sb("onesD", [128, 128], BF16)
    cx.ones256 = fw.sb("ones256", [128, 128], BF16)
    cx.gains = fw.sb("gains", [128, ngains], F32)
    fw.op("pool", lambda e: e.memset(cx.onesD[:, :], 1.0 / 1024), writes=[cx.onesD])
    fw.op("pool", lambda e: e.memset(cx.ones256[:, :], 1.0 / 256), writes=[cx.ones256])
    fw.dma("sp", [(cx.gains[:, :], gains_dram)], cx.gains, writes=[cx.gains])
    cx.dram = {}


def dbuf(cx, key):
    if key not in cx.dram:
        cx.dram[key] = Buf("dram_%s" % (key,))
    return cx.dram[key]


def cast_weight(fw, cx, key, dst_ap, src_ap, nsplit=1):
    b = dbuf(cx, key)
    if nsplit == 1:
        pairs = [(dst_ap, src_ap)]
    else:
        pairs = [(dst_ap[i], src_ap[i]) for i in range(nsplit)]
    fw.dma("pool", pairs, b, writes=[b])
    return b


def rms_rstd(fw, cx, src, nch, T, ones, sq, ps, rstd):
    fw.op("act", lambda e: e.activation(out=sq[:, 0:nch, 0:T], in_=src[:, 0:nch, 0:T], func=AF.Square),
          reads=[src], writes=[sq])
    fw.ops("pe", [
        (lambda e, c=c: e.matmul(ps[:, 0:T], ones[:, :], sq[:, c, 0:T], start=(c == 0), stop=(c == nch - 1)))
        for c in range(nch)], reads=[sq, ones], writes=[ps])
    fw.op("act", lambda e: e.activation(out=rstd[:, 0:T], in_=ps[:, 0:T], func=AF.Ln, bias=EPS, scale=1.0),
          reads=[ps], writes=[rstd])
    fw.op("act", lambda e: e.activation(out=rstd[:, 0:T], in_=rstd[:, 0:T], func=AF.Exp, scale=-0.5),
          reads=[rstd], writes=[rstd])


def ffn_phase(fw, cx, S, h_in, h_out, hkey_in, hkey_out, g_pre, g_post, wup, wdn, wup_b, wdn_b):
    nt = S // TT
    with ExitStack() as st:
        hts = [fw.sb("hT%d" % i, [128, KC, TT], F32, st) for i in range(2)]
        sq = fw.sb("sq", [128, KC, TT], BF16, st)
        xn = fw.sb("xn", [128, KC, TT], BF16, st)
        act = fw.sb("act", [128, 32, TT], BF16, st)
        mT = fw.sb("mT", [128, KC, TT], F32, st)
        tmp = fw.sb("tmp", [128, KC, TT], F32, st)
        rstd = fw.sb("rstd", [128, TT], F32, st)
        rstd2 = fw.sb("rstd2", [128, TT], F32, st)
        relus = [fw.sb("relu%d" % i, [128, TT], F32, st) for i in range(2)]
        wus = [fw.sb("wu%d" % i, [128, KC, 512], BF16, st) for i in range(2)]
        wds = [fw.sb("wd%d" % i, [128, 32, 128], BF16, st) for i in range(2)]
        ps_stat = fw.ps("ps_stat", stack=st)
        ps_up = [fw.ps("ps_up%d" % i, stack=st) for i in range(4)]
        ps_dn = [fw.ps("ps_dn%d" % i, stack=st) for i in range(2)]
        nu = 0
        nd = 0
        for t in range(nt):
            ht = hts[t % 2]
            tsl = slice(t * TT, (t + 1) * TT)
            hb_in = dbuf(cx, (hkey_in, t))
            hb_out = dbuf(cx, (hkey_out, t))
            fw.dma("sp", [(ht[:, :, :], h_in[:, :, tsl])], ht, reads=[hb_in], writes=[ht])
            rms_rstd(fw, cx, ht, KC, TT, cx.onesD, sq, ps_stat, rstd)
            for c in range(KC):
                fw.op("dve", lambda e, c=c: e.scalar_tensor_tensor(
                    out=xn[:, c, :], in0=ht[:, c, :], scalar=cx.gains[:, g_pre * KC + c:g_pre * KC + c + 1],
                    in1=rstd[:, :], op0=ALU.mult, op1=ALU.mult),
                    reads=[ht, rstd, cx.gains], writes=[xn])
            for blk in range(8):
                wu = wus[nu % 2]
                nu += 1
                fw.dma("sp", [(wu[:, :, :], wup[blk])], wu, reads=[wup_b], writes=[wu])
                for j in range(4):
                    f = blk * 4 + j
                    pu = ps_up[f % 4]
                    fw.ops("pe", [
                        (lambda e, c=c, j=j, pu=pu, wu=wu: e.matmul(pu[:, :], wu[:, c, j * 128:(j + 1) * 128], xn[:, c, :],
                                                                   start=(c == 0), stop=(c == KC - 1)))
                        for c in range(KC)], reads=[wu, xn], writes=[pu])
                    rl = relus[f % 2]
                    fw.op("act", lambda e, pu=pu, rl=rl: e.activation(out=rl[:, :], in_=pu[:, :], func=AF.Relu),
                          reads=[pu], writes=[rl])
                    fw.op("dve", lambda e, f=f, rl=rl: e.tensor_tensor(out=act[:, f, :], in0=rl[:, :], in1=rl[:, :], op=ALU.mult),
                          reads=[rl], writes=[act])
            for d in range(KC):
                wd = wds[nd % 2]
                nd += 1
                fw.dma("sp", [(wd[:, :, :], wdn[d])], wd, reads=[wdn_b], writes=[wd])
                pd = ps_dn[d % 2]
                fw.ops("pe", [
                    (lambda e, f=f, pd=pd, wd=wd: e.matmul(pd[:, :], wd[:, f, :], act[:, f, :],
                                                          start=(f == 0), stop=(f == 31)))
                    for f in range(32)], reads=[wd, act], writes=[pd])
                fw.op("act", lambda e, d=d, pd=pd: e.copy(out=mT[:, d, :], in_=pd[:, :]), reads=[pd], writes=[mT])
            rms_rstd(fw, cx, mT, KC, TT, cx.onesD, sq, ps_stat, rstd2)
            for c in range(KC):
                fw.op("dve", lambda e, c=c: e.scalar_tensor_tensor(
                    out=tmp[:, c, :], in0=mT[:, c, :], scalar=cx.gains[:, g_post * KC + c:g_post * KC + c + 1],
                    in1=rstd2[:, :], op0=ALU.mult, op1=ALU.mult),
                    reads=[mT, rstd2, cx.gains], writes=[tmp])
            fw.op("dve", lambda e: e.tensor_tensor(out=tmp[:, :, :], in0=tmp[:, :, :], in1=ht[:, :, :], op=ALU.add),
                  reads=[tmp, ht], writes=[tmp])
            fw.dma("pool", [(h_out[:, :, tsl], tmp[:, :, :])], tmp, reads=[tmp], writes=[hb_out])
        fw.end_phase()


class TileBufs:
    pass


def alloc_tile_bufs(fw, st, need_m=True, TT=TT, nh=2):
    tb = TileBufs()
    tb.TT = TT
    tb.hts = [fw.sb("hT%d" % i, [128, KC, TT], F32, st) for i in range(nh)]
    tb.sq = fw.sb("sq", [128, KC, TT], BF16, st)
    tb.xn = fw.sb("xn", [128, KC, TT], BF16, st)
    tb.rstd = fw.sb("rstd", [128, TT], F32, st)
    tb.ps_stat = fw.ps("ps_stat", stack=st)
    if need_m:
        tb.mT = fw.sb("mT", [128, KC, TT], F32, st)
        tb.tmp = fw.sb("tmp", [128, KC, TT], F32, st)
        tb.rstd2 = fw.sb("rstd2", [128, TT], F32, st)
    return tb


def prologue(fw, cx, tb, t, h_in, hkey_in, g_pre):
    TT = tb.TT
    ht = tb.hts[t % len(tb.hts)]
    tsl = slice(t * TT, (t + 1) * TT)
    fw.dma("sp", [(ht[:, :, :], h_in[:, :, tsl])], ht, reads=[dbuf(cx, (hkey_in, t))], writes=[ht])
    rms_rstd(fw, cx, ht, KC, TT, cx.onesD, tb.sq, tb.ps_stat, tb.rstd)
    for c in range(KC):
        fw.op("dve", lambda e, c=c: e.scalar_tensor_tensor(
            out=tb.xn[:, c, :], in0=ht[:, c, :], scalar=cx.gains[:, g_pre * KC + c:g_pre * KC + c + 1],
            in1=tb.rstd[:, :], op0=ALU.mult, op1=ALU.mult),
            reads=[ht, tb.rstd, cx.gains], writes=[tb.xn])
    return ht


def load_h(fw, cx, tb, t, h_in, hkey_in):
    TT = tb.TT
    ht = tb.hts[t % len(tb.hts)]
    tsl = slice(t * TT, (t + 1) * TT)
    fw.dma("sp", [(ht[:, :, :], h_in[:, :, tsl])], ht, reads=[dbuf(cx, (hkey_in, t))], writes=[ht])
    return ht


def epilogue(fw, cx, tb, t, ht, h_out, hkey_out, g_post):
    TT = tb.TT
    tsl = slice(t * TT, (t + 1) * TT)
    rms_rstd(fw, cx, tb.mT, KC, TT, cx.onesD, tb.sq, tb.ps_stat, tb.rstd2)
    for c in range(KC):
        fw.op("dve", lambda e, c=c: e.scalar_tensor_tensor(
            out=tb.tmp[:, c, :], in0=tb.mT[:, c, :], scalar=cx.gains[:, g_post * KC + c:g_post * KC + c + 1],
            in1=tb.rstd2[:, :], op0=ALU.mult, op1=ALU.mult),
            reads=[tb.mT, tb.rstd2, cx.gains], writes=[tb.tmp])
    fw.op("dve", lambda e: e.tensor_tensor(out=tb.tmp[:, :, :], in0=tb.tmp[:, :, :], in1=ht[:, :, :], op=ALU.add),
          reads=[tb.tmp, ht], writes=[tb.tmp])
    fw.dma("pool", [(h_out[:, :, tsl], tb.tmp[:, :, :])], tb.tmp, reads=[tb.tmp], writes=[dbuf(cx, (hkey_out, t))])


def out_proj(fw, cx, tb, wo, src, ps_list):
    for d in range(KC):
        pd = ps_list[d % len(ps_list)]
        fw.ops("pe", [
            (lambda e, c=c, d=d, pd=pd: e.matmul(pd[:, 0:tb.TT], wo[:, c, d * 128:(d + 1) * 128], src[:, c, :],
                                                 start=(c == 0), stop=(c == KC - 1)))
            for c in range(KC)], reads=[wo, src], writes=[pd])
        fw.op("act", lambda e, d=d, pd=pd: e.copy(out=tb.mT[:, d, :], in_=pd[:, 0:tb.TT]), reads=[pd], writes=[tb.mT])


def sb_proj_phase(fw, cx, S, h_in, hkey_in, g_pre, wqkv, wqkv_b, qT, kT, v, key):
    nt = S // TT
    with ExitStack() as st:
        tb = alloc_tile_bufs(fw, st, need_m=False)
        ws = [fw.sb("wqkv%d" % i, [128, KC, 512], BF16, st) for i in range(2)]
        qk_sb = [fw.sb("qk_sb%d" % i, [128, 4, TT], BF16, st) for i in range(2)]
        v_sb = fw.sb("v_sb", [128, 4, 1024], BF16, st)
        pss = [fw.ps("ps_p%d" % i, stack=st) for i in range(4)]
        nw = 0
        npz = 0
        for t in range(nt):
            tsl = slice(t * TT, (t + 1) * TT)
            prologue(fw, cx, tb, t, h_in, hkey_in, g_pre)
            for blk in range(6):
                w = ws[nw % 2]
                nw += 1
                fw.dma("sp", [(w[:, :, :], wqkv[blk])], w, reads=[wqkv_b], writes=[w])
                if blk < 4:
                    dst = qT if blk < 2 else kT
                    stage = qk_sb[blk % 2]
                    for j in range(4):
                        p = pss[npz % 4]
                        npz += 1
                        fw.ops("pe", [
                            (lambda e, c=c, j=j, p=p, w=w: e.matmul(p[:, :], w[:, c, j * 128:(j + 1) * 128], tb.xn[:, c, :],
                                                                   start=(c == 0), stop=(c == KC - 1)))
                            for c in range(KC)], reads=[w, tb.xn], writes=[p])
                        fw.op("act", lambda e, j=j, p=p, stage=stage: e.copy(out=stage[:, j, :], in_=p[:, :]),
                              reads=[p], writes=[stage])
                    r0 = (blk % 2) * 512
                    fw.dma("pool", [(dst[r0:r0 + 512, tsl].rearrange("(j p) t -> p j t", p=128), stage[:, :, :])],
                           stage, reads=[stage], writes=[dbuf(cx, (key + ("q" if blk < 2 else "k"), t, blk % 2))])
                else:
                    vb = blk - 4
                    for tk in range(4):
                        p = pss[npz % 4]
                        npz += 1
                        fw.ops("pe", [
                            (lambda e, c=c, tk=tk, p=p, w=w: e.matmul(p[:, :], tb.xn[:, c, tk * 128:(tk + 1) * 128], w[:, c, :],
                                                                     start=(c == 0), stop=(c == KC - 1)))
                            for c in range(KC)], reads=[w, tb.xn], writes=[p])
                        fw.op("act", lambda e, tk=tk, p=p, vb=vb: e.copy(out=v_sb[:, tk, vb * 512:(vb + 1) * 512], in_=p[:, :]),
                              reads=[p], writes=[v_sb])
            fw.dma("pool", [(v[tsl, :].rearrange("(b p) f -> p b f", p=128), v_sb[:, :, :])], v_sb,
                   reads=[v_sb], writes=[dbuf(cx, (key + "v", t))])
        fw.end_phase()


def sb_attn_phase(fw, cx, S, qT, kT, v, oT, key, consts):
    nT = S // 512
    nB = S // 128
    H = 16
    with ExitStack() as st:
        kts = [fw.sb("kt%d" % i, [64, S], BF16, st) for i in range(2)]
        qts = [fw.sb("qt%d" % i, [64, S], BF16, st) for i in range(2)]
        vhs = [fw.sb("vh%d" % i, [128, nB, 64], BF16, st) for i in range(2)]
        es_ = [fw.sb("e%d" % i, [128, 512], F32, st) for i in range(4)]
        sps = [fw.sb("sp%d" % i, [128, 512], BF16, st) for i in range(2)]
        lsums = [fw.sb("lsum%d" % i, [128, 512], BF16, st) for i in range(2)]
        e2s = [fw.sb("e2%d" % i, [128, 512], F32, st) for i in range(2)]
        ws_ = [fw.sb("w%d" % i, [128, 512], BF16, st) for i in range(3)]
        osb = [fw.sb("osb%d" % i, [64, 512], BF16, st) for i in range(2)]
        pz = [fw.ps("pz%d" % i, stack=st) for i in range(2)]
        psum_s = [fw.ps("pss%d" % i, stack=st) for i in range(2)]
        po = [fw.ps("po%d" % i, stack=st) for i in range(2)]
        pj = fw.ps("pj", stack=st)
        junk = fw.sb("junk", [128, 256], BF16, st)
        tri, negones, masks = consts.tri, consts.negones, consts.masks
        fw.op("dve", lambda e: e.memset(junk[:, :], 1.0), writes=[junk])
        PE = fw.E["pe"].h

        def filler(n):
            for _ in range(n):
                PE.matmul(pj[:, 0:256], tri[:, :], junk[:, :], start=True, stop=True)
        kdeps = [dbuf(cx, (key + "k", t, b)) for t in range(S // TT) for b in range(2)]
        qdeps = [dbuf(cx, (key + "q", t, b)) for t in range(S // TT) for b in range(2)]
        vdeps = [dbuf(cx, (key + "v", t)) for t in range(S // TT)]
        tiles = []
        g = 0
        for h in range(H):
            for T in range(nT):
                Js = list(range(4 * T + 3, -1, -1))
                for idx, J in enumerate(Js):
                    tiles.append((h, T, J, idx, len(Js), g))
                g += 1
        N = len(tiles)
        loaded = set()

        def load_head(h):
            if h in loaded or h >= H:
                return
            loaded.add(h)
            kt, qt, vh = kts[h % 2], qts[h % 2], vhs[h % 2]
            fw.dma("sp", [(kt[:, :], kT[h * 64:(h + 1) * 64, :])], kt, reads=kdeps, writes=[kt])
            fw.dma("sp", [(qt[:, :], qT[h * 64:(h + 1) * 64, :])], qt, reads=qdeps, writes=[qt])
            fw.dma("sp", [(vh[:, :, :], v[:, h * 64:(h + 1) * 64].rearrange("(b p) d -> p b d", p=128))], vh,
                   reads=vdeps, writes=[vh])

        def c0_of(i):
            h, T, J, idx, n, g = tiles[i]
            r = J - 4 * T
            return (r * 128) if r > 0 else 0

        def Amm(i):
            h, T, J, idx, n, g = tiles[i]
            load_head(h)
            kt, qt = kts[h % 2], qts[h % 2]
            z_b = pz[i % 2]
            c0 = c0_of(i)
            fw.op("pe", lambda e: e.matmul(z_b[:, c0:512], kt[:, J * 128:(J + 1) * 128], qt[:, T * 512 + c0:(T + 1) * 512],
                                           start=True, stop=True), reads=[kt, qt], writes=[z_b])

        def Aexp(i):
            h, T, J, idx, n, g = tiles[i]
            z_b, e_b = pz[i % 2], es_[i % 4]
            c0 = c0_of(i)
            fw.op("act", lambda e: e.activation(out=e_b[:, c0:512], in_=z_b[:, c0:512], func=AF.Exp, scale=0.125),
                  reads=[z_b], writes=[e_b])
            r = J - 4 * T
            if r >= 0:
                fw.op("dve", lambda e: e.tensor_tensor(out=e_b[:, c0:c0 + 128], in0=e_b[:, c0:c0 + 128],
                                                       in1=masks[:, 0, 0:128], op=ALU.mult),
                      reads=[e_b, masks], writes=[e_b])

        def Bln(i):
            e_b, sp_b = es_[i % 4], sps[i % 2]
            c0 = c0_of(i)
            fw.op("act", lambda e: e.activation(out=sp_b[:, c0:512], in_=e_b[:, c0:512], func=AF.Ln, bias=1.0, scale=1.0),
                  reads=[e_b], writes=[sp_b])

        def C(i):
            h, T, J, idx, n, g = tiles[i]
            sp_b, s_b = sps[i % 2], psum_s[i % 2]
            l_old, l_new = lsums[i % 2], lsums[(i + 1) % 2]
            c0 = c0_of(i)
            if idx == 0:
                fw.op("pe", lambda e: e.matmul(s_b[:, c0:512], tri[:, :], sp_b[:, c0:512], start=True, stop=True),
                      reads=[tri, sp_b], writes=[s_b])
                fw.op("dve", lambda e: e.tensor_copy(out=l_new[:, c0:512], in_=sp_b[:, c0:512]), reads=[sp_b], writes=[l_new])
            else:
                fw.ops("pe", [
                    lambda e: e.matmul(s_b[:, c0:512], tri[:, :], sp_b[:, c0:512], start=True, stop=False),
                    lambda e: e.matmul(s_b[:, c0:512], negones[:, :], l_old[:, c0:512], start=False, stop=True)],
                    reads=[tri, negones, sp_b, l_old], writes=[s_b])
                if idx != n - 1:
                    fw.op("dve", lambda e: e.tensor_tensor(out=l_new[:, c0:512], in0=l_old[:, c0:512], in1=sp_b[:, c0:512],
                                                           op=ALU.add),
                          reads=[sp_b, l_old], writes=[l_new])
            if c0 > 0:
                fw.op("dve", lambda e: e.memset(l_new[:, c0 - 128:c0], 0.0), writes=[l_new])

        def D(i):
            s_b, e2_b = psum_s[i % 2], e2s[i % 2]
            c0 = c0_of(i)
            fw.op("act", lambda e: e.activation(out=e2_b[:, c0:512], in_=s_b[:, c0:512], func=AF.Exp), reads=[s_b], writes=[e2_b])

        def E(i):
            e2_b, e_b, w_b = e2s[i % 2], es_[i % 4], ws_[i % 3]
            c0 = c0_of(i)
            fw.op("dve", lambda e: e.tensor_tensor(out=w_b[:, c0:512], in0=e_b[:, c0:512], in1=e2_b[:, c0:512], op=ALU.mult),
                  reads=[e_b, e2_b], writes=[w_b])

        def F(i):
            h, T, J, idx, n, g = tiles[i]
            vh, w_b, pout, ob = vhs[h % 2], ws_[i % 3], po[g % 2], osb[g % 2]
            c0 = c0_of(i)
            fw.op("pe", lambda e: e.matmul(pout[0:64, c0:512], vh[:, J, :], w_b[:, c0:512], start=(idx == 0), stop=(idx == n - 1)),
                  reads=[vh, w_b], writes=[pout])
            if idx == n - 1:
                fw.op("dve", lambda e: e.tensor_copy(out=ob[:, :], in_=pout[0:64, :]), reads=[pout], writes=[ob])
                fw.dma("pool", [(oT[h * 64:(h + 1) * 64, T * 512:(T + 1) * 512], ob[:, :])], ob, reads=[ob],
                       writes=[dbuf(cx, (key + "o", h, T))])
                if T == nT - 1:
                    load_head(h + 2)

        load_head(0)
        load_head(1)
        for s_ in range(-2, N + 2):
            if 0 <= s_ + 1 < N:
                Aexp(s_ + 1)
            if 0 <= s_ - 2 < N:
                E(s_ - 2)
            if 0 <= s_ - 1 < N:
                C(s_ - 1)
            if 0 <= s_ - 2 < N:
                F(s_ - 2)
            if 0 <= s_ + 2 < N:
                Amm(s_ + 2)
            if 0 <= s_ < N:
                filler(NFILL)
                Bln(s_)
            if 0 <= s_ - 1 < N:
                D(s_ - 1)
        fw.end_phase()


def sb_out_phase(fw, cx, S, h_in, h_out, hkey_in, hkey_out, g_post, wo_dram, wo_b, oT, key):
    nt = S // TT
    with ExitStack() as st:
        tb = alloc_tile_bufs(fw, st, need_m=True)
        wo = fw.sb("wo", [128, KC, 1024], BF16, st)
        ots = [fw.sb("ot%d" % i, [128, KC, TT], BF16, st) for i in range(2)]
        pss = [fw.ps("ps_o%d" % i, stack=st) for i in range(2)]
        fw.dma("sp", [(wo[:, :, :], wo_dram)], wo, reads=[wo_b], writes=[wo])
        for t in range(nt):
            tsl = slice(t * TT, (t + 1) * TT)
            ht = load_h(fw, cx, tb, t, h_in, hkey_in)
            ot = ots[t % 2]
            odeps = [dbuf(cx, (key + "o", h, t)) for h in range(16)]
            fw.dma("sp", [(ot[:, :, :], oT[:, tsl].rearrange("(c p) t -> p c t", p=128))], ot, reads=odeps, writes=[ot])
            out_proj(fw, cx, tb, wo, ot, pss)
            epilogue(fw, cx, tb, t, ht, h_out, hkey_out, g_post)
        fw.end_phase()


def conv_phase(fw, cx, S, h_in, h_out, hkey_in, hkey_out, g_pre, g_post, wcin, wcin_b, wo_dram, wo_b, cw_off):
    nt = S // TT
    with ExitStack() as st:
        tb = alloc_tile_bufs(fw, st, need_m=True)
        wo = fw.sb("wo", [128, KC, 1024], BF16, st)
        wis = [fw.sb("wci%d" % i, [128, KC, 384], BF16, st) for i in range(2)]
        hcs = [fw.sb("hc%d" % i, [128, TT + 2], F32, st) for i in range(8)]
        cs = [fw.sb("cs%d" % i, [128, TT], F32, st) for i in range(2)]
        t0s = [fw.sb("t0%d" % i, [128, TT], F32, st) for i in range(2)]
        gated = fw.sb("gated", [128, KC, TT], BF16, st)
        pss = [fw.ps("ps_c%d" % i, stack=st) for i in range(6)]
        pso = [fw.ps("ps_o%d" % i, stack=st) for i in range(1)]
        fw.dma("sp", [(wo[:, :, :], wo_dram)], wo, reads=[wo_b], writes=[wo])
        nw = 0
        for t in range(nt):
            ht = prologue(fw, cx, tb, t, h_in, hkey_in, g_pre)
            for i in range(8):
                w = wis[nw % 2]
                pb, pc, pu = pss[(nw % 2) * 3 + 0], pss[(nw % 2) * 3 + 1], pss[(nw % 2) * 3 + 2]
                c_sb, t0 = cs[nw % 2], t0s[nw % 2]
                nw += 1
                hc = hcs[i]
                fw.dma("sp", [(w[:, :, :], wcin[i])], w, reads=[wcin_b], writes=[w])
                for gi, p in ((1, pc), (2, pu), (0, pb)):
                    fw.ops("pe", [
                        (lambda e, c=c, gi=gi, p=p, w=w: e.matmul(p[:, :], w[:, c, gi * 128:(gi + 1) * 128], tb.xn[:, c, :],
                                                                 start=(c == 0), stop=(c == KC - 1)))
                        for c in range(KC)], reads=[w, tb.xn], writes=[p])
                if t == 0:
                    fw.op("dve", lambda e, hc=hc: e.memset(hc[:, 0:2], 0.0), writes=[hc])
                else:
                    fw.op("dve", lambda e, hc=hc: e.tensor_copy(out=hc[:, 0:2], in_=hc[:, TT:TT + 2]), reads=[hc], writes=[hc])
                fw.op("act", lambda e, c_sb=c_sb, pc=pc: e.copy(out=c_sb[:, :], in_=pc[:, :]), reads=[pc], writes=[c_sb])
                fw.op("dve", lambda e, hc=hc, c_sb=c_sb, pu=pu: e.tensor_tensor(out=hc[:, 2:TT + 2], in0=c_sb[:, :], in1=pu[:, :], op=ALU.mult),
                      reads=[c_sb, pu, hc], writes=[hc])
                k0 = cw_off + 0 * 8 + i
                k1 = cw_off + 1 * 8 + i
                k2 = cw_off + 2 * 8 + i
                fw.op("dve", lambda e, hc=hc, t0=t0, k0=k0: e.tensor_scalar(out=t0[:, :], in0=hc[:, 0:TT], scalar1=cx.gains[:, k0:k0 + 1],
                                                                          scalar2=None, op0=ALU.mult),
                      reads=[hc, cx.gains], writes=[t0])
                fw.op("dve", lambda e, hc=hc, t0=t0, k1=k1: e.scalar_tensor_tensor(out=t0[:, :], in0=hc[:, 1:TT + 1], scalar=cx.gains[:, k1:k1 + 1],
                                                                                  in1=t0[:, :], op0=ALU.mult, op1=ALU.add),
                      reads=[hc, t0, cx.gains], writes=[t0])
                fw.op("dve", lambda e, hc=hc, t0=t0, k2=k2: e.scalar_tensor_tensor(out=t0[:, :], in0=hc[:, 2:TT + 2], scalar=cx.gains[:, k2:k2 + 1],
                                                                                  in1=t0[:, :], op0=ALU.mult, op1=ALU.add),
                      reads=[hc, t0, cx.gains], writes=[t0])
                fw.op("dve", lambda e, i=i, t0=t0, pb=pb: e.tensor_tensor(out=gated[:, i, :], in0=t0[:, :], in1=pb[:, :], op=ALU.mult),
                      reads=[t0, pb], writes=[gated])
            out_proj(fw, cx, tb, wo, gated, pso)
            epilogue(fw, cx, tb, t, ht, h_out, hkey_out, g_post)
        fw.end_phase()


NFILL = 6
TG = 256
CH = 64


def gla_phase(fw, cx, S, h_in, h_out, hkey_in, hkey_out, g_pre, g_post, win_d, win_b, wa_d, wa_b, wg_d, wg_b,
              wo_dram, wo_b, hn_off, gc):
    nt = S // TG
    NCH = TG // CH
    with ExitStack() as st:
        tb = alloc_tile_bufs(fw, st, need_m=True, TT=TG, nh=1)
        wo = fw.sb("wo", [128, KC, 1024], BF16, st)
        win = [fw.sb("win%d" % i, [128, KC, 512], BF16, st) for i in range(6)]
        wa = fw.sb("wa", [128, KC, 16], BF16, st)
        wg = fw.sb("wg", [32, 512], BF16, st)
        a_ext = fw.sb("a_ext", [32, TG], BF16, st)
        qT_sb = fw.sb("qT_sb", [128, 4, TG], F32, st)
        kT_sb = fw.sb("kT_sb", [128, 4, TG], F32, st)
        sg = fw.sb("sg", [128, KC, TG], BF16, st)
        o_n = fw.sb("o_n", [128, KC, TG], BF16, st)
        egs = [fw.sb("eg%d" % i, [128, TG], F32, st) for i in range(2)]
        ex = fw.sb("ex", [64, 512], F32, st)
        sp = fw.sb("sp", [64, 512], BF16, st)
        ek = fw.sb("ek", [64, 512], F32, st)
        kt_tok = fw.sb("kt_tok", [64, 512], BF16, st)
        v_tok = fw.sb("v_tok", [64, 1024], BF16, st)
        eq = fw.sb("eq", [128, 4, CH], F32, st)
        ekT = fw.sb("ekT", [128, 4, CH], F32, st)
        qtl = fw.sb("qtl", [128, 4, CH], BF16, st)
        ktl = fw.sb("ktl", [128, 4, CH], BF16, st)
        sT = fw.sb("sT", [64, 4, CH], BF16, st)
        state = fw.sb("state", [128, 4, 256], F32, st)
        state_bf = fw.sb("state_bf", [128, 4, 256], BF16, st)
        rstdh = fw.sb("rstdh", [128, 4, TG], F32, st)
        pA = [fw.ps("pA%d" % i, stack=st) for i in range(2)]
        pct = fw.ps("pct", [128, 256], stack=st)
        pst = fw.ps("pst", [128, 256], stack=st)
        po = fw.ps("po", stack=st)
        pkv = [fw.ps("pkv%d" % i, stack=st) for i in range(2)]
        o_tile = tb.tmp

        fw.dma("sp", [(wo[:, :, :], wo_dram)], wo, reads=[wo_b], writes=[wo])
        for i in range(6):
            fw.dma("sp", [(win[i][:, :, :], win_d[i])], win[i], reads=[win_b], writes=[win[i]])
        fw.dma("sp", [(wa[:, :, :], wa_d)], wa, reads=[wa_b], writes=[wa])
        fw.dma("sp", [(wg[0:17, :], wg_d)], wg, reads=[wg_b], writes=[wg])
        fw.op("dve", lambda e: e.memset(a_ext[:, :], 1.0), writes=[a_ext])
        fw.op("dve", lambda e: e.memset(state[:, :, :], 0.0), writes=[state])
        fw.op("dve", lambda e: e.memset(state_bf[:, :, :], 0.0), writes=[state_bf])
        npa = [0]

        def nextp():
            p = pA[npa[0] % 2]
            npa[0] += 1
            return p

        def proj_fm(w, j0, width, p, N=TG):
            fw.ops("pe", [
                (lambda e, c=c: e.matmul(p[0:width, 0:N], w[:, c, j0:j0 + width], tb.xn[:, c, 0:N],
                                         start=(c == 0), stop=(c == KC - 1)))
                for c in range(KC)], reads=[w, tb.xn], writes=[p])

        for t in range(nt):
            ht = prologue(fw, cx, tb, t, h_in, hkey_in, g_pre)
            for hh in range(4):
                p = nextp()
                proj_fm(win[0], hh * 128, 128, p)
                fw.op("act", lambda e, hh=hh, p=p: e.copy(out=qT_sb[:, hh, :], in_=p[:, 0:TG]), reads=[p], writes=[qT_sb])
            for hh in range(4):
                p = nextp()
                proj_fm(win[1], hh * 128, 128, p)
                fw.op("act", lambda e, hh=hh, p=p: e.copy(out=kT_sb[:, hh, :], in_=p[:, 0:TG]), reads=[p], writes=[kT_sb])
            for c8 in range(8):
                p = nextp()
                eg = egs[c8 % 2]
                proj_fm(win[4 + c8 // 4], (c8 % 4) * 128, 128, p)
                fw.op("act", lambda e, p=p, eg=eg: e.activation(out=eg[:, :], in_=p[:, 0:TG], func=AF.Exp, scale=-1.0),
                      reads=[p], writes=[eg])
                fw.op("dve", lambda e, eg=eg: e.tensor_scalar(out=eg[:, :], in0=eg[:, :], scalar1=1.0, scalar2=None, op0=ALU.add),
                      reads=[eg], writes=[eg])
                fw.op("dve", lambda e, eg=eg: e.reciprocal(out=eg[:, :], in_=eg[:, :]), reads=[eg], writes=[eg])
                fw.op("dve", lambda e, c8=c8, p=p, eg=eg: e.tensor_tensor(out=sg[:, c8, :], in0=eg[:, :], in1=p[:, 0:TG], op=ALU.mult),
                      reads=[eg, p], writes=[sg])
            p = nextp()
            proj_fm(wa, 0, 16, p)
            fw.op("act", lambda e, p=p: e.copy(out=a_ext[0:16, :], in_=p[0:16, 0:TG]), reads=[p], writes=[a_ext])
            for n in range(NCH):
                csl = slice(n * CH, (n + 1) * CH)
                px = nextp()
                fw.op("pe", lambda e, px=px: e.matmul(px[0:64, :], a_ext[0:17, csl], wg[0:17, :], start=True, stop=True),
                      reads=[a_ext, wg], writes=[px])
                fw.op("act", lambda e, px=px: e.activation(out=ex[:, :], in_=px[0:64, :], func=AF.Exp, scale=-1.0),
                      reads=[px], writes=[ex])
                fw.op("act", lambda e: e.activation(out=sp[:, :], in_=ex[:, :], func=AF.Ln, bias=1.0, scale=1.0),
                      reads=[ex], writes=[sp])
                pc = nextp()
                fw.op("pe", lambda e, pc=pc: e.matmul(pc[0:64, :], gc.btn[:, :], sp[:, :], start=True, stop=True),
                      reads=[gc.btn, sp], writes=[pc])
                fw.op("act", lambda e, pc=pc: e.activation(out=ek[:, :], in_=pc[0:64, :], func=AF.Exp, scale=-1.0),
                      reads=[pc], writes=[ek])
                pk = nextp()
                fw.ops("pe", [
                    (lambda e, c=c, pk=pk: e.matmul(pk[0:64, :], tb.xn[:, c, csl], win[1][:, c, :], start=(c == 0), stop=(c == KC - 1)))
                    for c in range(KC)], reads=[win[1], tb.xn], writes=[pk])
                fw.op("dve", lambda e, pk=pk: e.tensor_tensor(out=kt_tok[:, :], in0=ek[:, :], in1=pk[0:64, :], op=ALU.mult),
                      reads=[ek, pk], writes=[kt_tok])
                for vb in range(2):
                    pv = nextp()
                    fw.ops("pe", [
                        (lambda e, c=c, pv=pv, vb=vb: e.matmul(pv[0:64, :], tb.xn[:, c, csl], win[2 + vb][:, c, :],
                                                             start=(c == 0), stop=(c == KC - 1)))
                        for c in range(KC)], reads=[win[2 + vb], tb.xn], writes=[pv])
                    fw.op("act", lambda e, pv=pv, vb=vb: e.copy(out=v_tok[:, vb * 512:(vb + 1) * 512], in_=pv[0:64, :]),
                          reads=[pv], writes=[v_tok])
                fw.ops("pe", [
                    (lambda e, hh=hh: e.matmul(pct[:, hh * CH:(hh + 1) * CH], sp[:, hh * 128:(hh + 1) * 128], gc.btn[:, :],
                                               start=True, stop=True))
                    for hh in range(4)], reads=[sp, gc.btn], writes=[pct])
                fw.op("act", lambda e: e.activation(out=eq[:, :, :], in_=pct[:, :].rearrange("p (h c) -> p h c", h=4), func=AF.Exp),
                      reads=[pct], writes=[eq])
                fw.op("act", lambda e: e.activation(out=ekT[:, :, :], in_=pct[:, :].rearrange("p (h c) -> p h c", h=4), func=AF.Exp, scale=-1.0),
                      reads=[pct], writes=[ekT])
                fw.op("dve", lambda e: e.scalar_tensor_tensor(out=qtl[:, :, :], in0=qT_sb[:, :, csl], scalar=128 ** -0.5, in1=eq[:, :, :],
                                                             op0=ALU.mult, op1=ALU.mult),
                      reads=[qT_sb, eq], writes=[qtl])
                fw.op("dve", lambda e: e.tensor_tensor(out=ktl[:, :, :], in0=kT_sb[:, :, csl], in1=ekT[:, :, :], op=ALU.mult),
                      reads=[kT_sb, ekT], writes=[ktl])
                fw.ops("pe", [
                    (lambda e, hh=hh: e.matmul(pst[0:64, hh * CH:(hh + 1) * CH], ktl[:, hh, :], qtl[:, hh, :], start=True, stop=True))
                    for hh in range(4)], reads=[ktl, qtl], writes=[pst])
                fw.op("dve", lambda e: e.tensor_tensor(out=sT[:, :, :], in0=pst[0:64, :].rearrange("p (h c) -> p h c", h=4),
                                                       in1=gc.mask4[:, :, :], op=ALU.mult),
                      reads=[pst, gc.mask4], writes=[sT])
                mm = []
                for hh in range(4):
                    for ec in range(2):
                        col = (hh * 2 + ec) * CH
                        mm.append(lambda e, hh=hh, ec=ec, col=col: e.matmul(
                            po[:, col:col + CH], state_bf[:, hh, ec * 128:(ec + 1) * 128], qtl[:, hh, :], start=True, stop=False))
                        mm.append(lambda e, hh=hh, ec=ec, col=col: e.matmul(
                            po[:, col:col + CH], v_tok[:, hh * 256 + ec * 128:hh * 256 + (ec + 1) * 128], sT[:, hh, :],
                            start=False, stop=True))
                fw.ops("pe", mm, reads=[state_bf, qtl, v_tok, sT], writes=[po])
                fw.op("act", lambda e: e.copy(out=o_tile[:, :, csl], in_=po[:, :].rearrange("p (k c) -> p k c", k=8)),
                      reads=[po], writes=[o_tile])
                for hh in range(4):
                    pkvb = pkv[hh // 2]
                    fw.op("pe", lambda e, hh=hh, pkvb=pkvb: e.matmul(
                        pkvb[:, (hh % 2) * 256:(hh % 2 + 1) * 256], kt_tok[:, hh * 128:(hh + 1) * 128], v_tok[:, hh * 256:(hh + 1) * 256],
                        start=True, stop=True), reads=[kt_tok, v_tok], writes=[pkvb])
                for hh in range(4):
                    pkvb = pkv[hh // 2]
                    fw.op("dve", lambda e, hh=hh: e.tensor_scalar(out=state[:, hh, :], in0=state[:, hh, :], scalar1=eq[:, hh, CH - 1:CH],
                                                                 scalar2=None, op0=ALU.mult),
                          reads=[state, eq], writes=[state])
                    fw.op("dve", lambda e, hh=hh, pkvb=pkvb: e.scalar_tensor_tensor(
                        out=state[:, hh, :], in0=pkvb[:, (hh % 2) * 256:(hh % 2 + 1) * 256], scalar=eq[:, hh, CH - 1:CH],
                        in1=state[:, hh, :], op0=ALU.mult, op1=ALU.add),
                        reads=[pkvb, eq, state], writes=[state])
                fw.op("act", lambda e: e.copy(out=state_bf[:, :, :], in_=state[:, :, :]), reads=[state], writes=[state_bf])
            fw.op("act", lambda e: e.activation(out=tb.sq[:, :, :], in_=o_tile[:, :, :], func=AF.Square), reads=[o_tile], writes=[tb.sq])
            for hh in range(4):
                pkvb = pkv[hh // 2]
                fw.ops("pe", [
                    (lambda e, hh=hh, ec=ec, pkvb=pkvb: e.matmul(pkvb[:, (hh % 2) * 256:(hh % 2 + 1) * 256], cx.ones256[:, :],
                                                                 tb.sq[:, hh * 2 + ec, :], start=(ec == 0), stop=(ec == 1)))
                    for ec in range(2)], reads=[cx.ones256, tb.sq], writes=[pkvb])
            for half in range(2):
                fw.op("act", lambda e, half=half: e.activation(
                    out=rstdh[:, 2 * half:2 * half + 2, :], in_=pkv[half][:, :].rearrange("p (h c) -> p h c", h=2),
                    func=AF.Ln, bias=EPS, scale=1.0), reads=[pkv[half]], writes=[rstdh])
            fw.op("act", lambda e: e.activation(out=rstdh[:, :, :], in_=rstdh[:, :, :], func=AF.Exp, scale=-0.5),
                  reads=[rstdh], writes=[rstdh])
            for k8 in range(8):
                fw.op("dve", lambda e, k8=k8: e.scalar_tensor_tensor(
                    out=o_tile[:, k8, :], in0=o_tile[:, k8, :], scalar=cx.gains[:, hn_off + k8:hn_off + k8 + 1],
                    in1=rstdh[:, k8 // 2, :], op0=ALU.mult, op1=ALU.mult),
                    reads=[o_tile, rstdh, cx.gains], writes=[o_tile])
            fw.op("dve", lambda e: e.tensor_tensor(out=o_n[:, :, :], in0=o_tile[:, :, :], in1=sg[:, :, :], op=ALU.mult),
                  reads=[o_tile, sg], writes=[o_n])
            out_proj(fw, cx, tb, wo, o_n, pA)
            epilogue(fw, cx, tb, t, ht, h_out, hkey_out, g_post)
        fw.end_phase()


S_FULL = 4096
NSMALL = 160
G_OFF, CW_OFF, HN_OFF = 0, 128, 152


def build_program(S=S_FULL):
    nc = bass.Bass("TRN2", target_bir_lowering=False)

    def din(name, shape, dt=F32):
        return nc.dram_tensor(name, list(shape), dt, kind="ExternalInput").ap()

    def dscr(name, shape, dt=BF16):
        return nc.dram_tensor(name, list(shape), dt).ap()

    x = din("x", [128, 8, S])
    y = nc.dram_tensor("y", [128, 8, S], F32, kind="ExternalOutput").ap()
    small = din("small", [128, NSMALL])
    tri_d = din("tri", [128, 128])
    neg_d = din("negones", [128, 128])
    masks_d = din("masks", [128, 4, 512])
    btn_d = din("btn", [64, 64])
    mask4_d = din("mask4", [64, 4, 64])
    w32 = {}
    wbf = {}

    def wpair(name, shape):
        w32[name] = din(name + "_f32", shape)
        wbf[name] = dscr(name + "_bf", shape)

    for j in range(2):
        wpair("sbqkv%d" % j, [6, 128, 8, 512])
        wpair("sbo%d" % j, [128, 8, 1024])
    wpair("cin", [8, 128, 8, 384])
    wpair("cout", [128, 8, 1024])
    wpair("gin", [6, 128, 8, 512])
    wpair("ga", [128, 8, 16])
    wpair("gg", [17, 512])
    wpair("go", [128, 8, 1024])
    for l in range(4):
        wpair("up%d" % l, [8, 128, 8, 512])
        wpair("dn%d" % l, [8, 128, 32, 128])
    qT = dscr("qT", [1024, S])
    kT = dscr("kT", [1024, S])
    v = dscr("v", [S, 1024])
    oT = dscr("oT", [1024, S])

    with ExitStack() as es:
        fw = FW(nc, es)
        cx = Ctx()
        setup_common(fw, cx, small, NSMALL)
        sbc = Ctx()
        sbc.tri = fw.sb("tri", [128, 128], BF16)
        sbc.negones = fw.sb("negones", [128, 128], BF16)
        sbc.masks = fw.sb("masks", [128, 4, 512], F32)
        gc = Ctx()
        gc.btn = fw.sb("btn", [64, 64], BF16)
        gc.mask4 = fw.sb("mask4", [64, 4, 64], F32)
        fw.dma("pool", [(sbc.tri[:, :], tri_d)], sbc.tri, writes=[sbc.tri])
        fw.dma("pool", [(sbc.negones[:, :], neg_d)], sbc.negones, writes=[sbc.negones])
        fw.dma("pool", [(gc.btn[:, :], btn_d)], gc.btn, writes=[gc.btn])
        fw.dma("sp", [(sbc.masks[:, :, :], masks_d)], sbc.masks, writes=[sbc.masks])
        fw.dma("sp", [(gc.mask4[:, :, :], mask4_d)], gc.mask4, writes=[gc.mask4])

        wb = {}

        def cast(name):
            shape = wbf[name].shape
            n = shape[0] if len(shape) == 4 else 1
            wb[name] = cast_weight(fw, cx, name, wbf[name], w32[name], n)

        order = ["sbqkv0", "sbo0", "up0", "dn0", "cin", "cout", "up1", "dn1",
                 "gin", "ga", "gg", "go", "up2", "dn2", "sbqkv1", "sbo1", "up3", "dn3"]
        for name in order:
            cast(name)

        def gi(l, k):
            return l * 4 + k

        def ffn(l, hin, hkin):
            ffn_phase(fw, cx, S, hin, y, hkin, "y", gi(l, 2), gi(l, 3), wbf["up%d" % l], wbf["dn%d" % l],
                      wb["up%d" % l], wb["dn%d" % l])

        def sb(l, j, hin, hkin):
            key = "sb%d" % j
            sb_proj_phase(fw, cx, S, hin, hkin, gi(l, 0), wbf["sbqkv%d" % j], wb["sbqkv%d" % j], qT, kT, v, key)
            sb_attn_phase(fw, cx, S, qT, kT, v, oT, key, sbc)
            sb_out_phase(fw, cx, S, hin, y, hkin, "y", gi(l, 1), wbf["sbo%d" % j], wb["sbo%d" % j], oT, key)

        sb(0, 0, x, "x")
        ffn(0, y, "y")
        conv_phase(fw, cx, S, y, y, "y", "y", gi(1, 0), gi(1, 1), wbf["cin"], wb["cin"], wbf["cout"], wb["cout"], CW_OFF)
        ffn(1, y, "y")
        gla_phase(fw, cx, S, y, y, "yg", "yg", gi(2, 0), gi(2, 1), wbf["gin"], wb["gin"], wbf["ga"], wb["ga"],
                  wbf["gg"], wb["gg"], wbf["go"], wb["go"], HN_OFF, gc)
        ffn(2, y, "y")
        sb(3, 1, y, "y")
        ffn(3, y, "y")
        fw.finish([cx.dram[("y", t)] for t in range(S // TT)])
        build_program.stats = fw.stats()
    return nc


def _fm(a):
    return np.ascontiguousarray(a.T.reshape(8, 128, -1).transpose(1, 0, 2))


def _blk512(W, nb):
    return np.ascontiguousarray(W.reshape(8, 128, nb, 512).transpose(2, 1, 0, 3))


def _rows(W):
    return np.ascontiguousarray(W.reshape(8, 128, W.shape[1]).transpose(1, 0, 2))


def prepare_inputs(x, norm_gains, sb_w_qkv, sb_w_o, conv_w_in, conv_w, conv_w_out,
                   gla_w_in, gla_w_gate_up, gla_b_gate, gla_head_norm, gla_w_o, ffn_w_up, ffn_w_down):
    f = np.float32
    shared = {}
    small = np.zeros((128, NSMALL), f)
    ng = np.asarray(norm_gains, f)
    small[:, 0:128] = ng.reshape(16, 8, 128).transpose(2, 0, 1).reshape(128, 128)
    small[:, 128:152] = np.asarray(conv_w, f)[0].reshape(3, 8, 128).transpose(2, 0, 1).reshape(128, 24)
    small[:, 152:160] = np.asarray(gla_head_norm, f)[0].reshape(8, 128).T
    shared["small"] = small
    k = np.arange(128)
    shared["tri"] = -(k[:, None] >= k[None, :]).astype(f)
    shared["negones"] = -np.ones((128, 128), f)
    col = np.arange(512)
    shared["masks"] = np.ascontiguousarray(
        np.stack([(col[None, :] > (r * 128 + k[:, None])).astype(f) for r in range(4)], 1))
    s = np.arange(64)
    shared["btn"] = np.ascontiguousarray(-(s[:, None] <= s[None, :]).astype(f) / 16.0)
    shared["mask4"] = np.ascontiguousarray(np.repeat((s[:, None] <= s[None, :]).astype(f)[:, None, :], 4, 1))
    for j in range(2):
        shared["sbqkv%d_f32" % j] = _blk512(np.asarray(sb_w_qkv[j], f), 6)
        shared["sbo%d_f32" % j] = _rows(np.asarray(sb_w_o[j], f))
    Win = np.asarray(conv_w_in[0], f)
    shared["cin_f32"] = np.ascontiguousarray(
        Win.reshape(8, 128, 3, 8, 128).transpose(3, 1, 0, 2, 4).reshape(8, 128, 8, 384))
    shared["cout_f32"] = _rows(np.asarray(conv_w_out[0], f))
    Wg = np.asarray(gla_w_in[0], f)
    shared["gin_f32"] = _blk512(np.ascontiguousarray(Wg[:, :3072]), 6)
    shared["ga_f32"] = _rows(np.ascontiguousarray(Wg[:, 3072:]))
    shared["gg_f32"] = np.ascontiguousarray(
        np.concatenate([np.asarray(gla_w_gate_up[0], f), np.asarray(gla_b_gate[0], f)[None]], 0))
    shared["go_f32"] = _rows(np.asarray(gla_w_o[0], f))
    for l in range(4):
        shared["up%d_f32" % l] = _blk512(np.asarray(ffn_w_up[l], f), 8)
        Wd = np.asarray(ffn_w_down[l], f)
        shared["dn%d_f32" % l] = np.ascontiguousarray(Wd.reshape(32, 128, 8, 128).transpose(2, 1, 0, 3))
    xs = np.asarray(x, f)
    in_maps = []
    for b in range(8):
        m = dict(shared)
        m["x"] = _fm(xs[b])
        in_maps.append(m)
    return in_maps


_NC_CACHE = {}


def kernel(**inputs):
    in_maps = prepare_inputs(**inputs)
    if "nc" not in _NC_CACHE:
        _NC_CACHE["nc"] = build_program(S_FULL)
    nc = _NC_CACHE["nc"]
    res = run_bass_kernel_spmd(nc, in_maps, core_ids=list(range(8)))
    out = np.empty((8, S_FULL, 1024), np.float32)
    for b in range(8):
        yb = np.asarray(res.results[b]["y"])
        out[b] = yb.transpose(1, 0, 2).reshape(1024, S_FULL).T
    return out
```
